# Optimizing a Trainium2 kernel written in Bass

```python
import math
import jax, jax.numpy as jnp
from jax import lax
import numpy as np

D_MODEL = 1024
BATCH = 8
SEQ = 2048
DEPTH = 2
DEC_BATCH = 128
DEC_SEQ = 4
PAST_LEN = 16384
PAGE_SIZE = 128

N_MIXERS = 2
N_RET_LAYERS = (DEPTH + 1) // 2
N_REC_LAYERS = DEPTH // 2
RET_HEADS = 4
RET_DK = D_MODEL // RET_HEADS
RET_DV = 2 * D_MODEL // RET_HEADS
RET_QK = RET_HEADS * RET_DK
RET_VDIM = RET_HEADS * RET_DV
RET_CHUNK = 128
ROPE_BASE = 10000.0
D_RNN = 1280
LRU_BLOCKS = 8
LRU_BS = D_RNN // LRU_BLOCKS
CONV_W = 4
LRU_C = 8.0
D_FF = 2816
ALPHA = (2.0 * DEPTH) ** 0.25
BETA = (8.0 * DEPTH) ** -0.25
LN_EPS = 1e-5
GN_EPS = 1e-6

kernel_name = 'retnet_hawk_macaron_deepnorm_step'


def layer_norm(x, g, b):
    xf = x.astype(jnp.float32)
    mu = jnp.mean(xf, -1, keepdims=True)
    var = jnp.mean(jnp.square(xf - mu), -1, keepdims=True)
    return ((xf - mu) * lax.rsqrt(var + LN_EPS) * g + b).astype(x.dtype)


def swiglu(x, w_in, w_out):
    g, u = jnp.split(x @ w_in, 2, axis=-1)
    return (jax.nn.silu(g) * u) @ w_out


def rotary(x, pos):
    half = x.shape[-1] // 2
    inv = ROPE_BASE ** (-jnp.arange(half, dtype=jnp.float32) / half)
    ang = pos.astype(jnp.float32)[:, None] * inv[None, :]
    cos, sin = jnp.cos(ang), jnp.sin(ang)
    x1, x2 = x[..., :half], x[..., half:]
    return jnp.concatenate([x1 * cos - x2 * sin, x1 * sin + x2 * cos], axis=-1)


def retention_log_gamma():
    return jnp.log1p(-jnp.exp2(-5.0 - jnp.arange(RET_HEADS, dtype=jnp.float32)))


def retention_chunkwise(q, k, v, s0):
    B, H, T, dk = q.shape
    dv = v.shape[-1]
    C = math.gcd(T, RET_CHUNK)
    n = T // C
    lg = retention_log_gamma()
    idx = jnp.arange(C, dtype=jnp.float32)
    rel = idx[:, None] - idx[None, :]
    dmask = jnp.where(rel >= 0, jnp.exp(lg[:, None, None] * jnp.maximum(rel, 0.0)), 0.0)
    q_dec = jnp.exp(lg[:, None] * (idx + 1.0))[:, :, None]
    k_dec = jnp.exp(lg[:, None] * (C - 1.0 - idx))[:, :, None]
    chunk_dec = jnp.exp(lg * C)[:, None, None]

    def to_chunks(a):
        return a.reshape(B, H, n, C, a.shape[-1]).transpose(2, 0, 1, 3, 4)

    def step(s, qkv):
        qc, kc, vc = qkv
        scores = jnp.einsum('bhcd,bhed->bhce', qc, kc) * dmask
        o = (jnp.einsum('bhce,bhev->bhcv', scores, vc)
             + jnp.einsum('bhcd,bhdv->bhcv', qc * q_dec, s))
        s = s * chunk_dec + jnp.einsum('bhcd,bhcv->bhdv', kc * k_dec, vc)
        return s, o

    s, o = lax.scan(step, s0, (to_chunks(q), to_chunks(k), to_chunks(v)))
    o = o.transpose(1, 2, 0, 3, 4).reshape(B, H, T, dv)
    return o, s


def retention_mixer(x, pos, s0, w_in, gn_g, w_out):
    B, T, _ = x.shape
    proj = x @ w_in
    q, k, v, g = jnp.split(proj, [RET_QK, 2 * RET_QK, 2 * RET_QK + RET_VDIM], axis=-1)

    def heads(a, d):
        return a.reshape(B, T, RET_HEADS, d).transpose(0, 2, 1, 3).astype(jnp.float32)

    q = rotary(heads(q, RET_DK), pos)
    k = rotary(heads(k, RET_DK), pos) * (RET_DK ** -0.5)
    v = heads(v, RET_DV)
    o, s = retention_chunkwise(q, k, v, s0.astype(jnp.float32))
    mu = jnp.mean(o, -1, keepdims=True)
    var = jnp.mean(jnp.square(o - mu), -1, keepdims=True)
    o = ((o - mu) * lax.rsqrt(var + GN_EPS)).transpose(0, 2, 1, 3).reshape(B, T, RET_VDIM) * gn_g
    y = (jax.nn.silu(g.astype(jnp.float32)) * o).astype(x.dtype) @ w_out
    return y, s


def rglru_mixer(x, conv_buf, h0, w_in, conv_w, conv_b, w_a, b_a, w_i, b_i, lam, w_out):
    B, T, _ = x.shape
    gate_br, xb = jnp.split(x @ w_in, 2, axis=-1)
    gate_br = jax.nn.gelu(gate_br, approximate=True)
    xp = jnp.concatenate([conv_buf.astype(xb.dtype), xb], axis=1)
    xc = conv_b + sum(conv_w[j] * xp[:, j:j + T] for j in range(CONV_W))
    new_buf = xp[:, T:]
    xcb = xc.reshape(B, T, LRU_BLOCKS, LRU_BS)
    r = jax.nn.sigmoid(jnp.einsum('btni,nij->btnj', xcb, w_a).reshape(B, T, D_RNN) + b_a)
    i = jax.nn.sigmoid(jnp.einsum('btni,nij->btnj', xcb, w_i).reshape(B, T, D_RNN) + b_i)
    log_a = -LRU_C * r.astype(jnp.float32) * jax.nn.softplus(-lam.astype(jnp.float32))
    a = jnp.exp(log_a)
    u = jnp.sqrt(-jnp.expm1(2.0 * log_a)) * (i * xc).astype(jnp.float32)

    def step(h, au):
        a_t, u_t = au
        h = a_t * h + u_t
        return h, h

    h_last, hs = lax.scan(step, h0.astype(jnp.float32), (a.transpose(1, 0, 2), u.transpose(1, 0, 2)))
    hs = hs.transpose(1, 0, 2).astype(x.dtype)
    y = (gate_br * hs) @ w_out
    return y, new_buf, h_last


def trunk(x, pos, ret_states, conv_states, lru_states, ln_g, ln_b, ffn1_w_in, ffn1_w_out,
          ffn2_w_in, ffn2_w_out, ret_w_in, ret_gn_g, ret_w_out, rec_w_in, rec_conv_w, rec_conv_b,
          rec_w_a, rec_b_a, rec_w_i, rec_b_i, rec_lam, rec_w_out):
    new_ret, new_conv, new_lru = [], [], []
    for layer in range(DEPTH):
        j = layer // N_MIXERS
        x = layer_norm(ALPHA * x + 0.5 * swiglu(x, ffn1_w_in[layer], ffn1_w_out[layer]),
                       ln_g[layer, 0], ln_b[layer, 0])
        if layer % N_MIXERS == 0:
            m, s = retention_mixer(x, pos, ret_states[j], ret_w_in[j], ret_gn_g[j], ret_w_out[j])
            new_ret.append(s)
        else:
            m, cb, h = rglru_mixer(x, conv_states[j], lru_states[j], rec_w_in[j], rec_conv_w[j],
                                   rec_conv_b[j], rec_w_a[j], rec_b_a[j], rec_w_i[j], rec_b_i[j],
                                   rec_lam[j], rec_w_out[j])
            new_conv.append(cb)
            new_lru.append(h)
        x = layer_norm(ALPHA * x + m, ln_g[layer, 1], ln_b[layer, 1])
        x = layer_norm(ALPHA * x + 0.5 * swiglu(x, ffn2_w_in[layer], ffn2_w_out[layer]),
                       ln_g[layer, 2], ln_b[layer, 2])
    return x, jnp.stack(new_ret), jnp.stack(new_conv), jnp.stack(new_lru)


def setup_inputs(seed: int = 0) -> dict:
    key = jax.random.key(seed)
    ks = jax.random.split(key, 32)
    f32 = jnp.float32
    nrm = lambda k, shape, s: jax.random.normal(k, shape, f32) * s
    u = jax.random.uniform(ks[24], (N_REC_LAYERS, D_RNN), f32, 0.9, 0.999)
    a0 = u ** (1.0 / LRU_C)
    return {
        'x_prompt': nrm(ks[0], (BATCH, SEQ, D_MODEL), 1.0),
        'x_sample': nrm(ks[1], (DEC_BATCH, DEC_SEQ, D_MODEL), 1.0),
        'state_ret': nrm(ks[2], (N_RET_LAYERS, DEC_BATCH, RET_HEADS, RET_DK, RET_DV), 0.5),
        'state_conv': nrm(ks[3], (N_REC_LAYERS, DEC_BATCH, CONV_W - 1, D_RNN), 1.0),
        'state_lru': nrm(ks[4], (N_REC_LAYERS, DEC_BATCH, D_RNN), 0.5),
        'ln_g': 1.0 + nrm(ks[5], (DEPTH, 3, D_MODEL), 0.02),
        'ln_b': nrm(ks[6], (DEPTH, 3, D_MODEL), 0.02),
        'ffn1_w_in': nrm(ks[7], (DEPTH, D_MODEL, 2 * D_FF), D_MODEL ** -0.5),
        'ffn1_w_out': nrm(ks[8], (DEPTH, D_FF, D_MODEL), BETA * D_FF ** -0.5),
        'ffn2_w_in': nrm(ks[9], (DEPTH, D_MODEL, 2 * D_FF), D_MODEL ** -0.5),
        'ffn2_w_out': nrm(ks[10], (DEPTH, D_FF, D_MODEL), BETA * D_FF ** -0.5),
        'ret_w_in': nrm(ks[11], (N_RET_LAYERS, D_MODEL, 2 * RET_QK + 2 * RET_VDIM), D_MODEL ** -0.5),
        'ret_gn_g': 1.0 + nrm(ks[12], (N_RET_LAYERS, RET_VDIM), 0.02),
        'ret_w_out': nrm(ks[13], (N_RET_LAYERS, RET_VDIM, D_MODEL), BETA * RET_VDIM ** -0.5),
        'rec_w_in': nrm(ks[14], (N_REC_LAYERS, D_MODEL, 2 * D_RNN), D_MODEL ** -0.5),
        'rec_conv_w': nrm(ks[15], (N_REC_LAYERS, CONV_W, D_RNN), CONV_W ** -0.5),
        'rec_conv_b': nrm(ks[16], (N_REC_LAYERS, D_RNN), 0.02),
        'rec_w_a': nrm(ks[17], (N_REC_LAYERS, LRU_BLOCKS, LRU_BS, LRU_BS), LRU_BS ** -0.5),
        'rec_b_a': nrm(ks[18], (N_REC_LAYERS, D_RNN), 0.02),
        'rec_w_i': nrm(ks[19], (N_REC_LAYERS, LRU_BLOCKS, LRU_BS, LRU_BS), LRU_BS ** -0.5),
        'rec_b_i': nrm(ks[20], (N_REC_LAYERS, D_RNN), 0.02),
        'rec_lam': jnp.log(a0) - jnp.log1p(-a0),
        'rec_w_out': nrm(ks[21], (N_REC_LAYERS, D_RNN, D_MODEL), BETA * D_RNN ** -0.5),
    }


def reference(x_prompt, x_sample, state_ret, state_conv, state_lru, ln_g, ln_b, ffn1_w_in,
              ffn1_w_out, ffn2_w_in, ffn2_w_out, ret_w_in, ret_gn_g, ret_w_out, rec_w_in,
              rec_conv_w, rec_conv_b, rec_w_a, rec_b_a, rec_w_i, rec_b_i, rec_lam, rec_w_out):
    weights = (ln_g, ln_b, ffn1_w_in, ffn1_w_out, ffn2_w_in, ffn2_w_out, ret_w_in, ret_gn_g,
               ret_w_out, rec_w_in, rec_conv_w, rec_conv_b, rec_w_a, rec_b_a, rec_w_i, rec_b_i,
               rec_lam, rec_w_out)
    bp, tp, _ = x_prompt.shape
    ts = x_sample.shape[1]
    zero_ret = jnp.zeros((N_RET_LAYERS, bp, RET_HEADS, RET_DK, RET_DV), jnp.float32)
    zero_conv = jnp.zeros((N_REC_LAYERS, bp, CONV_W - 1, D_RNN), x_prompt.dtype)
    zero_lru = jnp.zeros((N_REC_LAYERS, bp, D_RNN), jnp.float32)
    pos_prompt = jnp.arange(tp, dtype=jnp.int32)
    y_prompt, ret_p, conv_p, lru_p = trunk(x_prompt, pos_prompt, zero_ret, zero_conv, zero_lru, *weights)
    pos_sample = PAST_LEN + jnp.arange(ts, dtype=jnp.int32)
    y_sample, ret_s, conv_s, lru_s = trunk(x_sample, pos_sample, state_ret, state_conv, state_lru, *weights)
    return (y_prompt, y_sample, ret_p, conv_p, lru_p, ret_s, conv_s, lru_s)
```

```python
import math
from contextlib import ExitStack

import numpy as np
import concourse.bass as bass
import concourse.mybir as mybir
from concourse.bass_utils import run_bass_kernel_spmd

F32 = mybir.dt.float32
BF16 = mybir.dt.bfloat16
AF = mybir.ActivationFunctionType
ALU = mybir.AluOpType

NCORES = 8
D = 1024
KC = 8
SEQ = 2048
DEC_B = 128
DEC_T = 4
NSAMP = DEC_B // NCORES
NG = 2
PG = SEQ // NG
SG = NSAMP // NG
SCOL = SG * DEC_T
T = PG + SCOL
CS = [(0, 512), (512, 512), (1024, SCOL)]
NCH = PG // 128
HEADS = 4
DK = 256
DV = 512
D_RNN = 1280
RC = 10
D_FF = 2816
FJ = 22
PAST = 16384
ALPHA = (2.0 * 2) ** 0.25
LN_EPS = 1e-5
GN_EPS = 1e-6
LRU_C = 8.0
EPOCH_MAX = 8000

ENGS = ["sync", "act", "dve", "pool", "pe"]


class DSem:
    def __init__(self, handle):
        self.h = handle
        self.total = 0


class Buf:
    __slots__ = ("name", "w", "r", "dsem")

    def __init__(self, name):
        self.name = name
        self.w = None
        self.r = {}
        self.dsem = None


class FW:
    def __init__(self, nc, stack):
        self.nc = nc
        self.stack = stack
        self.prog = {e: [] for e in ENGS}
        self.cnt = {e: 0 for e in ENGS}
        self.epoch = {e: 0 for e in ENGS}
        self.seen = {e: {} for e in ENGS}
        self.esems = {}
        self.dsems = []
        self.out_dsems = []
        self.arena_bufs = []
        self.legacy = {}
        self.tag = "init"
        self.pe_tags = []

    def esem(self, eng, epoch):
        k = (eng, epoch)
        if k not in self.esems:
            self.esems[k] = self.stack.enter_context(self.nc.semaphore(f"e_{eng}_{epoch}"))
        return self.esems[k]

    def new_dsem(self, name):
        d = DSem(self.stack.enter_context(self.nc.semaphore(f"d_{name}_{len(self.dsems)}")))
        self.dsems.append(d)
        return d

    def _need(self, eng, tok):
        if tok is None:
            return
        if tok[0] == "e":
            _, te, tep, tc = tok
            if te == eng:
                if eng == "pe" or eng == "sync":
                    return
                if tep == self.epoch[eng] and tc + 2 <= self.cnt[eng]:
                    return
                if tep < self.epoch[eng] and self.cnt[eng] >= 2:
                    return
            key = (te, tep)
            if self.seen[eng].get(key, 0) >= tc:
                return
            self.seen[eng][key] = tc
            self.prog[eng].append(("w", self.esem(te, tep), tc))
        else:
            d = tok[1]
            key = ("d", id(d))
            if self.seen[eng].get(key, 0) >= d.total:
                return
            self.seen[eng][key] = d.total
            self.prog[eng].append(("w", d.h, d.total))

    def _deps(self, eng, reads, writes):
        for b in reads:
            self._need(eng, b.w)
        for b in writes:
            self._need(eng, b.w)
            for t in list(b.r.values()):
                self._need(eng, t)

    def op(self, eng, fn, reads=(), writes=(), npe=1):
        if eng == "pe":
            self.pe_tags.append((self.tag, npe))
        self._deps(eng, reads, writes)
        ep = self.epoch[eng]
        self.cnt[eng] += 1
        tok = ("e", eng, ep, self.cnt[eng])
        self.prog[eng].append(("o", fn, self.esem(eng, ep), 1))
        if self.cnt[eng] >= EPOCH_MAX:
            self.epoch[eng] += 1
            self.cnt[eng] = 0
        for b in reads:
            b.r[eng] = tok
        for b in writes:
            b.w = tok
            b.r = {}
        return tok

    def dma(self, out, in_, buf, write, is_output=False, extra=()):
        eng = "sync"
        if write:
            self._deps(eng, (), (buf,) + tuple(extra))
        else:
            self._deps(eng, (buf,) + tuple(extra), ())
        if buf.dsem is None:
            buf.dsem = self.new_dsem(buf.name)
        d = buf.dsem
        d.total += 16
        self.prog[eng].append(("o", lambda e, o=out, i=in_: e.dma_start(out=o, in_=i), d.h, 16))
        tok = ("d", d)
        for bb in (buf,) + tuple(extra):
            if write:
                bb.w = tok
                bb.r = {}
            else:
                bb.r["dma" + str(id(d))] = tok
        if is_output and d not in self.out_dsems:
            self.out_dsems.append(d)
        return tok

    def arena_reset(self):
        leg = dict(self.legacy)
        for b in self.arena_bufs:
            toks = list(b.r.values())
            if b.w is not None:
                toks.append(b.w)
            for t in toks:
                if t[0] == "e":
                    k = ("e", t[1])
                    o = leg.get(k)
                    if o is None or (o[2], o[3]) < (t[2], t[3]):
                        leg[k] = t
                else:
                    leg[("d", id(t[1]))] = t
        self.legacy = leg
        self.arena_bufs = []

    def abuf(self, name):
        b = Buf(name)
        b.r = dict(self.legacy)
        self.arena_bufs.append(b)
        return b

    def finish(self):
        for d in self.out_dsems:
            self._need("sync", ("d", d))

    def replay(self, block):
        prog = self.prog

        def run(name, e):
            for ent in prog[name]:
                if ent[0] == "w":
                    e.wait_ge(ent[1], ent[2])
                else:
                    ins = ent[1](e)
                    ins.then_inc(ent[2], ent[3])

        @block.sync
        def _(e):
            run("sync", e)

        @block.scalar
        def _(e):
            run("act", e)

        @block.vector
        def _(e):
            run("dve", e)

        @block.gpsimd
        def _(e):
            run("pool", e)

        @block.tensor
        def _(e):
            run("pe", e)


class Arena:
    def __init__(self, ap_f32, nbytes):
        self.ap = ap_f32
        self.nbytes = nbytes
        self.off = 0

    def reset(self):
        self.off = 0

    def alloc(self, shape_free, dtype):
        esz = 4 if dtype == F32 else 2
        n = int(np.prod(shape_free))
        nb = (n * esz + 31) // 32 * 32
        assert self.off + nb <= self.nbytes, f"arena overflow {self.off}+{nb}>{self.nbytes}"
        a = self.ap[:, self.off // 4:(self.off + nb) // 4]
        self.off += nb
        if dtype != F32:
            a = a.bitcast(dtype)
        a = a[:, 0:n]
        if len(shape_free) == 2:
            a = a.rearrange("p (a b) -> p a b", a=shape_free[0])
        elif len(shape_free) == 3:
            a = a.rearrange("p (a b c) -> p a b c", a=shape_free[0], b=shape_free[1])
        return a


def _ctab_layout():
    lay = {}
    off = 0
    for name, w in [("maskT", 4 * 128), ("qdec", 4 * 128), ("maskTs", 4 * 32), ("qdecs", 4 * 32),
                    ("colmask", SG * 32), ("rowmask", SG), ("kdec", 4), ("kdecs", 4),
                    ("lneps", 1), ("gneps", 1), ("ident", 128), ("one", 1)]:
        lay[name] = (off, w)
        off += w
    return lay, off


CT_LAY, CT_W = _ctab_layout()


def _ptab_layout():
    lay = {}
    off = 0
    for name, w in [("ln_g", 48), ("ln_b", 48), ("gn_g", 16), ("conv_w", 40), ("conv_b", 10),
                    ("b_a", 10), ("b_i", 10), ("lam", 10)]:
        lay[name] = (off, w)
        off += w
    return lay, off


PT_LAY, PT_W = _ptab_layout()


def build_ctab():
    c = np.zeros((128, CT_W), np.float64)
    lg = np.log1p(-np.exp2(-5.0 - np.arange(HEADS)))
    idx = np.arange(128)
    o, _ = CT_LAY["maskT"]
    for h in range(HEADS):
        rel = idx[None, :] - idx[:, None]
        m = np.where(rel >= 0, np.exp(lg[h] * np.maximum(rel, 0)), 0.0) / 16.0
        c[:, o + h * 128:o + (h + 1) * 128] = m
    o, _ = CT_LAY["qdec"]
    for h in range(HEADS):
        c[:, o + h * 128:o + (h + 1) * 128] = np.exp(lg[h] * (idx + 1.0))[None, :]
    m32 = np.arange(32)
    bb, tt = m32 // 4, m32 % 4
    o, _ = CT_LAY["maskTs"]
    for h in range(HEADS):
        rel = tt[None, :] - tt[:, None]
        m = np.where((rel >= 0) & (bb[None, :] == bb[:, None]), np.exp(lg[h] * np.maximum(rel, 0)), 0.0) / 16.0
        c[0:32, o + h * 32:o + (h + 1) * 32] = m
    o, _ = CT_LAY["qdecs"]
    for h in range(HEADS):
        c[:, o + h * 32:o + (h + 1) * 32] = np.exp(lg[h] * (tt + 1.0))[None, :]
    o, _ = CT_LAY["colmask"]
    for b in range(SG):
        c[:, o + b * 32:o + (b + 1) * 32] = (bb == b).astype(np.float64)[None, :]
    o, _ = CT_LAY["rowmask"]
    for b in range(SG):
        c[0:32, o + b] = (bb == b)
    o, _ = CT_LAY["kdec"]
    for h in range(HEADS):
        c[:, o + h] = np.exp(lg[h] * (127.0 - idx)) / 16.0
    o, _ = CT_LAY["kdecs"]
    for h in range(HEADS):
        c[0:32, o + h] = np.exp(lg[h] * (3.0 - tt)) / 16.0
    c[:, CT_LAY["lneps"][0]] = LN_EPS
    c[:, CT_LAY["gneps"][0]] = GN_EPS
    o, _ = CT_LAY["ident"]
    c[:, o:o + 128] = np.eye(128)
    c[:, CT_LAY["one"][0]] = 1.0
    cd = [float(np.exp(lg[h] * 128.0)) for h in range(HEADS)]
    cds = [float(np.exp(lg[h] * 4.0)) for h in range(HEADS)]
    return c.astype(np.float32), cd, cds


def build_rot():
    half = DK // 2
    inv = (10000.0 ** (-np.arange(half, dtype=np.float32) / np.float32(half))).astype(np.float32)
    rot = np.zeros((NG, 128, 2, T), np.float32)
    for g in range(NG):
        pos = np.concatenate([np.arange(g * PG, (g + 1) * PG), np.tile(PAST + np.arange(DEC_T), SG)]).astype(np.float32)
        ang = (pos[None, :] * inv[:, None]).astype(np.float32)
        rot[g, :, 0, :] = np.cos(ang)
        rot[g, :, 1, :] = np.sin(ang)
    return rot


def rec_gate_kcs():
    res = []
    for fo in range(5):
        lo, hi = fo * 128, fo * 128 + 127
        b0, b1 = lo // 160, hi // 160
        ilo, ihi = b0 * 160, b1 * 160 + 159
        res.append(list(range(ilo // 128, ihi // 128 + 1)))
    return res


GATE_KCS = rec_gate_kcs()


def fm(v, nchunk):
    return np.ascontiguousarray(v.reshape(nchunk, 128).T)


def prep_shared(inp):
    f = lambda a: np.ascontiguousarray(np.asarray(a, dtype=np.float32))
    sh = {}
    for l in range(2):
        for nm, wi, wo in (("f1", "ffn1_w_in", "ffn1_w_out"), ("f2", "ffn2_w_in", "ffn2_w_out")):
            W = f(inp[wi][l])
            Wr = W.reshape(KC, 128, 2, FJ, 128).transpose(3, 1, 0, 2, 4)
            sh[f"{nm}in{l}"] = np.ascontiguousarray(Wr).reshape(FJ, 128, KC * 256)
            Wo = f(inp[wo][l])
            Wor = Wo.reshape(2, 11, 128, 8, 128).transpose(3, 0, 2, 1, 4)
            sh[f"{nm}out{l}"] = np.ascontiguousarray(Wor).reshape(8, 2, 128, 11 * 128)
    W = f(inp["ret_w_in"][0])
    blocks = []
    for h in range(HEADS):
        cols = [W[:, h * 256:(h + 1) * 256], W[:, 1024 + h * 256:1024 + (h + 1) * 256],
                W[:, 2048 + h * 512:2048 + h * 512 + 256], W[:, 2048 + h * 512 + 256:2048 + (h + 1) * 512],
                W[:, 4096 + h * 512:4096 + h * 512 + 256], W[:, 4096 + h * 512 + 256:4096 + (h + 1) * 512]]
        for cblk in cols:
            blocks.append(cblk.reshape(KC, 128, 256).transpose(1, 0, 2).reshape(128, KC * 256))
    sh["retin"] = np.ascontiguousarray(np.stack(blocks)).reshape(HEADS, 6, 128, KC * 256)
    Wo = f(inp["ret_w_out"][0])
    sh["retout"] = np.ascontiguousarray(Wo.reshape(HEADS, 4, 128, 8, 128).transpose(0, 3, 2, 1, 4)).reshape(HEADS, 8, 128, 4 * 128)
    W = f(inp["rec_w_in"][0])
    sh["recin"] = np.ascontiguousarray(W.reshape(KC, 128, 20, 128).transpose(2, 1, 0, 3)).reshape(20, 128, KC * 128)
    for nm, key in (("reca", "rec_w_a"), ("reci", "rec_w_i")):
        Wb = f(inp[key][0])
        dense = np.zeros((2, 640, 640), np.float32)
        for n in range(8):
            hf, nl = n // 4, n % 4
            dense[hf, nl * 160:(nl + 1) * 160, nl * 160:(nl + 1) * 160] = Wb[n]
        sh[nm] = np.ascontiguousarray(dense.reshape(2, 5, 128, 5, 128).transpose(0, 3, 2, 1, 4)).reshape(2, 5, 128, 5 * 128)
    Wo = f(inp["rec_w_out"][0])
    sh["recout"] = np.ascontiguousarray(Wo.reshape(RC, 128, 8, 128).transpose(2, 1, 0, 3)).reshape(8, 128, RC * 128)
    pt = np.zeros((128, PT_W), np.float32)
    o = PT_LAY["ln_g"][0]
    for l in range(2):
        for i in range(3):
            pt[:, o + (l * 3 + i) * 8:o + (l * 3 + i + 1) * 8] = fm(f(inp["ln_g"][l, i]), 8)
    o = PT_LAY["ln_b"][0]
    for l in range(2):
        for i in range(3):
            pt[:, o + (l * 3 + i) * 8:o + (l * 3 + i + 1) * 8] = fm(f(inp["ln_b"][l, i]), 8)
    o = PT_LAY["gn_g"][0]
    pt[:, o:o + 16] = fm(f(inp["ret_gn_g"][0]), 16)
    o = PT_LAY["conv_w"][0]
    cw = f(inp["rec_conv_w"][0])
    for j in range(4):
        pt[:, o + j * 10:o + (j + 1) * 10] = fm(cw[j], RC)
    pt[:, PT_LAY["conv_b"][0]:PT_LAY["conv_b"][0] + 10] = fm(f(inp["rec_conv_b"][0]), RC)
    pt[:, PT_LAY["b_a"][0]:PT_LAY["b_a"][0] + 10] = fm(f(inp["rec_b_a"][0]), RC)
    pt[:, PT_LAY["b_i"][0]:PT_LAY["b_i"][0] + 10] = fm(f(inp["rec_b_i"][0]), RC)
    pt[:, PT_LAY["lam"][0]:PT_LAY["lam"][0] + 10] = fm(f(inp["rec_lam"][0]), RC)
    sh["ptab"] = pt
    ct, cd, cds = build_ctab()
    sh["ctab"] = ct
    sh["rot"] = build_rot()
    return sh


def prep_core(inp, c):
    f = lambda a: np.asarray(a, dtype=np.float32)
    xp = f(inp["x_prompt"][c])
    xs = f(inp["x_sample"][c * NSAMP:(c + 1) * NSAMP]).reshape(NSAMP * DEC_T, D)
    xall = np.concatenate([xp, xs], axis=0)
    xT = np.ascontiguousarray(xall.reshape(SEQ + NSAMP * DEC_T, KC, 128).transpose(2, 1, 0))
    d = {"xT": xT}
    d["sret"] = np.ascontiguousarray(f(inp["state_ret"][0, c * NSAMP:(c + 1) * NSAMP]))
    sc = f(inp["state_conv"][0, c * NSAMP:(c + 1) * NSAMP])
    d["sconvT"] = np.ascontiguousarray(sc.reshape(NSAMP, 3, RC, 128).transpose(3, 2, 0, 1))
    sl = f(inp["state_lru"][0, c * NSAMP:(c + 1) * NSAMP])
    d["slruT"] = np.ascontiguousarray(sl.reshape(NSAMP, RC, 128).transpose(2, 1, 0))
    return d


def build_program(cd, cds, stop=None):
    nc = bass.Bass("TRN2", target_bir_lowering=False)
    NTOK = SEQ + NSAMP * DEC_T

    def din(name, shape):
        return nc.dram_tensor(name, list(shape), F32, kind="ExternalInput").ap()

    def dout(name, shape):
        return nc.dram_tensor(name, list(shape), F32, kind="ExternalOutput").ap()

    xT_d = din("xT", [128, KC, NTOK])
    sret_d = din("sret", [NSAMP, HEADS, DK, DV])
    sconv_d = din("sconvT", [128, RC, NSAMP, 3])
    slru_d = din("slruT", [128, RC, NSAMP])
    ptab_d = din("ptab", [128, PT_W])
    ctab_d = din("ctab", [128, CT_W])
    rot_d = din("rot", [NG, 128, 2, T])
    wd = {}
    for l in range(2):
        for nm in ("f1", "f2"):
            wd[f"{nm}in{l}"] = din(f"{nm}in{l}", [FJ, 128, KC * 256])
            wd[f"{nm}out{l}"] = din(f"{nm}out{l}", [8, 2, 128, 11 * 128])
    wd["retin"] = din("retin", [HEADS, 6, 128, KC * 256])
    wd["retout"] = din("retout", [HEADS, 8, 128, 4 * 128])
    wd["recin"] = din("recin", [20, 128, KC * 128])
    wd["reca"] = din("reca", [2, 5, 128, 5 * 128])
    wd["reci"] = din("reci", [2, 5, 128, 5 * 128])
    wd["recout"] = din("recout", [8, 128, RC * 128])

    yT_d = dout("yT", [128, KC, NTOK])
    retp_d = dout("ret_p", [HEADS, DK, DV])
    rets_d = dout("ret_s", [NSAMP, HEADS, DK, DV])
    convp_d = dout("convT_p", [128, RC, 3])
    convs_d = dout("convT_s", [128, RC, NSAMP, 3])
    lrup_d = dout("lruT_p", [128, RC])
    lrus_d = dout("lruT_s", [128, RC, NSAMP])

    st = ExitStack()
    with st:
        fw = FW(nc, st)

        def sb(name, shape, dt):
            return st.enter_context(nc.sbuf_tensor("s_" + name, list(shape), dt))

        xa_t = sb("xa", [128, KC, T], F32)
        xb_t = sb("xb", [128, KC, T], BF16)
        NSTG, NWB = 2, 4
        stg_t = [sb(f"stg{i}", [128, 2048], F32) for i in range(NSTG)]
        wb_t = [sb(f"wb{i}", [128, 2048], BF16) for i in range(NWB)]
        s32_t = sb("s32", [128, HEADS, 2, DV], F32)
        ctab_t = sb("ctab", [128, CT_W], F32)
        ptab_t = sb("ptab", [128, PT_W], F32)
        pder_t = sb("pder", [128, 48 + 48 + 10 + 10 + 30], F32)
        ident_t = sb("ident", [128, 128], BF16)
        ones_t = sb("ones", [128, 128], BF16)
        onesf_t = sb("onesf", [128, 128], F32)
        LNW = 512
        lnz_t = sb("lnz", [128, 4 * T], BF16)
        zsq_t = lnz_t[:, 0:KC * LNW].rearrange("p (k n) -> p k n", k=KC)
        lns_t = sb("lns", [128, 4, LNW], F32)
        lnx_t = sb("lnx", [128, 2, SCOL], F32)
        hc_t = sb("hcarry", [128, RC], F32)
        cc_t = sb("ccarry", [128, RC, 3], F32)
        ARENA_BYTES = 84 * 1024
        arena_t = sb("arena", [128, ARENA_BYTES // 4], F32)
        arena = Arena(arena_t[:], ARENA_BYTES)
        psum_t = [st.enter_context(nc.psum_tensor(f"ps{i}", [128, 512], F32)) for i in range(8)]

        xa_b = [Buf(f"xa{i}") for i in range(len(CS))]
        xb_b = [Buf(f"xb{i}") for i in range(len(CS))]
        stg_b = [Buf(f"stg{i}") for i in range(NSTG)]
        wb_b = [Buf(f"wb{i}") for i in range(NWB)]
        s32_b = [Buf(f"s32_{h}") for h in range(HEADS)]
        ctab_b = Buf("ctab")
        ptab_b = Buf("ptab")
        pder_b = Buf("pder")
        ident_b = Buf("ident")
        ones_b = Buf("ones")
        onesf_b = Buf("onesf")
        zb_b = Buf("lnz")
        zsq_b = zb_b
        lns_b = [Buf(f"lns{i}") for i in range(4)]
        lnx_b = [Buf(f"lnx{i}") for i in range(2)]
        hc_b = Buf("hc")
        cc_b = Buf("cc")
        ps_b = [Buf(f"ps{i}") for i in range(8)]
        ps_rr = [0]

        ps_pin = set()

        def psum():
            while True:
                i = ps_rr[0] % 8
                ps_rr[0] += 1
                if i not in ps_pin:
                    return psum_t[i], ps_b[i]

        def CT(name, lo=0, hi=None, rows=128):
            o, w = CT_LAY[name]
            hi = w if hi is None else hi
            return ctab_t[0:rows, o + lo:o + hi]

        def PT(name, lo=0, hi=None):
            o, w = PT_LAY[name]
            hi = w if hi is None else hi
            return ptab_t[:, o + lo:o + hi]

        wctr = [0]
        cast_pat = ["pool", "pool", "act"]
        cast_pats = {"ffn": ["pool", "pool", "act"], "ret": ["act", "act", "pool"], "rec": ["pool"]}

        def wload(dram_ap, nelem, force_pool=False):
            i = wctr[0]
            wctr[0] += 1
            s, w = i % NSTG, i % NWB
            fw.dma(stg_t[s][:, 0:nelem], dram_ap, stg_b[s], write=True)
            ce = "pool" if force_pool else cast_pat[i % len(cast_pat)]
            if ce == "act":
                fw.op("act", lambda e, o=wb_t[w][:, 0:nelem], a=stg_t[s][:, 0:nelem]: e.copy(out=o, in_=a),
                      reads=(stg_b[s],), writes=(wb_b[w],))
            else:
                fw.op(ce, lambda e, o=wb_t[w][:, 0:nelem], a=stg_t[s][:, 0:nelem]: e.tensor_copy(out=o, in_=a),
                      reads=(stg_b[s],), writes=(wb_b[w],))
            return wb_t[w][:, 0:nelem], wb_b[w]

        class WQ:
            def __init__(self, items, depth=2):
                self.items = list(items)
                self.loaded = []
                self.depth = depth
                self.n = 0

            def get(self):
                while len(self.loaded) < self.depth + 1 and self.n < len(self.items):
                    ap_, ne = self.items[self.n]
                    self.loaded.append(wload(ap_, ne, force_pool=(self.n < 3)))
                    self.n += 1
                return self.loaded.pop(0)

        fw.dma(ctab_t[:], ctab_d[:, :], ctab_b, write=True)
        fw.dma(ptab_t[:], ptab_d[:, :], ptab_b, write=True)
        fw.op("dve", lambda e: e.tensor_copy(out=ident_t[:], in_=CT("ident")), reads=(ctab_b,), writes=(ident_b,))
        fw.op("dve", lambda e: e.memset(ones_t[:], 1.0 / D), writes=(ones_b,))
        fw.op("dve", lambda e: e.memset(onesf_t[:], 1.0 / D), writes=(onesf_b,))
        fw.op("dve", lambda e: e.tensor_scalar(out=pder_t[:, 0:96], in0=ptab_t[:, 0:96], scalar1=ALPHA, scalar2=None,
                                               op0=ALU.mult), reads=(ptab_b,), writes=(pder_b,))
        fw.op("act", lambda e: e.activation(out=pder_t[:, 96:106], in_=PT("lam"), func=AF.Exp, scale=-1.0),
              reads=(ptab_b,), writes=(pder_b,))
        fw.op("act", lambda e: e.activation(out=pder_t[:, 96:106], in_=pder_t[:, 96:106], func=AF.Ln,
                                            bias=CT("one"), scale=1.0), reads=(pder_b, ctab_b), writes=(pder_b,))
        fw.op("dve", lambda e: e.tensor_scalar(out=pder_t[:, 106:116], in0=pder_t[:, 96:106], scalar1=-2.0 * LRU_C,
                                               scalar2=None, op0=ALU.mult), reads=(pder_b,), writes=(pder_b,))
        fw.op("dve", lambda e: e.tensor_scalar(out=pder_t[:, 96:106], in0=pder_t[:, 96:106], scalar1=-LRU_C,
                                               scalar2=None, op0=ALU.mult), reads=(pder_b,), writes=(pder_b,))
        fw.op("dve", lambda e: e.tensor_scalar(out=pder_t[:, 116:126], in0=PT("b_a"), scalar1=0.5, scalar2=None, op0=ALU.mult),
              reads=(ptab_b,), writes=(pder_b,))
        fw.op("dve", lambda e: e.tensor_scalar(out=pder_t[:, 126:136], in0=PT("b_i"), scalar1=0.5, scalar2=None, op0=ALU.mult),
              reads=(ptab_b,), writes=(pder_b,))
        fw.op("dve", lambda e: e.tensor_scalar(out=pder_t[:, 136:146], in0=pder_t[:, 96:106], scalar1=0.5, scalar2=None, op0=ALU.mult),
              reads=(pder_b,), writes=(pder_b,))
        fw.op("dve", lambda e: e.memset(s32_t[:], 0.0), writes=tuple(s32_b))
        fw.op("dve", lambda e: e.memset(hc_t[:], 0.0), writes=(hc_b,))
        fw.op("dve", lambda e: e.memset(cc_t[:], 0.0), writes=(cc_b,))

        def AG(idx, kc):
            return pder_t[:, idx * 8 + kc:idx * 8 + kc + 1]

        def AB(idx, kc):
            return pder_t[:, 48 + idx * 8 + kc:48 + idx * 8 + kc + 1]

        def mm_group(out_ap, out_buf, pairs, reads, npe=None):
            def fn(e, pairs=pairs, out_ap=out_ap):
                ins = None
                n = len(pairs)
                for i, (l, r) in enumerate(pairs):
                    ins = e.matmul(out_ap, l, r, start=(i == 0), stop=(i == n - 1))
                return ins
            fw.op("pe", fn, reads=reads, writes=(out_buf,), npe=(npe or len(pairs)))

        def layer_norm(idx, final=False, g=0):
            fw.tag = f"ln{idx}"
            stats = []
            for ci, (c0, n) in enumerate(CS):
                z3 = xa_t[:, :, c0:c0 + n]
                pm, pmb = psum()
                mm_group(pm[:, 0:n], pmb, [(onesf_t[:], xa_t[:, kc, c0:c0 + n]) for kc in range(KC)], (onesf_b, xa_b[ci]), npe=2 * KC)
                fw.op("act", lambda e, z3=z3, n=n: e.activation(out=zsq_t[:, :, 0:n], in_=z3, func=AF.Square),
                      reads=(xa_b[ci],), writes=(zsq_b,))
                pe2, pe2b = psum()
                mm_group(pe2[:, 0:n], pe2b, [(ones_t[:], zsq_t[:, kc, 0:n]) for kc in range(KC)], (ones_b, zsq_b))
                if ci < 2:
                    vv, nmr = lns_t[:, 2 * ci, 0:n], lns_t[:, 2 * ci + 1, 0:n]
                    vb, nb_ = lns_b[2 * ci], lns_b[2 * ci + 1]
                else:
                    vv, nmr = lnx_t[:, 0, 0:n], lnx_t[:, 1, 0:n]
                    vb, nb_ = lnx_b[0], lnx_b[1]
                fw.op("act", lambda e, vv=vv, pm=pm, n=n: e.activation(out=vv, in_=pm[:, 0:n], func=AF.Square),
                      reads=(pmb,), writes=(vb,))
                fw.op("dve", lambda e, vv=vv, pe2=pe2, n=n: e.tensor_tensor(out=vv, in0=pe2[:, 0:n], in1=vv, op=ALU.subtract),
                      reads=(pe2b, vb), writes=(vb,))
                fw.op("act", lambda e, vv=vv: e.activation(out=vv, in_=vv, func=AF.Sqrt, bias=CT("lneps"), scale=1.0),
                      reads=(vb, ctab_b), writes=(vb,))
                fw.op("dve", lambda e, vv=vv: e.reciprocal(out=vv, in_=vv), reads=(vb,), writes=(vb,))
                fw.op("dve", lambda e, pm=pm, n=n, vv=vv, nmr=nmr: e.scalar_tensor_tensor(
                    out=nmr, in0=pm[:, 0:n], scalar=-1.0, in1=vv, op0=ALU.mult, op1=ALU.mult),
                    reads=(pmb, vb), writes=(nb_,))
                stats.append((vv, nmr, vb, nb_))
            for ci, (c0, n) in enumerate(CS):
                z3 = xa_t[:, :, c0:c0 + n]
                vv, nmr, vb, nb_ = stats[ci]
                fw.op("dve", lambda e, z3=z3, vv=vv, n=n: e.tensor_tensor(
                    out=z3, in0=z3, in1=vv.unsqueeze(1).broadcast_to([128, KC, n]), op=ALU.mult),
                    reads=(xa_b[ci], vb), writes=(xa_b[ci],))
                fw.op("dve", lambda e, z3=z3, nmr=nmr, n=n: e.tensor_tensor(
                    out=z3, in0=z3, in1=nmr.unsqueeze(1).broadcast_to([128, KC, n]), op=ALU.add),
                    reads=(xa_b[ci], nb_), writes=(xa_b[ci],))
                for kc in range(KC):
                    zc = xa_t[:, kc, c0:c0 + n]
                    xo = xb_t[:, kc, c0:c0 + n]
                    if final:
                        fw.op("act", lambda e, zc=zc, kc=kc: e.activation(
                            out=zc, in_=zc, func=AF.Identity, bias=PT("ln_b", idx * 8 + kc, idx * 8 + kc + 1),
                            scale=PT("ln_g", idx * 8 + kc, idx * 8 + kc + 1)),
                            reads=(xa_b[ci], ptab_b), writes=(xa_b[ci],))
                    elif ci == 1:
                        fw.op("act", lambda e, zc=zc, xo=xo, kc=kc: e.activation(
                            out=xo, in_=zc, func=AF.Identity, bias=PT("ln_b", idx * 8 + kc, idx * 8 + kc + 1),
                            scale=PT("ln_g", idx * 8 + kc, idx * 8 + kc + 1)),
                            reads=(xa_b[ci], ptab_b), writes=(xb_b[ci],))
                    else:
                        fw.op("dve", lambda e, zc=zc, xo=xo, kc=kc: e.tensor_scalar(
                            out=xo, in0=zc, scalar1=PT("ln_g", idx * 8 + kc, idx * 8 + kc + 1),
                            scalar2=PT("ln_b", idx * 8 + kc, idx * 8 + kc + 1), op0=ALU.mult, op1=ALU.add),
                            reads=(xa_b[ci], ptab_b), writes=(xb_b[ci],))
            if not final:
                for ci, (c0, n) in enumerate(CS):
                    for kc in range(KC):
                        zc = xa_t[:, kc, c0:c0 + n]
                        deferred.append(lambda zc=zc, kc=kc, ci=ci, idx=idx: fw.op(
                            "act", lambda e: e.activation(out=zc, in_=zc, func=AF.Identity, bias=AB(idx, kc), scale=AG(idx, kc)),
                            reads=(xa_b[ci], pder_b), writes=(xa_b[ci],)))

        deferred = []

        def flush_deferred(k=None):
            n = len(deferred) if k is None else min(k, len(deferred))
            for _ in range(n):
                deferred.pop(0)()

        def ffn(nm, l):
            cast_pat[:] = cast_pats["ffn"]
            arena.reset()
            fw.arena_reset()
            h_t = arena.alloc([FJ, T], BF16)
            s_t = [arena.alloc([512], BF16) for _ in range(2)]
            h_b = [fw.abuf(f"h{ci}") for ci in range(len(CS))]
            s_b = [fw.abuf(f"s{i}") for i in range(2)]
            win, wout = wd[f"{nm}in{l}"], wd[f"{nm}out{l}"]
            items = [(win[j], KC * 256) for j in range(FJ)] + \
                    [(wout[oc, kh], 11 * 128) for oc in range(8) for kh in range(2)]
            wq = WQ(items)
            fw.tag = f"{nm}{l}.in"

            def ffn_in(j, ci, w3, wbuf):
                c0, n = CS[ci]
                pg, pgb = psum()
                pu, pub = psum()
                mm_group(pg[:, 0:n], pgb, [(w3[:, kc, 0:128], xb_t[:, kc, c0:c0 + n]) for kc in range(KC)],
                         (wbuf, xb_b[ci]))
                mm_group(pu[:, 0:n], pub, [(w3[:, kc, 128:256], xb_t[:, kc, c0:c0 + n]) for kc in range(KC)],
                         (wbuf, xb_b[ci]))
                sa, sab = s_t[sctr_[0] % 2], s_b[sctr_[0] % 2]
                sctr_[0] += 1
                fw.op("act", lambda e, sa=sa, pg=pg, n=n: e.activation(out=sa[:, 0:n], in_=pg[:, 0:n], func=AF.Silu),
                      reads=(pgb,), writes=(sab,))
                fw.op("dve", lambda e, sa=sa, pu=pu, n=n, j=j, c0=c0: e.tensor_tensor(
                    out=h_t[:, j, c0:c0 + n], in0=pu[:, 0:n], in1=sa[:, 0:n], op=ALU.mult),
                    reads=(pub, sab), writes=(h_b[ci],))

            sctr_ = [0]
            wA = wq.get()
            wB = wq.get()
            wA3 = wA[0].rearrange("p (k c) -> p k c", k=KC)
            wB3 = wB[0].rearrange("p (k c) -> p k c", k=KC)
            ffn_in(0, 0, wA3, wA[1])
            ffn_in(1, 0, wB3, wB[1])
            for ci in (1, 2):
                ffn_in(0, ci, wA3, wA[1])
            for ci in (1, 2):
                ffn_in(1, ci, wB3, wB[1])
            for j in range(2, FJ):
                w, wbuf = wq.get()
                w3 = w.rearrange("p (k c) -> p k c", k=KC)
                for ci in range(len(CS)):
                    ffn_in(j, ci, w3, wbuf)
                flush_deferred(3)
            flush_deferred()
            fw.tag = f"{nm}{l}.out"
            for oc in range(8):
                wA, wAb = wq.get()
                wB, wBb = wq.get()
                wA3 = wA.rearrange("p (k c) -> p k c", k=11)
                wB3 = wB.rearrange("p (k c) -> p k c", k=11)
                for ci, (c0, n) in enumerate(CS):
                    po, pob = psum()
                    pairs = [(wA3[:, k, :], h_t[:, k, c0:c0 + n]) for k in range(11)] + \
                            [(wB3[:, k, :], h_t[:, 11 + k, c0:c0 + n]) for k in range(11)]
                    mm_group(po[:, 0:n], pob, pairs, (wAb, wBb, h_b[ci]))
                    xs = xa_t[:, oc, c0:c0 + n]
                    fw.op("dve", lambda e, xs=xs, po=po, n=n: e.scalar_tensor_tensor(
                        out=xs, in0=po[:, 0:n], scalar=0.5, in1=xs, op0=ALU.mult, op1=ALU.add),
                        reads=(pob, xa_b[ci]), writes=(xa_b[ci],))

        def load_x(g):
            for ci, (c0, n) in enumerate(CS):
                src0 = g * PG + c0 if ci < 2 else SEQ + g * SCOL
                fw.dma(xa_t[:, :, c0:c0 + n], xT_d[:, :, src0:src0 + n], xa_b[ci], write=True)
                fw.op("pool", lambda e, c0=c0, n=n: e.tensor_copy(out=xb_t[:, :, c0:c0 + n], in_=xa_t[:, :, c0:c0 + n]),
                      reads=(xa_b[ci],), writes=(xb_b[ci],))
                fw.op("act", lambda e, c0=c0, n=n: e.mul(out=xa_t[:, :, c0:c0 + n], in_=xa_t[:, :, c0:c0 + n], mul=ALPHA),
                      reads=(xa_b[ci],), writes=(xa_b[ci],))

        def store_y(g):
            for ci, (c0, n) in enumerate(CS):
                dst0 = g * PG + c0 if ci < 2 else SEQ + g * SCOL
                fw.dma(yT_d[:, :, dst0:dst0 + n], xa_t[:, :, c0:c0 + n], xa_b[ci], write=False, is_output=True)

        def retention(g):
            cast_pat[:] = cast_pats["ret"]
            arena.reset()
            fw.arena_reset()
            onT = arena.alloc([4, T], BF16)
            onT_b = [fw.abuf(f"onT{ci}") for ci in range(len(CS))]
            rot_t = lnz_t[:].bitcast(F32)[:, 0:2 * T].rearrange("p (a b) -> p a b", a=2)
            rot_b = zb_b
            fw.dma(rot_t, rot_d[g], rot_b, write=True)
            qT = arena.alloc([2, T], BF16)
            qsT = arena.alloc([2, T], BF16)
            kT = arena.alloc([2, T], BF16)
            kd = arena.alloc([NCH + 1, 256], BF16)
            vt = arena.alloc([NCH + 1, 512], BF16)
            sg = arena.alloc([4, T], BF16)
            rt = [lns_t[:, i, :] for i in range(4)]
            NSS = 3
            sst = [arena.alloc([2, 512], F32) for _ in range(NSS)]
            ssb = [arena.alloc([2, 512], BF16) for _ in range(NSS)]
            NSB = 4
            sbf = [arena.alloc([2, 512], BF16) for _ in range(NSB)]
            sout = [lns_t[:, 2 * j:2 * j + 2, :] for j in range(2)]
            ontm = [arena.alloc([512], BF16) for _ in range(3)]
            sT = [arena.alloc([128], BF16) for _ in range(4)]
            qsm = arena.alloc([2, SG, 32], BF16)
            kdb = arena.alloc([SG, 256], BF16)
            gst = [arena.alloc([16], F32) for _ in range(3)]
            nb = lambda nm, k=1: [fw.abuf(f"{nm}{i}") for i in range(k)]
            qT_b, qsT_b, kT_b = nb("qT", 3), nb("qsT", 3), nb("kT", 3)
            kd_b, vt_b = nb("kd", NCH + 1), nb("vt", NCH + 1)
            sg_b = nb("sg", 3)
            rt_b = lns_b
            sst_b, ssb_b, sbf_b, ontm_b, sT_b = nb("sst", NSS), nb("ssb", NSS), nb("sbf", NSB), nb("ontm", 3), nb("sT", 4)
            qsm_b, kdb_b, gst_b = fw.abuf("qsm"), fw.abuf("kdb"), nb("gst", 3)

            win = wd["retin"]
            items = []
            for h in range(HEADS):
                items += [(win[h, k], KC * 256) for k in range(6)]
                items += [(wd["retout"][h, oc], 4 * 128) for oc in range(8)]
            wq = WQ(items)
            sctr = [0]
            def s_in(h, b):
                k = (h * SG + b) % NSS
                fw.dma(sst[k], sret_d[g * SG + b, h].rearrange("(f p) v -> p f v", p=128), sst_b[k], write=True)
                fw.op("pool", lambda e, k=k: e.tensor_copy(out=ssb[k][:, :, :], in_=sst[k][:, :, :]),
                      reads=(sst_b[k],), writes=(ssb_b[k],))

            for b in range(NSS):
                s_in(0, b)
            pend = []
            for h in range(HEADS):
                fw.tag = "ret.qk"
                for which, dstT, dst_b in (("q", qT, qT_b), ("k", kT, kT_b)):
                    w, wbuf = wq.get()
                    w3 = w.rearrange("p (k c) -> p k c", k=KC)
                    for ci, (c0, n) in enumerate(CS):
                        p1, p1b = psum()
                        p2, p2b = psum()
                        mm_group(p1[:, 0:n], p1b, [(w3[:, kc, 0:128], xb_t[:, kc, c0:c0 + n]) for kc in range(KC)],
                                 (wbuf, xb_b[ci]))
                        mm_group(p2[:, 0:n], p2b, [(w3[:, kc, 128:256], xb_t[:, kc, c0:c0 + n]) for kc in range(KC)],
                                 (wbuf, xb_b[ci]))
                        cosv, sinv = rot_t[:, 0, c0:c0 + n], rot_t[:, 1, c0:c0 + n]
                        t1, t2, t3, t4 = (rt[i][:, 0:n] for i in range(4))
                        fw.op("dve", lambda e, t1=t1, p1=p1, cosv=cosv, n=n: e.tensor_tensor(out=t1, in0=p1[:, 0:n], in1=cosv, op=ALU.mult),
                              reads=(p1b, rot_b), writes=(rt_b[0],))
                        fw.op("dve", lambda e, t2=t2, p2=p2, sinv=sinv, n=n: e.tensor_tensor(out=t2, in0=p2[:, 0:n], in1=sinv, op=ALU.mult),
                              reads=(p2b, rot_b), writes=(rt_b[1],))
                        fw.op("dve", lambda e, t3=t3, p1=p1, sinv=sinv, n=n: e.tensor_tensor(out=t3, in0=p1[:, 0:n], in1=sinv, op=ALU.mult),
                              reads=(p1b, rot_b), writes=(rt_b[2],))
                        fw.op("dve", lambda e, t4=t4, p2=p2, cosv=cosv, n=n: e.tensor_tensor(out=t4, in0=p2[:, 0:n], in1=cosv, op=ALU.mult),
                              reads=(p2b, rot_b), writes=(rt_b[3],))
                        fw.op("pool", lambda e, t1=t1, t2=t2, d=dstT[:, 0, c0:c0 + n]: e.tensor_tensor(out=d, in0=t1, in1=t2, op=ALU.subtract),
                              reads=(rt_b[0], rt_b[1]), writes=(dst_b[ci],))
                        fw.op("pool", lambda e, t3=t3, t4=t4, d=dstT[:, 1, c0:c0 + n]: e.tensor_tensor(out=d, in0=t3, in1=t4, op=ALU.add),
                              reads=(rt_b[2], rt_b[3]), writes=(dst_b[ci],))
                        if which == "q":
                            if ci < 2:
                                qd = CT("qdec", h * 128, (h + 1) * 128)
                                for fc in range(2):
                                    fw.op("pool", lambda e, fc=fc, c0=c0, qd=qd: e.tensor_tensor(
                                        out=qsT[:, fc, c0:c0 + 512].rearrange("p (a b) -> p a b", a=4),
                                        in0=qT[:, fc, c0:c0 + 512].rearrange("p (a b) -> p a b", a=4),
                                        in1=qd.unsqueeze(1).broadcast_to([128, 4, 128]), op=ALU.mult),
                                        reads=(qT_b[ci], ctab_b), writes=(qsT_b[ci],))
                            else:
                                qd = CT("qdecs", h * 32, (h + 1) * 32)
                                for fc in range(2):
                                    fw.op("pool", lambda e, fc=fc, c0=c0, qd=qd: e.tensor_tensor(
                                        out=qsT[:, fc, c0:c0 + 32], in0=qT[:, fc, c0:c0 + 32], in1=qd, op=ALU.mult),
                                        reads=(qT_b[ci], ctab_b), writes=(qsT_b[ci],))
                flush_deferred()
                fw.tag = "ret.v"
                wv = [wq.get(), wq.get()]
                for c in range(NCH + 1):
                    ci = c // 4
                    c0 = c * 128
                    m = 128 if c < NCH else SCOL
                    pv, pvb = psum()
                    for half in range(2):
                        w3 = wv[half][0].rearrange("p (k c) -> p k c", k=KC)
                        mm_group(pv[0:m, half * 256:(half + 1) * 256], pvb,
                                 [(xb_t[:, kc, c0:c0 + m], w3[:, kc, :]) for kc in range(KC)],
                                 (wv[half][1], xb_b[ci]))
                    fw.op("act", lambda e, c=c, m=m, pv=pv: e.activation(out=vt[0:m, c, :], in_=pv[0:m, :], func=AF.Copy),
                          reads=(pvb,), writes=(vt_b[c],))
                fw.tag = "ret.g"
                for half in range(2):
                    w, wbuf = wq.get()
                    w3 = w.rearrange("p (k c) -> p k c", k=KC)
                    for vc2 in range(2):
                        vc = half * 2 + vc2
                        for ci, (c0, n) in enumerate(CS):
                            pg, pgb = psum()
                            mm_group(pg[:, 0:n], pgb, [(w3[:, kc, vc2 * 128:(vc2 + 1) * 128], xb_t[:, kc, c0:c0 + n])
                                                      for kc in range(KC)], (wbuf, xb_b[ci]))
                            fw.op("act", lambda e, pg=pg, n=n, vc=vc, c0=c0: e.activation(
                                out=sg[:, vc, c0:c0 + n], in_=pg[:, 0:n], func=AF.Silu), reads=(pgb,), writes=(sg_b[ci],))
                fw.tag = "ret.kT"
                for c in range(NCH + 1):
                    ci = c // 4
                    c0 = c * 128
                    m = 128 if c < NCH else SCOL
                    pk, pkb = psum()
                    pkb16 = pk[:].bitcast(BF16)

                    def fn(e, c0=c0, m=m, pkb16=pkb16):
                        ins = None
                        for fc in range(2):
                            ins = e.transpose(pkb16[0:m, fc * 128:(fc + 1) * 128], kT[:, fc, c0:c0 + m], ident_t[:])
                        return ins
                    fw.op("pe", fn, reads=(kT_b[ci], ident_b), writes=(pkb,), npe=2)
                    sc = CT("kdec", h, h + 1) if c < NCH else CT("kdecs", h, h + 1, rows=32)
                    fw.op("act", lambda e, c=c, m=m, pkb16=pkb16, sc=sc: e.activation(
                        out=kd[0:m, c, :], in_=pkb16[0:m, 0:256], func=AF.Identity, scale=sc),
                        reads=(pkb, ctab_b), writes=(kd_b[c],))
                cm = CT("colmask")
                for fc in range(2):
                    fw.op("pool", lambda e, fc=fc, cm=cm: e.tensor_tensor(
                        out=qsm[:, fc, :, :], in0=qsT[:, fc, PG:PG + 32].unsqueeze(1).broadcast_to([128, SG, 32]),
                        in1=cm.rearrange("p (a b) -> p a b", a=SG), op=ALU.mult),
                        reads=(qsT_b[2], ctab_b), writes=(qsm_b,))
                for b in range(SG):
                    fw.op("dve", lambda e, b=b: e.tensor_scalar(
                        out=kdb[0:32, b, :], in0=kd[0:32, NCH, :], scalar1=CT("rowmask", b, b + 1, rows=32), scalar2=None,
                        op0=ALU.mult), reads=(kd_b[NCH], ctab_b), writes=(kdb_b,))
                fw.tag = "ret.chunk"
                fw.op("act", lambda e, h=h: e.activation(out=sbf[0][:, :, :], in_=s32_t[:, h, :, :], func=AF.Copy),
                      reads=(s32_b[h],), writes=(sbf_b[0],))

                def SU(c):
                    for fc in range(2):
                        pS, pSb = psum()
                        mm_group(pS[:, :], pSb, [(kd[:, c, fc * 128:(fc + 1) * 128], vt[:, c, :])], (kd_b[c], vt_b[c]))
                        fw.op("dve", lambda e, pS=pS, fc=fc, h=h: e.scalar_tensor_tensor(
                            out=s32_t[:, h, fc, :], in0=s32_t[:, h, fc, :], scalar=cd[h], in1=pS[:, :],
                            op0=ALU.mult, op1=ALU.add), reads=(pSb, s32_b[h]), writes=(s32_b[h],))
                    nxt = (c + 1) % NSB
                    fw.op("act", lambda e, h=h, nxt=nxt: e.activation(out=sbf[nxt][:, :, :], in_=s32_t[:, h, :, :], func=AF.Copy),
                          reads=(s32_b[h],), writes=(sbf_b[nxt],))

                def SC(c, slot=None):
                    ci, c0 = c // 4, c * 128
                    samp = (c == NCH)
                    m = SCOL if samp else 128
                    psc, pscb = psum()
                    mm_group(psc[0:m, 0:m], pscb, [(kT[:, fc, c0:c0 + m], qT[:, fc, c0:c0 + m]) for fc in range(2)],
                             (kT_b[ci], qT_b[ci]))
                    sTt, sTb = (sT[c % 3], sT_b[c % 3]) if slot is None else (sT[slot], sT_b[slot])
                    mk = CT("maskT", h * 128, (h + 1) * 128) if not samp else CT("maskTs", h * 32, (h + 1) * 32, rows=32)
                    fw.op("dve", lambda e, sTt=sTt, psc=psc, mk=mk, m=m: e.tensor_tensor(
                        out=sTt[0:m, 0:m], in0=psc[0:m, 0:m], in1=mk, op=ALU.mult),
                        reads=(pscb, ctab_b), writes=(sTb,))

                def GN(c, po, pob, mid=None):
                    m = SCOL if c == NCH else 128
                    gs, gsb = gst[c % 3], gst_b[c % 3]
                    eps_ap = CT("gneps", rows=m)
                    fw.op("dve", lambda e, po=po, m=m, gs=gs: e.bn_stats(out=gs[0:m, 0:6], in_=po[0:m, :]),
                          reads=(pob,), writes=(gsb,))
                    fw.op("dve", lambda e, m=m, gs=gs: e.bn_aggr(out=gs[0:m, 6:8], in_=gs[0:m, 0:6]),
                          reads=(gsb,), writes=(gsb,))
                    fw.op("act", lambda e, m=m, eps_ap=eps_ap, gs=gs: e.activation(out=gs[0:m, 8:9], in_=gs[0:m, 7:8], func=AF.Sqrt,
                                                                                   bias=eps_ap, scale=1.0),
                          reads=(gsb, ctab_b), writes=(gsb,))
                    if mid is not None:
                        mid()
                    fw.op("dve", lambda e, m=m, gs=gs: e.reciprocal(out=gs[0:m, 8:9], in_=gs[0:m, 8:9]),
                          reads=(gsb,), writes=(gsb,))
                    fw.op("dve", lambda e, m=m, gs=gs: e.scalar_tensor_tensor(out=gs[0:m, 9:10], in0=gs[0:m, 6:7], scalar=-1.0,
                                                                              in1=gs[0:m, 8:9], op0=ALU.mult, op1=ALU.mult),
                          reads=(gsb,), writes=(gsb,))
                    ot, otb = ontm[c % 3], ontm_b[c % 3]
                    fw.op("act", lambda e, ot=ot, po=po, m=m, gs=gs: e.activation(out=ot[0:m, :], in_=po[0:m, :], func=AF.Identity,
                                                                                 bias=gs[0:m, 9:10], scale=gs[0:m, 8:9]),
                          reads=(pob, gsb), writes=(otb,))

                def TR(c):
                    ci, c0 = c // 4, c * 128
                    m = SCOL if c == NCH else 128
                    ot, otb = ontm[c % 3], ontm_b[c % 3]
                    pt_, ptb = psum()
                    pt16 = pt_[:].bitcast(BF16)

                    def fnT(e, ot=ot, m=m, pt16=pt16):
                        ins = None
                        for vc in range(4):
                            ins = e.transpose(pt16[:, vc * 128:vc * 128 + m], ot[0:m, vc * 128:(vc + 1) * 128], ident_t[0:m, 0:m])
                        return ins
                    fw.op("pe", fnT, reads=(otb, ident_b), writes=(ptb,), npe=4)
                    for vc in range(4):
                        fw.op("act", lambda e, vc=vc, m=m, c0=c0, pt16=pt16, h=h: e.activation(
                            out=onT[:, vc, c0:c0 + m], in_=pt16[:, vc * 128:vc * 128 + m], func=AF.Identity,
                            scale=PT("gn_g", h * 4 + vc, h * 4 + vc + 1)), reads=(ptb, ptab_b), writes=(onT_b[ci],))

                def O(c):
                    ci, c0 = c // 4, c * 128
                    sTt, sTb = sT[c % 3], sT_b[c % 3]
                    cur = c % NSB
                    po, pob = psum()
                    pairs = [(sTt[:, 0:128], vt[:, c, :])] + \
                            [(qsT[:, fc, c0:c0 + 128], sbf[cur][:, fc, :]) for fc in range(2)]
                    mm_group(po[:, :], pob, pairs, (sTb, vt_b[c], qsT_b[ci], sbf_b[cur]))
                    return po, pob

                fw.tag = "ret.chunk"
                SC(NCH, slot=3)
                sTt, sTb = sT[3], sT_b[3]
                po_s, pob_s = psum()
                po_idx = ps_b.index(pob_s)
                ps_pin.add(po_idx)

                def SAMP(b, sTt=sTt, sTb=sTb, po=po_s, pob=pob_s):
                    k = (h * SG + b) % NSS
                    oj = b % 2
                    sidx = g * SG + b
                    pairs = []
                    if b == 0:
                        pairs.append((sTt[0:32, 0:32], vt[0:32, NCH, :]))
                    pairs += [(qsm[:, fc, b, :], ssb[k][:, fc, :]) for fc in range(2)]

                    def fn(e, pairs=pairs, b=b, po=po):
                        ins = None
                        for i, (l, r) in enumerate(pairs):
                            ins = e.matmul(po[0:32, :], l, r, start=(b == 0 and i == 0),
                                           stop=(b == SG - 1 and i == len(pairs) - 1))
                        return ins
                    fw.op("pe", fn, reads=(sTb, vt_b[NCH], qsm_b, ssb_b[k]), writes=(pob,), npe=len(pairs))
                    for fc in range(2):
                        pS, pSb = psum()
                        mm_group(pS[:, :], pSb, [(kdb[0:32, b, fc * 128:(fc + 1) * 128], vt[0:32, NCH, :])],
                                 (kdb_b, vt_b[NCH]))
                        fw.op("dve", lambda e, pS=pS, fc=fc, k=k, h=h, oj=oj: e.scalar_tensor_tensor(
                            out=sout[oj][:, fc, :], in0=sst[k][:, fc, :], scalar=cds[h], in1=pS[:, :],
                            op0=ALU.mult, op1=ALU.add), reads=(pSb, sst_b[k]), writes=(lns_b[2 * oj], lns_b[2 * oj + 1]))
                    fw.dma(rets_d[sidx, h].rearrange("(f p) v -> p f v", p=128), sout[oj], lns_b[2 * oj], write=False,
                           is_output=True, extra=(lns_b[2 * oj + 1],))
                    pend.append((h, b))
                    if len(pend) > 0:
                        ph, pb = pend.pop(0)
                        nb_, nh_ = pb + NSS, ph
                        if nb_ >= SG:
                            nb_, nh_ = nb_ - SG, ph + 1
                        if nh_ < HEADS:
                            s_in(nh_, nb_)

                SU(0)
                SU(1)
                SC(0)
                for c in range(NCH):
                    if c + 2 < NCH:
                        SU(c + 2)
                    if c + 1 < NCH:
                        SC(c + 1)
                    po, pob = O(c)
                    GN(c, po, pob, mid=(lambda c=c: TR(c - 1)) if c >= 1 else None)
                    SAMP(c)
                TR(NCH - 1)
                c = NCH
                po, pob = po_s, pob_s
                ps_pin.discard(po_idx)
                GN(c, po, pob)
                TR(c)
                if g == NG - 1:
                    fw.dma(retp_d[h].rearrange("(f p) v -> p f v", p=128), s32_t[:, h, :, :], s32_b[h], write=False,
                           is_output=True)
                for ci, (c0, n) in enumerate(CS):
                    fw.op("dve", lambda e, c0=c0, n=n: e.tensor_tensor(
                        out=onT[:, :, c0:c0 + n], in0=onT[:, :, c0:c0 + n], in1=sg[:, :, c0:c0 + n], op=ALU.mult),
                        reads=(onT_b[ci], sg_b[ci]), writes=(onT_b[ci],))
                fw.tag = "ret.out"
                for oc in range(8):
                    w, wbuf = wq.get()
                    w3 = w.rearrange("p (k c) -> p k c", k=4)
                    for ci, (c0, n) in enumerate(CS):
                        po, pob = psum()
                        mm_group(po[:, 0:n], pob, [(w3[:, k, :], onT[:, k, c0:c0 + n]) for k in range(4)], (wbuf, onT_b[ci]))
                        xs = xa_t[:, oc, c0:c0 + n]
                        fw.op("dve", lambda e, xs=xs, po=po, n=n: e.tensor_tensor(out=xs, in0=po[:, 0:n], in1=xs, op=ALU.add),
                              reads=(pob, xa_b[ci]), writes=(xa_b[ci],))

        def rglru(g):
            cast_pat[:] = cast_pats["rec"]
            arena.reset()
            fw.arena_reset()
            gh = arena.alloc([RC, T], BF16)
            gh_b = [fw.abuf(f"gh{ci}") for ci in range(len(CS))]
            xc = arena.alloc([5, T], F32)
            xcb = arena.alloc([5, T], BF16)
            XPW = 3 + PG + SG * 7
            xp = [arena.alloc([XPW], F32), lns_t[:, :, :].rearrange("p a b -> p (a b)")[:, 0:XPW]]
            tmp = [[arena.alloc([T], F32) for _ in range(2)] for _ in range(2)]
            a2buf = [arena.alloc([T], F32) for _ in range(2)]
            cvs = arena.alloc([RC, SG, 3], F32)
            lrs = arena.alloc([RC, SG], F32)
            xc_b = [fw.abuf(f"xc{i}") for i in range(5)]
            xcb_b = [fw.abuf(f"xcb{i}") for i in range(5)]
            xp_b = [(fw.abuf("xp0"),), (lns_b[0], lns_b[1], lns_b[2])]
            tmp_b = [[fw.abuf(f"tmp{i}{j}") for j in range(2)] for i in range(2)]
            a2buf_b = [fw.abuf(f"a2buf{i}") for i in range(2)]
            cvs_b, lrs_b = fw.abuf("cvs"), fw.abuf("lrs")
            fw.dma(lrs, slru_d[:, :, g * SG:(g + 1) * SG], lrs_b, write=True)
            items = []
            for hf in range(2):
                items += [(wd["recin"][hf * 5 + fc], KC * 128) for fc in range(5)]
                items += [(wd["recin"][10 + hf * 5 + fc], KC * 128) for fc in range(5)]
                for fo in range(5):
                    items += [(wd["reca"][hf, fo], 5 * 128), (wd["reci"][hf, fo], 5 * 128)]
            items += [(wd["recout"][oc], RC * 128) for oc in range(8)]
            wq = WQ(items)
            tctr = 0
            for hf in range(2):
                fw.tag = "rec.gate"
                for fc in range(5):
                    F = hf * 5 + fc
                    w, wbuf = wq.get()
                    w3 = w.rearrange("p (k c) -> p k c", k=KC)
                    for ci, (c0, n) in enumerate(CS):
                        pg, pgb = psum()
                        mm_group(pg[:, 0:n], pgb, [(w3[:, kc, :], xb_t[:, kc, c0:c0 + n]) for kc in range(KC)], (wbuf, xb_b[ci]))
                        fw.op("act", lambda e, pg=pg, n=n, F=F, c0=c0: e.activation(
                            out=gh[:, F, c0:c0 + n], in_=pg[:, 0:n], func=AF.Gelu_apprx_tanh), reads=(pgb,), writes=(gh_b[ci],))
                flush_deferred()
                fw.tag = "rec.conv"
                for fc in range(5):
                    F = hf * 5 + fc
                    w, wbuf = wq.get()
                    w3 = w.rearrange("p (k c) -> p k c", k=KC)
                    xpt, xpbs = xp[F % 2], xp_b[F % 2]
                    xps = xpt[:, 3 + PG:3 + PG + SG * 7].rearrange("p (b t) -> p b t", b=SG)
                    fw.op("dve", lambda e, xpt=xpt, F=F: e.tensor_copy(out=xpt[:, 0:3], in_=cc_t[:, F, :]),
                          reads=(cc_b,), writes=xpbs)
                    fw.dma(xps[:, :, 0:3], sconv_d[:, F, g * SG:(g + 1) * SG, :], xpbs[0], write=True, extra=xpbs[1:])
                    for ci, (c0, n) in enumerate(CS):
                        pg, pgb = psum()
                        mm_group(pg[:, 0:n], pgb, [(w3[:, kc, :], xb_t[:, kc, c0:c0 + n]) for kc in range(KC)], (wbuf, xb_b[ci]))
                        if ci < 2:
                            fw.op("act", lambda e, pg=pg, n=n, c0=c0, xpt=xpt: e.activation(
                                out=xpt[:, 3 + c0:3 + c0 + n], in_=pg[:, 0:n], func=AF.Copy), reads=(pgb,), writes=xpbs)
                        else:
                            fw.op("act", lambda e, pg=pg, xps=xps: e.activation(
                                out=xps[:, :, 3:7], in_=pg[:, 0:SCOL].rearrange("p (b t) -> p b t", b=SG), func=AF.Copy),
                                reads=(pgb,), writes=xpbs)
                    cw = lambda j, F=F: PT("conv_w", j * 10 + F, j * 10 + F + 1)
                    cbias = PT("conv_b", F, F + 1)
                    for (dst, src_of) in ((xc[:, fc, 0:PG], lambda j, xpt=xpt: xpt[:, j:j + PG]),
                                          (xc[:, fc, PG:T].rearrange("p (b t) -> p b t", b=SG), lambda j, xps=xps: xps[:, :, j:j + 4])):
                        fw.op("act", lambda e, dst=dst, src_of=src_of, cw=cw, cbias=cbias: e.activation(
                            out=dst, in_=src_of(3), func=AF.Identity, bias=cbias, scale=cw(3)),
                            reads=xpbs + (ptab_b,), writes=(xc_b[fc],))
                        for j in (2, 1, 0):
                            fw.op("dve", lambda e, dst=dst, src_of=src_of, cw=cw, j=j: e.scalar_tensor_tensor(
                                out=dst, in0=src_of(j), scalar=cw(j), in1=dst, op0=ALU.mult, op1=ALU.add),
                                reads=xpbs + (ptab_b, xc_b[fc]), writes=(xc_b[fc],))
                    fw.op("act", lambda e, fc=fc: e.activation(out=xcb[:, fc, :], in_=xc[:, fc, :], func=AF.Copy),
                          reads=(xc_b[fc],), writes=(xcb_b[fc],))
                    fw.op("dve", lambda e, xpt=xpt, F=F: e.tensor_copy(out=cc_t[:, F, :], in_=xpt[:, PG:PG + 3]),
                          reads=xpbs, writes=(cc_b,))
                    fw.op("dve", lambda e, xps=xps, F=F: e.tensor_copy(out=cvs[:, F, :, :], in_=xps[:, :, 4:7]),
                          reads=xpbs, writes=(cvs_b,))
                fw.tag = "rec.lru"
                for fo in range(5):
                    F = hf * 5 + fo
                    wa, wab = wq.get()
                    wi_, wib = wq.get()
                    wa3 = wa.rearrange("p (k c) -> p k c", k=5)
                    wi3 = wi_.rearrange("p (k c) -> p k c", k=5)
                    tm, tmb = tmp[tctr % 2], tmp_b[tctr % 2]
                    tctr += 1
                    r_t, i_t = tm
                    a2_t = a2buf[tctr % 2]
                    hs_t = a2_t
                    tmb = list(tmb) + [a2buf_b[tctr % 2], a2buf_b[tctr % 2]]
                    kcs = GATE_KCS[fo]
                    for ci, (c0, n) in enumerate(CS):
                        pr, prb = psum()
                        pi_, pib = psum()
                        mm_group(pr[:, 0:n], prb, [(wa3[:, kc, :], xcb[:, kc, c0:c0 + n]) for kc in kcs],
                                 (wab,) + tuple(xcb_b[kc] for kc in kcs))
                        mm_group(pi_[:, 0:n], pib, [(wi3[:, kc, :], xcb[:, kc, c0:c0 + n]) for kc in kcs],
                                 (wib,) + tuple(xcb_b[kc] for kc in kcs))
                        fw.op("act", lambda e, pr=pr, n=n, c0=c0, r_t=r_t, F=F: e.activation(
                            out=r_t[:, c0:c0 + n], in_=pr[:, 0:n], func=AF.Tanh, bias=pder_t[:, 116 + F:117 + F], scale=0.5),
                            reads=(prb, pder_b), writes=(tmb[0],))
                        fw.op("act", lambda e, pi_=pi_, n=n, c0=c0, i_t=i_t, F=F: e.activation(
                            out=i_t[:, c0:c0 + n], in_=pi_[:, 0:n], func=AF.Tanh, bias=pder_t[:, 126 + F:127 + F], scale=0.5),
                            reads=(pib, pder_b), writes=(tmb[1],))
                    cF = pder_t[:, 96 + F:97 + F]
                    c2F = pder_t[:, 106 + F:107 + F]
                    chF = pder_t[:, 136 + F:137 + F]
                    fw.op("act", lambda e, r_t=r_t, a2_t=a2_t, cF=cF: e.activation(out=a2_t[:, :], in_=r_t[:, :], func=AF.Exp, scale=cF, bias=cF),
                          reads=(tmb[0], pder_b), writes=(tmb[2],))
                    fw.op("act", lambda e, r_t=r_t, chF=chF: e.activation(out=r_t[:, :], in_=r_t[:, :], func=AF.Exp, scale=chF, bias=chF),
                          reads=(tmb[0], pder_b), writes=(tmb[0],))
                    fw.op("act", lambda e, a2_t=a2_t: e.activation(out=a2_t[:, :], in_=a2_t[:, :], func=AF.Sqrt,
                                                                   bias=CT("one"), scale=-1.0),
                          reads=(tmb[2], ctab_b), writes=(tmb[2],))
                    fw.op("dve", lambda e, i_t=i_t, fo=fo: e.scalar_tensor_tensor(out=i_t[:, :], in0=i_t[:, :], scalar=1.0, in1=xc[:, fo, :],
                                                                                 op0=ALU.add, op1=ALU.mult),
                          reads=(tmb[1], xc_b[fo]), writes=(tmb[1],))
                    fw.op("dve", lambda e, i_t=i_t, a2_t=a2_t: e.scalar_tensor_tensor(out=i_t[:, :], in0=i_t[:, :], scalar=0.5, in1=a2_t[:, :],
                                                                                     op0=ALU.mult, op1=ALU.mult),
                          reads=(tmb[1], tmb[2]), writes=(tmb[1],))
                    fw.op("dve", lambda e, r_t=r_t, i_t=i_t, hs_t=hs_t, F=F: e.tensor_tensor_scan(
                        out=hs_t[:, 0:PG], data0=r_t[:, 0:PG], data1=i_t[:, 0:PG], initial=hc_t[:, F:F + 1],
                        op0=ALU.mult, op1=ALU.add), reads=(tmb[0], tmb[1], hc_b), writes=(tmb[3],))
                    fw.op("dve", lambda e, hs_t=hs_t, F=F: e.tensor_copy(out=hc_t[:, F:F + 1], in_=hs_t[:, PG - 1:PG]),
                          reads=(tmb[3],), writes=(hc_b,))
                    av = r_t[:, PG:T].rearrange("p (b t) -> p b t", b=SG)
                    uv = i_t[:, PG:T].rearrange("p (b t) -> p b t", b=SG)
                    hv = hs_t[:, PG:T].rearrange("p (b t) -> p b t", b=SG)
                    for t in range(DEC_T):
                        prev = lrs[:, F, :] if t == 0 else hv[:, :, t - 1]
                        fw.op("dve", lambda e, av=av, hv=hv, prev=prev, t=t: e.tensor_tensor(out=hv[:, :, t], in0=av[:, :, t], in1=prev, op=ALU.mult),
                              reads=(tmb[0], tmb[3], lrs_b), writes=(tmb[3],))
                        fw.op("dve", lambda e, uv=uv, hv=hv, t=t: e.tensor_tensor(out=hv[:, :, t], in0=hv[:, :, t], in1=uv[:, :, t], op=ALU.add),
                              reads=(tmb[1], tmb[3]), writes=(tmb[3],))
                    fw.op("dve", lambda e, hv=hv, F=F: e.tensor_copy(out=lrs[:, F, :], in_=hv[:, :, DEC_T - 1]),
                          reads=(tmb[3],), writes=(lrs_b,))
                    for ci, (c0, n) in enumerate(CS):
                        fw.op("dve", lambda e, hs_t=hs_t, F=F, c0=c0, n=n: e.tensor_tensor(
                            out=gh[:, F, c0:c0 + n], in0=gh[:, F, c0:c0 + n], in1=hs_t[:, c0:c0 + n], op=ALU.mult),
                            reads=(tmb[3], gh_b[ci]), writes=(gh_b[ci],))
            fw.dma(convs_d[:, :, g * SG:(g + 1) * SG, :], cvs, cvs_b, write=False, is_output=True)
            fw.dma(lrus_d[:, :, g * SG:(g + 1) * SG], lrs, lrs_b, write=False, is_output=True)
            if g == NG - 1:
                fw.dma(convp_d[:, :, :], cc_t[:], cc_b, write=False, is_output=True)
                fw.dma(lrup_d[:, :], hc_t[:], hc_b, write=False, is_output=True)
            fw.tag = "rec.out"
            for oc in range(8):
                w, wbuf = wq.get()
                w3 = w.rearrange("p (k c) -> p k c", k=RC)
                for ci, (c0, n) in enumerate(CS):
                    po, pob = psum()
                    mm_group(po[:, 0:n], pob, [(w3[:, k, :], gh[:, k, c0:c0 + n]) for k in range(RC)], (wbuf, gh_b[ci]))
                    xs = xa_t[:, oc, c0:c0 + n]
                    fw.op("dve", lambda e, xs=xs, po=po, n=n: e.tensor_tensor(out=xs, in0=po[:, 0:n], in1=xs, op=ALU.add),
                          reads=(pob, xa_b[ci]), writes=(xa_b[ci],))

        stages = []
        for g in range(NG):
            stages.append(("load", g))
            for l in range(2):
                stages += [("ffn", "f1", l), ("ln", l * 3 + 0), ("mix", l, g), ("ln", l * 3 + 1), ("ffn", "f2", l), ("ln", l * 3 + 2)]
            stages.append(("store", g))
        nstage = 0
        for g in range(NG):
            load_x(g)
            done = False
            seq = [("ffn", "f1", 0), ("ln", 0), ("mix", 0), ("ln", 1), ("ffn", "f2", 0), ("ln", 2),
                   ("ffn", "f1", 1), ("ln", 3), ("mix", 1), ("ln", 4), ("ffn", "f2", 1), ("ln", 5)]
            for si, s in enumerate(seq):
                if stop is not None and si >= stop:
                    break
                if s[0] == "ffn":
                    ffn(s[1], s[2])
                elif s[0] == "ln":
                    layer_norm(s[1], final=(s[1] == 5))
                else:
                    if s[1] == 0:
                        retention(g)
                    else:
                        rglru(g)
            store_y(g)
        fw.finish()
        build_program.pe_tags = fw.pe_tags
        with nc.Block() as block:
            fw.replay(block)
    return nc


_CACHE = {}


def kernel(**inputs):
    stop = inputs.pop("_stop", None)
    ncores = inputs.pop("_cores", NCORES)
    ct, cd, cds = build_ctab()
    sh = prep_shared(inputs)
    in_maps = []
    for c in range(NCORES):
        d = dict(sh)
        d.update(prep_core(inputs, c))
        in_maps.append(d)
    key = ("nc", stop)
    nc = build_program(cd, cds, stop=stop)
    res = run_bass_kernel_spmd(nc, in_maps[:ncores], core_ids=list(range(ncores)))
    R = list(res.results) + [res.results[0]] * (NCORES - ncores)
    y_p = np.zeros((8, SEQ, D), np.float32)
    y_s = np.zeros((DEC_B, DEC_T, D), np.float32)
    ret_p = np.zeros((1, 8, HEADS, DK, DV), np.float32)
    conv_p = np.zeros((1, 8, 3, D_RNN), np.float32)
    lru_p = np.zeros((1, 8, D_RNN), np.float32)
    ret_s = np.zeros((1, DEC_B, HEADS, DK, DV), np.float32)
    conv_s = np.zeros((1, DEC_B, 3, D_RNN), np.float32)
    lru_s = np.zeros((1, DEC_B, D_RNN), np.float32)
    for c in range(NCORES):
        r = R[c]
        yT = np.asarray(r["yT"])
        yall = yT.transpose(2, 1, 0).reshape(SEQ + NSAMP * DEC_T, D)
        y_p[c] = yall[:SEQ]
        y_s[c * NSAMP:(c + 1) * NSAMP] = yall[SEQ:].reshape(NSAMP, DEC_T, D)
        ret_p[0, c] = np.asarray(r["ret_p"])
        ret_s[0, c * NSAMP:(c + 1) * NSAMP] = np.asarray(r["ret_s"])
        conv_p[0, c] = np.asarray(r["convT_p"]).transpose(2, 1, 0).reshape(3, D_RNN)
        conv_s[0, c * NSAMP:(c + 1) * NSAMP] = np.asarray(r["convT_s"]).transpose(2, 3, 1, 0).reshape(NSAMP, 3, D_RNN)
        lru_p[0, c] = np.asarray(r["lruT_p"]).T.reshape(D_RNN)
        lru_s[0, c * NSAMP:(c + 1) * NSAMP] = np.asarray(r["lruT_s"]).transpose(2, 1, 0).reshape(NSAMP, D_RNN)
    return (y_p, y_s, ret_p, conv_p, lru_p, ret_s, conv_s, lru_s)
```

```python
import math
from contextlib import ExitStack

import numpy as np
import concourse.bass as bass
import concourse.mybir as mybir
from concourse.bass_utils import run_bass_kernel_spmd

F32 = mybir.dt.float32
BF16 = mybir.dt.bfloat16
AF = mybir.ActivationFunctionType
ALU = mybir.AluOpType

NCORES = 8
D = 1024
KC = 8
SEQ = 2048
DEC_B = 128
DEC_T = 4
NSAMP = DEC_B // NCORES
NG = 2
PG = SEQ // NG
SG = NSAMP // NG
SCOL = SG * DEC_T
T = PG + SCOL
CS = [(0, 512), (512, 512), (1024, SCOL)]
NCH = PG // 128
HEADS = 4
DK = 256
DV = 512
D_RNN = 1280
RC = 10
D_FF = 2816
FJ = 22
PAST = 16384
ALPHA = (2.0 * 2) ** 0.25
LN_EPS = 1e-5
GN_EPS = 1e-6
LRU_C = 8.0
EPOCH_MAX = 8000

ENGS = ["sync", "act", "dve", "pool", "pe"]


class DSem:
    def __init__(self, handle):
        self.h = handle
        self.total = 0


class Buf:
    __slots__ = ("name", "w", "r", "dsem")

    def __init__(self, name):
        self.name = name
        self.w = None
        self.r = {}
        self.dsem = None


class FW:
    def __init__(self, nc, stack):
        self.nc = nc
        self.stack = stack
        self.prog = {e: [] for e in ENGS}
        self.cnt = {e: 0 for e in ENGS}
        self.epoch = {e: 0 for e in ENGS}
        self.seen = {e: {} for e in ENGS}
        self.esems = {}
        self.dsems = []
        self.out_dsems = []
        self.arena_bufs = []
        self.legacy = {}
        self.tag = "init"
        self.pe_tags = []

    def esem(self, eng, epoch):
        k = (eng, epoch)
        if k not in self.esems:
            self.esems[k] = self.stack.enter_context(self.nc.semaphore(f"e_{eng}_{epoch}"))
        return self.esems[k]

    def new_dsem(self, name):
        d = DSem(self.stack.enter_context(self.nc.semaphore(f"d_{name}_{len(self.dsems)}")))
        self.dsems.append(d)
        return d

    def _need(self, eng, tok):
        if tok is None:
            return
        if tok[0] == "e":
            _, te, tep, tc = tok
            if te == eng:
                if eng == "pe" or eng == "sync":
                    return
                if tep == self.epoch[eng] and tc + 2 <= self.cnt[eng]:
                    return
                if tep < self.epoch[eng] and self.cnt[eng] >= 2:
                    return
            key = (te, tep)
            if self.seen[eng].get(key, 0) >= tc:
                return
            self.seen[eng][key] = tc
            self.prog[eng].append(("w", self.esem(te, tep), tc))
        else:
            d = tok[1]
            key = ("d", id(d))
            if self.seen[eng].get(key, 0) >= d.total:
                return
            self.seen[eng][key] = d.total
            self.prog[eng].append(("w", d.h, d.total))

    def _deps(self, eng, reads, writes):
        for b in reads:
            self._need(eng, b.w)
        for b in writes:
            self._need(eng, b.w)
            for t in list(b.r.values()):
                self._need(eng, t)

    def op(self, eng, fn, reads=(), writes=(), npe=1):
        if eng == "pe":
            self.pe_tags.append((self.tag, npe))
        self._deps(eng, reads, writes)
        ep = self.epoch[eng]
        self.cnt[eng] += 1
        tok = ("e", eng, ep, self.cnt[eng])
        self.prog[eng].append(("o", fn, self.esem(eng, ep), 1))
        if self.cnt[eng] >= EPOCH_MAX:
            self.epoch[eng] += 1
            self.cnt[eng] = 0
        for b in reads:
            b.r[eng] = tok
        for b in writes:
            b.w = tok
            b.r = {}
        return tok

    def dma(self, out, in_, buf, write, is_output=False, extra=()):
        eng = "sync"
        if write:
            self._deps(eng, (), (buf,) + tuple(extra))
        else:
            self._deps(eng, (buf,) + tuple(extra), ())
        if buf.dsem is None:
            buf.dsem = self.new_dsem(buf.name)
        d = buf.dsem
        d.total += 16
        self.prog[eng].append(("o", lambda e, o=out, i=in_: e.dma_start(out=o, in_=i), d.h, 16))
        tok = ("d", d)
        for bb in (buf,) + tuple(extra):
            if write:
                bb.w = tok
                bb.r = {}
            else:
                bb.r["dma" + str(id(d))] = tok
        if is_output and d not in self.out_dsems:
            self.out_dsems.append(d)
        return tok

    def arena_reset(self):
        leg = dict(self.legacy)
        for b in self.arena_bufs:
            toks = list(b.r.values())
            if b.w is not None:
                toks.append(b.w)
            for t in toks:
                if t[0] == "e":
                    k = ("e", t[1])
                    o = leg.get(k)
                    if o is None or (o[2], o[3]) < (t[2], t[3]):
                        leg[k] = t
                else:
                    leg[("d", id(t[1]))] = t
        self.legacy = leg
        self.arena_bufs = []

    def abuf(self, name):
        b = Buf(name)
        b.r = dict(self.legacy)
        self.arena_bufs.append(b)
        return b

    def finish(self):
        for d in self.out_dsems:
            self._need("sync", ("d", d))

    def replay(self, block):
        prog = self.prog

        def run(name, e):
            for ent in prog[name]:
                if ent[0] == "w":
                    e.wait_ge(ent[1], ent[2])
                else:
                    ins = ent[1](e)
                    ins.then_inc(ent[2], ent[3])

        @block.sync
        def _(e):
            run("sync", e)

        @block.scalar
        def _(e):
            run("act", e)

        @block.vector
        def _(e):
            run("dve", e)

        @block.gpsimd
        def _(e):
            run("pool", e)

        @block.tensor
        def _(e):
            run("pe", e)


class Arena:
    def __init__(self, ap_f32, nbytes):
        self.ap = ap_f32
        self.nbytes = nbytes
        self.off = 0

    def reset(self):
        self.off = 0

    def alloc(self, shape_free, dtype):
        esz = 4 if dtype == F32 else 2
        n = int(np.prod(shape_free))
        nb = (n * esz + 31) // 32 * 32
        assert self.off + nb <= self.nbytes, f"arena overflow {self.off}+{nb}>{self.nbytes}"
        a = self.ap[:, self.off // 4:(self.off + nb) // 4]
        self.off += nb
        if dtype != F32:
            a = a.bitcast(dtype)
        a = a[:, 0:n]
        if len(shape_free) == 2:
            a = a.rearrange("p (a b) -> p a b", a=shape_free[0])
        elif len(shape_free) == 3:
            a = a.rearrange("p (a b c) -> p a b c", a=shape_free[0], b=shape_free[1])
        return a


def _ctab_layout():
    lay = {}
    off = 0
    for name, w in [("maskT", 4 * 128), ("qdec", 4 * 128), ("maskTs", 4 * 32), ("qdecs", 4 * 32),
                    ("colmask", SG * 32), ("rowmask", SG), ("kdec", 4), ("kdecs", 4),
                    ("lneps", 1), ("gneps", 1), ("ident", 128), ("one", 1)]:
        lay[name] = (off, w)
        off += w
    return lay, off


CT_LAY, CT_W = _ctab_layout()


def _ptab_layout():
    lay = {}
    off = 0
    for name, w in [("ln_g", 48), ("ln_b", 48), ("gn_g", 16), ("conv_w", 40), ("conv_b", 10),
                    ("b_a", 10), ("b_i", 10), ("lam", 10)]:
        lay[name] = (off, w)
        off += w
    return lay, off


PT_LAY, PT_W = _ptab_layout()


def build_ctab():
    c = np.zeros((128, CT_W), np.float64)
    lg = np.log1p(-np.exp2(-5.0 - np.arange(HEADS)))
    idx = np.arange(128)
    o, _ = CT_LAY["maskT"]
    for h in range(HEADS):
        rel = idx[None, :] - idx[:, None]
        m = np.where(rel >= 0, np.exp(lg[h] * np.maximum(rel, 0)), 0.0) / 16.0
        c[:, o + h * 128:o + (h + 1) * 128] = m
    o, _ = CT_LAY["qdec"]
    for h in range(HEADS):
        c[:, o + h * 128:o + (h + 1) * 128] = np.exp(lg[h] * (idx + 1.0))[None, :]
    m32 = np.arange(32)
    bb, tt = m32 // 4, m32 % 4
    o, _ = CT_LAY["maskTs"]
    for h in range(HEADS):
        rel = tt[None, :] - tt[:, None]
        m = np.where((rel >= 0) & (bb[None, :] == bb[:, None]), np.exp(lg[h] * np.maximum(rel, 0)), 0.0) / 16.0
        c[0:32, o + h * 32:o + (h + 1) * 32] = m
    o, _ = CT_LAY["qdecs"]
    for h in range(HEADS):
        c[:, o + h * 32:o + (h + 1) * 32] = np.exp(lg[h] * (tt + 1.0))[None, :]
    o, _ = CT_LAY["colmask"]
    for b in range(SG):
        c[:, o + b * 32:o + (b + 1) * 32] = (bb == b).astype(np.float64)[None, :]
    o, _ = CT_LAY["rowmask"]
    for b in range(SG):
        c[0:32, o + b] = (bb == b)
    o, _ = CT_LAY["kdec"]
    for h in range(HEADS):
        c[:, o + h] = np.exp(lg[h] * (127.0 - idx)) / 16.0
    o, _ = CT_LAY["kdecs"]
    for h in range(HEADS):
        c[0:32, o + h] = np.exp(lg[h] * (3.0 - tt)) / 16.0
    c[:, CT_LAY["lneps"][0]] = LN_EPS
    c[:, CT_LAY["gneps"][0]] = GN_EPS
    o, _ = CT_LAY["ident"]
    c[:, o:o + 128] = np.eye(128)
    c[:, CT_LAY["one"][0]] = 1.0
    cd = [float(np.exp(lg[h] * 128.0)) for h in range(HEADS)]
    cds = [float(np.exp(lg[h] * 4.0)) for h in range(HEADS)]
    return c.astype(np.float32), cd, cds


def build_rot():
    half = DK // 2
    inv = (10000.0 ** (-np.arange(half, dtype=np.float32) / np.float32(half))).astype(np.float32)
    rot = np.zeros((NG, 128, 2, T), np.float32)
    for g in range(NG):
        pos = np.concatenate([np.arange(g * PG, (g + 1) * PG), np.tile(PAST + np.arange(DEC_T), SG)]).astype(np.float32)
        ang = (pos[None, :] * inv[:, None]).astype(np.float32)
        rot[g, :, 0, :] = np.cos(ang)
        rot[g, :, 1, :] = np.sin(ang)
    return rot


def rec_gate_kcs():
    res = []
    for fo in range(5):
        lo, hi = fo * 128, fo * 128 + 127
        b0, b1 = lo // 160, hi // 160
        ilo, ihi = b0 * 160, b1 * 160 + 159
        res.append(list(range(ilo // 128, ihi // 128 + 1)))
    return res


GATE_KCS = rec_gate_kcs()


def fm(v, nchunk):
    return np.ascontiguousarray(v.reshape(nchunk, 128).T)


def prep_shared(inp):
    f = lambda a: np.ascontiguousarray(np.asarray(a, dtype=np.float32))
    sh = {}
    for l in range(2):
        for nm, wi, wo in (("f1", "ffn1_w_in", "ffn1_w_out"), ("f2", "ffn2_w_in", "ffn2_w_out")):
            W = f(inp[wi][l])
            Wr = W.reshape(KC, 128, 2, FJ, 128).transpose(3, 1, 0, 2, 4)
            sh[f"{nm}in{l}"] = np.ascontiguousarray(Wr).reshape(FJ, 128, KC * 256)
            Wo = f(inp[wo][l])
            Wor = Wo.reshape(2, 11, 128, 8, 128).transpose(3, 0, 2, 1, 4)
            sh[f"{nm}out{l}"] = np.ascontiguousarray(Wor).reshape(8, 2, 128, 11 * 128)
    W = f(inp["ret_w_in"][0])
    blocks = []
    for h in range(HEADS):
        cols = [W[:, h * 256:(h + 1) * 256], W[:, 1024 + h * 256:1024 + (h + 1) * 256],
                W[:, 2048 + h * 512:2048 + h * 512 + 256], W[:, 2048 + h * 512 + 256:2048 + (h + 1) * 512],
                W[:, 4096 + h * 512:4096 + h * 512 + 256], W[:, 4096 + h * 512 + 256:4096 + (h + 1) * 512]]
        for cblk in cols:
            blocks.append(cblk.reshape(KC, 128, 256).transpose(1, 0, 2).reshape(128, KC * 256))
    sh["retin"] = np.ascontiguousarray(np.stack(blocks)).reshape(HEADS, 6, 128, KC * 256)
    Wo = f(inp["ret_w_out"][0])
    sh["retout"] = np.ascontiguousarray(Wo.reshape(HEADS, 4, 128, 8, 128).transpose(0, 3, 2, 1, 4)).reshape(HEADS, 8, 128, 4 * 128)
    W = f(inp["rec_w_in"][0])
    sh["recin"] = np.ascontiguousarray(W.reshape(KC, 128, 20, 128).transpose(2, 1, 0, 3)).reshape(20, 128, KC * 128)
    for nm, key in (("reca", "rec_w_a"), ("reci", "rec_w_i")):
        Wb = f(inp[key][0])
        dense = np.zeros((2, 640, 640), np.float32)
        for n in range(8):
            hf, nl = n // 4, n % 4
            dense[hf, nl * 160:(nl + 1) * 160, nl * 160:(nl + 1) * 160] = Wb[n]
        sh[nm] = np.ascontiguousarray(dense.reshape(2, 5, 128, 5, 128).transpose(0, 3, 2, 1, 4)).reshape(2, 5, 128, 5 * 128)
    Wo = f(inp["rec_w_out"][0])
    sh["recout"] = np.ascontiguousarray(Wo.reshape(RC, 128, 8, 128).transpose(2, 1, 0, 3)).reshape(8, 128, RC * 128)
    pt = np.zeros((128, PT_W), np.float32)
    o = PT_LAY["ln_g"][0]
    for l in range(2):
        for i in range(3):
            pt[:, o + (l * 3 + i) * 8:o + (l * 3 + i + 1) * 8] = fm(f(inp["ln_g"][l, i]), 8)
    o = PT_LAY["ln_b"][0]
    for l in range(2):
        for i in range(3):
            pt[:, o + (l * 3 + i) * 8:o + (l * 3 + i + 1) * 8] = fm(f(inp["ln_b"][l, i]), 8)
    o = PT_LAY["gn_g"][0]
    pt[:, o:o + 16] = fm(f(inp["ret_gn_g"][0]), 16)
    o = PT_LAY["conv_w"][0]
    cw = f(inp["rec_conv_w"][0])
    for j in range(4):
        pt[:, o + j * 10:o + (j + 1) * 10] = fm(cw[j], RC)
    pt[:, PT_LAY["conv_b"][0]:PT_LAY["conv_b"][0] + 10] = fm(f(inp["rec_conv_b"][0]), RC)
    pt[:, PT_LAY["b_a"][0]:PT_LAY["b_a"][0] + 10] = fm(f(inp["rec_b_a"][0]), RC)
    pt[:, PT_LAY["b_i"][0]:PT_LAY["b_i"][0] + 10] = fm(f(inp["rec_b_i"][0]), RC)
    pt[:, PT_LAY["lam"][0]:PT_LAY["lam"][0] + 10] = fm(f(inp["rec_lam"][0]), RC)
    sh["ptab"] = pt
    ct, cd, cds = build_ctab()
    sh["ctab"] = ct
    sh["rot"] = build_rot()
    return sh


def prep_core(inp, c):
    f = lambda a: np.asarray(a, dtype=np.float32)
    xp = f(inp["x_prompt"][c])
    xs = f(inp["x_sample"][c * NSAMP:(c + 1) * NSAMP]).reshape(NSAMP * DEC_T, D)
    xall = np.concatenate([xp, xs], axis=0)
    xT = np.ascontiguousarray(xall.reshape(SEQ + NSAMP * DEC_T, KC, 128).transpose(2, 1, 0))
    d = {"xT": xT}
    d["sret"] = np.ascontiguousarray(f(inp["state_ret"][0, c * NSAMP:(c + 1) * NSAMP]))
    sc = f(inp["state_conv"][0, c * NSAMP:(c + 1) * NSAMP])
    d["sconvT"] = np.ascontiguousarray(sc.reshape(NSAMP, 3, RC, 128).transpose(3, 2, 0, 1))
    sl = f(inp["state_lru"][0, c * NSAMP:(c + 1) * NSAMP])
    d["slruT"] = np.ascontiguousarray(sl.reshape(NSAMP, RC, 128).transpose(2, 1, 0))
    return d


def build_program(cd, cds, stop=None):
    nc = bass.Bass("TRN2", target_bir_lowering=False)
    NTOK = SEQ + NSAMP * DEC_T

    def din(name, shape):
        return nc.dram_tensor(name, list(shape), F32, kind="ExternalInput").ap()

    def dout(name, shape):
        return nc.dram_tensor(name, list(shape), F32, kind="ExternalOutput").ap()

    xT_d = din("xT", [128, KC, NTOK])
    sret_d = din("sret", [NSAMP, HEADS, DK, DV])
    sconv_d = din("sconvT", [128, RC, NSAMP, 3])
    slru_d = din("slruT", [128, RC, NSAMP])
    ptab_d = din("ptab", [128, PT_W])
    ctab_d = din("ctab", [128, CT_W])
    rot_d = din("rot", [NG, 128, 2, T])
    wd = {}
    for l in range(2):
        for nm in ("f1", "f2"):
            wd[f"{nm}in{l}"] = din(f"{nm}in{l}", [FJ, 128, KC * 256])
            wd[f"{nm}out{l}"] = din(f"{nm}out{l}", [8, 2, 128, 11 * 128])
    wd["retin"] = din("retin", [HEADS, 6, 128, KC * 256])
    wd["retout"] = din("retout", [HEADS, 8, 128, 4 * 128])
    wd["recin"] = din("recin", [20, 128, KC * 128])
    wd["reca"] = din("reca", [2, 5, 128, 5 * 128])
    wd["reci"] = din("reci", [2, 5, 128, 5 * 128])
    wd["recout"] = din("recout", [8, 128, RC * 128])

    yT_d = dout("yT", [128, KC, NTOK])
    retp_d = dout("ret_p", [HEADS, DK, DV])
    rets_d = dout("ret_s", [NSAMP, HEADS, DK, DV])
    convp_d = dout("convT_p", [128, RC, 3])
    convs_d = dout("convT_s", [128, RC, NSAMP, 3])
    lrup_d = dout("lruT_p", [128, RC])
    lrus_d = dout("lruT_s", [128, RC, NSAMP])

    st = ExitStack()
    with st:
        fw = FW(nc, st)

        def sb(name, shape, dt):
            return st.enter_context(nc.sbuf_tensor("s_" + name, list(shape), dt))

        xa_t = sb("xa", [128, KC, T], F32)
        xb_t = sb("xb", [128, KC, T], BF16)
        NSTG, NWB = 2, 4
        stg_t = [sb(f"stg{i}", [128, 2048], F32) for i in range(NSTG)]
        wb_t = [sb(f"wb{i}", [128, 2048], BF16) for i in range(NWB)]
        s32_t = sb("s32", [128, HEADS, 2, DV], F32)
        ctab_t = sb("ctab", [128, CT_W], F32)
        ptab_t = sb("ptab", [128, PT_W], F32)
        pder_t = sb("pder", [128, 48 + 48 + 10 + 10 + 30], F32)
        ident_t = sb("ident", [128, 128], BF16)
        ones_t = sb("ones", [128, 128], BF16)
        onesf_t = sb("onesf", [128, 128], F32)
        LNW = 512
        lnz_t = sb("lnz", [128, 4 * T], BF16)
        zsq_t = lnz_t[:, 0:KC * LNW].rearrange("p (k n) -> p k n", k=KC)
        lns_t = sb("lns", [128, 4, LNW], F32)
        lnx_t = sb("lnx", [128, 2, SCOL], F32)
        hc_t = sb("hcarry", [128, RC], F32)
        cc_t = sb("ccarry", [128, RC, 3], F32)
        ARENA_BYTES = 84 * 1024
        arena_t = sb("arena", [128, ARENA_BYTES // 4], F32)
        arena = Arena(arena_t[:], ARENA_BYTES)
        psum_t = [st.enter_context(nc.psum_tensor(f"ps{i}", [128, 512], F32)) for i in range(8)]

        xa_b = [Buf(f"xa{i}") for i in range(len(CS))]
        xb_b = [Buf(f"xb{i}") for i in range(len(CS))]
        stg_b = [Buf(f"stg{i}") for i in range(NSTG)]
        wb_b = [Buf(f"wb{i}") for i in range(NWB)]
        s32_b = [Buf(f"s32_{h}") for h in range(HEADS)]
        ctab_b = Buf("ctab")
        ptab_b = Buf("ptab")
        pder_b = Buf("pder")
        ident_b = Buf("ident")
        ones_b = Buf("ones")
        onesf_b = Buf("onesf")
        zb_b = Buf("lnz")
        zsq_b = zb_b
        lns_b = [Buf(f"lns{i}") for i in range(4)]
        lnx_b = [Buf(f"lnx{i}") for i in range(2)]
        hc_b = Buf("hc")
        cc_b = Buf("cc")
        ps_b = [Buf(f"ps{i}") for i in range(8)]
        ps_rr = [0]

        ps_pin = set()

        def psum():
            while True:
                i = ps_rr[0] % 8
                ps_rr[0] += 1
                if i not in ps_pin:
                    return psum_t[i], ps_b[i]

        def CT(name, lo=0, hi=None, rows=128):
            o, w = CT_LAY[name]
            hi = w if hi is None else hi
            return ctab_t[0:rows, o + lo:o + hi]

        def PT(name, lo=0, hi=None):
            o, w = PT_LAY[name]
            hi = w if hi is None else hi
            return ptab_t[:, o + lo:o + hi]

        wctr = [0]
        cast_pat = ["pool", "pool", "act"]
        cast_pats = {"ffn": ["pool", "pool", "act"], "ret": ["act", "act", "pool"], "rec": ["pool"]}

        def wload(dram_ap, nelem, force_pool=False):
            i = wctr[0]
            wctr[0] += 1
            s, w = i % NSTG, i % NWB
            fw.dma(stg_t[s][:, 0:nelem], dram_ap, stg_b[s], write=True)
            ce = "pool" if force_pool else cast_pat[i % len(cast_pat)]
            if ce == "act":
                fw.op("act", lambda e, o=wb_t[w][:, 0:nelem], a=stg_t[s][:, 0:nelem]: e.copy(out=o, in_=a),
                      reads=(stg_b[s],), writes=(wb_b[w],))
            else:
                fw.op(ce, lambda e, o=wb_t[w][:, 0:nelem], a=stg_t[s][:, 0:nelem]: e.tensor_copy(out=o, in_=a),
                      reads=(stg_b[s],), writes=(wb_b[w],))
            return wb_t[w][:, 0:nelem], wb_b[w]

        class WQ:
            def __init__(self, items, depth=2):
                self.items = list(items)
                self.loaded = []
                self.depth = depth
                self.n = 0

            def get(self):
                while len(self.loaded) < self.depth + 1 and self.n < len(self.items):
                    ap_, ne = self.items[self.n]
                    self.loaded.append(wload(ap_, ne))
                    self.n += 1
                return self.loaded.pop(0)

        fw.dma(ctab_t[:], ctab_d[:, :], ctab_b, write=True)
        fw.dma(ptab_t[:], ptab_d[:, :], ptab_b, write=True)
        fw.op("dve", lambda e: e.tensor_copy(out=ident_t[:], in_=CT("ident")), reads=(ctab_b,), writes=(ident_b,))
        fw.op("dve", lambda e: e.memset(ones_t[:], 1.0 / D), writes=(ones_b,))
        fw.op("dve", lambda e: e.memset(onesf_t[:], 1.0 / D), writes=(onesf_b,))
        fw.op("dve", lambda e: e.tensor_scalar(out=pder_t[:, 0:96], in0=ptab_t[:, 0:96], scalar1=ALPHA, scalar2=None,
                                               op0=ALU.mult), reads=(ptab_b,), writes=(pder_b,))
        fw.op("act", lambda e: e.activation(out=pder_t[:, 96:106], in_=PT("lam"), func=AF.Exp, scale=-1.0),
              reads=(ptab_b,), writes=(pder_b,))
        fw.op("act", lambda e: e.activation(out=pder_t[:, 96:106], in_=pder_t[:, 96:106], func=AF.Ln,
                                            bias=CT("one"), scale=1.0), reads=(pder_b, ctab_b), writes=(pder_b,))
        fw.op("dve", lambda e: e.tensor_scalar(out=pder_t[:, 106:116], in0=pder_t[:, 96:106], scalar1=-2.0 * LRU_C,
                                               scalar2=None, op0=ALU.mult), reads=(pder_b,), writes=(pder_b,))
        fw.op("dve", lambda e: e.tensor_scalar(out=pder_t[:, 96:106], in0=pder_t[:, 96:106], scalar1=-LRU_C,
                                               scalar2=None, op0=ALU.mult), reads=(pder_b,), writes=(pder_b,))
        fw.op("dve", lambda e: e.tensor_scalar(out=pder_t[:, 116:126], in0=PT("b_a"), scalar1=0.5, scalar2=None, op0=ALU.mult),
              reads=(ptab_b,), writes=(pder_b,))
        fw.op("dve", lambda e: e.tensor_scalar(out=pder_t[:, 126:136], in0=PT("b_i"), scalar1=0.5, scalar2=None, op0=ALU.mult),
              reads=(ptab_b,), writes=(pder_b,))
        fw.op("dve", lambda e: e.tensor_scalar(out=pder_t[:, 136:146], in0=pder_t[:, 96:106], scalar1=0.5, scalar2=None, op0=ALU.mult),
              reads=(pder_b,), writes=(pder_b,))
        fw.op("dve", lambda e: e.memset(s32_t[:], 0.0), writes=tuple(s32_b))
        fw.op("dve", lambda e: e.memset(hc_t[:], 0.0), writes=(hc_b,))
        fw.op("dve", lambda e: e.memset(cc_t[:], 0.0), writes=(cc_b,))

        def AG(idx, kc):
            return pder_t[:, idx * 8 + kc:idx * 8 + kc + 1]

        def AB(idx, kc):
            return pder_t[:, 48 + idx * 8 + kc:48 + idx * 8 + kc + 1]

        def mm_group(out_ap, out_buf, pairs, reads, npe=None):
            def fn(e, pairs=pairs, out_ap=out_ap):
                ins = None
                n = len(pairs)
                for i, (l, r) in enumerate(pairs):
                    ins = e.matmul(out_ap, l, r, start=(i == 0), stop=(i == n - 1))
                return ins
            fw.op("pe", fn, reads=reads, writes=(out_buf,), npe=(npe or len(pairs)))

        def layer_norm(idx, final=False, g=0):
            fw.tag = f"ln{idx}"
            stats = []
            for ci, (c0, n) in enumerate(CS):
                z3 = xa_t[:, :, c0:c0 + n]
                pm, pmb = psum()
                mm_group(pm[:, 0:n], pmb, [(onesf_t[:], xa_t[:, kc, c0:c0 + n]) for kc in range(KC)], (onesf_b, xa_b[ci]), npe=2 * KC)
                fw.op("act", lambda e, z3=z3, n=n: e.activation(out=zsq_t[:, :, 0:n], in_=z3, func=AF.Square),
                      reads=(xa_b[ci],), writes=(zsq_b,))
                pe2, pe2b = psum()
                mm_group(pe2[:, 0:n], pe2b, [(ones_t[:], zsq_t[:, kc, 0:n]) for kc in range(KC)], (ones_b, zsq_b))
                if ci < 2:
                    vv, nmr = lns_t[:, 2 * ci, 0:n], lns_t[:, 2 * ci + 1, 0:n]
                    vb, nb_ = lns_b[2 * ci], lns_b[2 * ci + 1]
                else:
                    vv, nmr = lnx_t[:, 0, 0:n], lnx_t[:, 1, 0:n]
                    vb, nb_ = lnx_b[0], lnx_b[1]
                fw.op("act", lambda e, vv=vv, pm=pm, n=n: e.activation(out=vv, in_=pm[:, 0:n], func=AF.Square),
                      reads=(pmb,), writes=(vb,))
                fw.op("dve", lambda e, vv=vv, pe2=pe2, n=n: e.tensor_tensor(out=vv, in0=pe2[:, 0:n], in1=vv, op=ALU.subtract),
                      reads=(pe2b, vb), writes=(vb,))
                fw.op("act", lambda e, vv=vv: e.activation(out=vv, in_=vv, func=AF.Sqrt, bias=CT("lneps"), scale=1.0),
                      reads=(vb, ctab_b), writes=(vb,))
                fw.op("dve", lambda e, vv=vv: e.reciprocal(out=vv, in_=vv), reads=(vb,), writes=(vb,))
                fw.op("dve", lambda e, pm=pm, n=n, vv=vv, nmr=nmr: e.scalar_tensor_tensor(
                    out=nmr, in0=pm[:, 0:n], scalar=-1.0, in1=vv, op0=ALU.mult, op1=ALU.mult),
                    reads=(pmb, vb), writes=(nb_,))
                stats.append((vv, nmr, vb, nb_))
            for ci, (c0, n) in enumerate(CS):
                z3 = xa_t[:, :, c0:c0 + n]
                vv, nmr, vb, nb_ = stats[ci]
                fw.op("dve", lambda e, z3=z3, vv=vv, n=n: e.tensor_tensor(
                    out=z3, in0=z3, in1=vv.unsqueeze(1).broadcast_to([128, KC, n]), op=ALU.mult),
                    reads=(xa_b[ci], vb), writes=(xa_b[ci],))
                fw.op("dve", lambda e, z3=z3, nmr=nmr, n=n: e.tensor_tensor(
                    out=z3, in0=z3, in1=nmr.unsqueeze(1).broadcast_to([128, KC, n]), op=ALU.add),
                    reads=(xa_b[ci], nb_), writes=(xa_b[ci],))
                for kc in range(KC):
                    zc = xa_t[:, kc, c0:c0 + n]
                    xo = xb_t[:, kc, c0:c0 + n]
                    if final:
                        fw.op("act", lambda e, zc=zc, kc=kc: e.activation(
                            out=zc, in_=zc, func=AF.Identity, bias=PT("ln_b", idx * 8 + kc, idx * 8 + kc + 1),
                            scale=PT("ln_g", idx * 8 + kc, idx * 8 + kc + 1)),
                            reads=(xa_b[ci], ptab_b), writes=(xa_b[ci],))
                    elif ci == 1:
                        fw.op("act", lambda e, zc=zc, xo=xo, kc=kc: e.activation(
                            out=xo, in_=zc, func=AF.Identity, bias=PT("ln_b", idx * 8 + kc, idx * 8 + kc + 1),
                            scale=PT("ln_g", idx * 8 + kc, idx * 8 + kc + 1)),
                            reads=(xa_b[ci], ptab_b), writes=(xb_b[ci],))
                    else:
                        fw.op("dve", lambda e, zc=zc, xo=xo, kc=kc: e.tensor_scalar(
                            out=xo, in0=zc, scalar1=PT("ln_g", idx * 8 + kc, idx * 8 + kc + 1),
                            scalar2=PT("ln_b", idx * 8 + kc, idx * 8 + kc + 1), op0=ALU.mult, op1=ALU.add),
                            reads=(xa_b[ci], ptab_b), writes=(xb_b[ci],))
            if not final:
                for ci, (c0, n) in enumerate(CS):
                    for kc in range(KC):
                        zc = xa_t[:, kc, c0:c0 + n]
                        deferred.append(lambda zc=zc, kc=kc, ci=ci, idx=idx: fw.op(
                            "act", lambda e: e.activation(out=zc, in_=zc, func=AF.Identity, bias=AB(idx, kc), scale=AG(idx, kc)),
                            reads=(xa_b[ci], pder_b), writes=(xa_b[ci],)))

        deferred = []

        def flush_deferred(k=None):
            n = len(deferred) if k is None else min(k, len(deferred))
            for _ in range(n):
                deferred.pop(0)()

        def ffn(nm, l):
            cast_pat[:] = cast_pats["ffn"]
            arena.reset()
            fw.arena_reset()
            h_t = arena.alloc([FJ, T], BF16)
            s_t = [arena.alloc([512], BF16) for _ in range(2)]
            h_b = [fw.abuf(f"h{ci}") for ci in range(len(CS))]
            s_b = [fw.abuf(f"s{i}") for i in range(2)]
            win, wout = wd[f"{nm}in{l}"], wd[f"{nm}out{l}"]
            items = [(win[j], KC * 256) for j in range(FJ)] + \
                    [(wout[oc, kh], 11 * 128) for oc in range(8) for kh in range(2)]
            wq = WQ(items)
            fw.tag = f"{nm}{l}.in"

            def ffn_in(j, ci, w3, wbuf):
                c0, n = CS[ci]
                pg, pgb = psum()
                pu, pub = psum()
                mm_group(pg[:, 0:n], pgb, [(w3[:, kc, 0:128], xb_t[:, kc, c0:c0 + n]) for kc in range(KC)],
                         (wbuf, xb_b[ci]))
                mm_group(pu[:, 0:n], pub, [(w3[:, kc, 128:256], xb_t[:, kc, c0:c0 + n]) for kc in range(KC)],
                         (wbuf, xb_b[ci]))
                sa, sab = s_t[sctr_[0] % 2], s_b[sctr_[0] % 2]
                sctr_[0] += 1
                fw.op("act", lambda e, sa=sa, pg=pg, n=n: e.activation(out=sa[:, 0:n], in_=pg[:, 0:n], func=AF.Silu),
                      reads=(pgb,), writes=(sab,))
                fw.op("dve", lambda e, sa=sa, pu=pu, n=n, j=j, c0=c0: e.tensor_tensor(
                    out=h_t[:, j, c0:c0 + n], in0=pu[:, 0:n], in1=sa[:, 0:n], op=ALU.mult),
                    reads=(pub, sab), writes=(h_b[ci],))

            sctr_ = [0]
            wA = wq.get()
            wB = wq.get()
            wA3 = wA[0].rearrange("p (k c) -> p k c", k=KC)
            wB3 = wB[0].rearrange("p (k c) -> p k c", k=KC)
            ffn_in(0, 0, wA3, wA[1])
            ffn_in(1, 0, wB3, wB[1])
            for ci in (1, 2):
                ffn_in(0, ci, wA3, wA[1])
            for ci in (1, 2):
                ffn_in(1, ci, wB3, wB[1])
            for j in range(2, FJ):
                w, wbuf = wq.get()
                w3 = w.rearrange("p (k c) -> p k c", k=KC)
                for ci in range(len(CS)):
                    ffn_in(j, ci, w3, wbuf)
                flush_deferred(3)
            flush_deferred()
            fw.tag = f"{nm}{l}.out"
            for oc in range(8):
                wA, wAb = wq.get()
                wB, wBb = wq.get()
                wA3 = wA.rearrange("p (k c) -> p k c", k=11)
                wB3 = wB.rearrange("p (k c) -> p k c", k=11)
                for ci, (c0, n) in enumerate(CS):
                    po, pob = psum()
                    pairs = [(wA3[:, k, :], h_t[:, k, c0:c0 + n]) for k in range(11)] + \
                            [(wB3[:, k, :], h_t[:, 11 + k, c0:c0 + n]) for k in range(11)]
                    mm_group(po[:, 0:n], pob, pairs, (wAb, wBb, h_b[ci]))
                    xs = xa_t[:, oc, c0:c0 + n]
                    fw.op("dve", lambda e, xs=xs, po=po, n=n: e.scalar_tensor_tensor(
                        out=xs, in0=po[:, 0:n], scalar=0.5, in1=xs, op0=ALU.mult, op1=ALU.add),
                        reads=(pob, xa_b[ci]), writes=(xa_b[ci],))

        def load_x(g):
            for ci, (c0, n) in enumerate(CS):
                src0 = g * PG + c0 if ci < 2 else SEQ + g * SCOL
                fw.dma(xa_t[:, :, c0:c0 + n], xT_d[:, :, src0:src0 + n], xa_b[ci], write=True)
                fw.op("pool", lambda e, c0=c0, n=n: e.tensor_copy(out=xb_t[:, :, c0:c0 + n], in_=xa_t[:, :, c0:c0 + n]),
                      reads=(xa_b[ci],), writes=(xb_b[ci],))
                fw.op("act", lambda e, c0=c0, n=n: e.mul(out=xa_t[:, :, c0:c0 + n], in_=xa_t[:, :, c0:c0 + n], mul=ALPHA),
                      reads=(xa_b[ci],), writes=(xa_b[ci],))

        def store_y(g):
            for ci, (c0, n) in enumerate(CS):
                dst0 = g * PG + c0 if ci < 2 else SEQ + g * SCOL
                fw.dma(yT_d[:, :, dst0:dst0 + n], xa_t[:, :, c0:c0 + n], xa_b[ci], write=False, is_output=True)

        def retention(g):
            cast_pat[:] = cast_pats["ret"]
            arena.reset()
            fw.arena_reset()
            onT = arena.alloc([4, T], BF16)
            onT_b = [fw.abuf(f"onT{ci}") for ci in range(len(CS))]
            rot_t = lnz_t[:].bitcast(F32)[:, 0:2 * T].rearrange("p (a b) -> p a b", a=2)
            rot_b = zb_b
            fw.dma(rot_t, rot_d[g], rot_b, write=True)
            qT = arena.alloc([2, T], BF16)
            qsT = arena.alloc([2, T], BF16)
            kT = arena.alloc([2, T], BF16)
            kd = arena.alloc([NCH + 1, 256], BF16)
            vt = arena.alloc([NCH + 1, 512], BF16)
            sg = arena.alloc([4, T], BF16)
            rt = [lns_t[:, i, :] for i in range(4)]
            NSS = 3
            sst = [arena.alloc([2, 512], F32) for _ in range(NSS)]
            ssb = [arena.alloc([2, 512], BF16) for _ in range(NSS)]
            NSB = 4
            sbf = [arena.alloc([2, 512], BF16) for _ in range(NSB)]
            sout = [lns_t[:, 2 * j:2 * j + 2, :] for j in range(2)]
            ontm = [arena.alloc([512], BF16) for _ in range(3)]
            sT = [arena.alloc([128], BF16) for _ in range(4)]
            qsm = arena.alloc([2, SG, 32], BF16)
            kdb = arena.alloc([SG, 256], BF16)
            gst = [arena.alloc([16], F32) for _ in range(3)]
            nb = lambda nm, k=1: [fw.abuf(f"{nm}{i}") for i in range(k)]
            qT_b, qsT_b, kT_b = nb("qT", 3), nb("qsT", 3), nb("kT", 3)
            kd_b, vt_b = nb("kd", NCH + 1), nb("vt", NCH + 1)
            sg_b = nb("sg", 3)
            rt_b = lns_b
            sst_b, ssb_b, sbf_b, ontm_b, sT_b = nb("sst", NSS), nb("ssb", NSS), nb("sbf", NSB), nb("ontm", 3), nb("sT", 4)
            qsm_b, kdb_b, gst_b = fw.abuf("qsm"), fw.abuf("kdb"), nb("gst", 3)

            win = wd["retin"]
            items = []
            for h in range(HEADS):
                items += [(win[h, k], KC * 256) for k in range(6)]
                items += [(wd["retout"][h, oc], 4 * 128) for oc in range(8)]
            wq = WQ(items)
            sctr = [0]
            def s_in(h, b):
                k = (h * SG + b) % NSS
                fw.dma(sst[k], sret_d[g * SG + b, h].rearrange("(f p) v -> p f v", p=128), sst_b[k], write=True)
                fw.op("pool", lambda e, k=k: e.tensor_copy(out=ssb[k][:, :, :], in_=sst[k][:, :, :]),
                      reads=(sst_b[k],), writes=(ssb_b[k],))

            for b in range(NSS):
                s_in(0, b)
            pend = []
            for h in range(HEADS):
                fw.tag = "ret.qk"
                for which, dstT, dst_b in (("q", qT, qT_b), ("k", kT, kT_b)):
                    w, wbuf = wq.get()
                    w3 = w.rearrange("p (k c) -> p k c", k=KC)
                    for ci, (c0, n) in enumerate(CS):
                        p1, p1b = psum()
                        p2, p2b = psum()
                        mm_group(p1[:, 0:n], p1b, [(w3[:, kc, 0:128], xb_t[:, kc, c0:c0 + n]) for kc in range(KC)],
                                 (wbuf, xb_b[ci]))
                        mm_group(p2[:, 0:n], p2b, [(w3[:, kc, 128:256], xb_t[:, kc, c0:c0 + n]) for kc in range(KC)],
                                 (wbuf, xb_b[ci]))
                        cosv, sinv = rot_t[:, 0, c0:c0 + n], rot_t[:, 1, c0:c0 + n]
                        t1, t2, t3, t4 = (rt[i][:, 0:n] for i in range(4))
                        fw.op("dve", lambda e, t1=t1, p1=p1, cosv=cosv, n=n: e.tensor_tensor(out=t1, in0=p1[:, 0:n], in1=cosv, op=ALU.mult),
                              reads=(p1b, rot_b), writes=(rt_b[0],))
                        fw.op("dve", lambda e, t2=t2, p2=p2, sinv=sinv, n=n: e.tensor_tensor(out=t2, in0=p2[:, 0:n], in1=sinv, op=ALU.mult),
                              reads=(p2b, rot_b), writes=(rt_b[1],))
                        fw.op("dve", lambda e, t3=t3, p1=p1, sinv=sinv, n=n: e.tensor_tensor(out=t3, in0=p1[:, 0:n], in1=sinv, op=ALU.mult),
                              reads=(p1b, rot_b), writes=(rt_b[2],))
                        fw.op("dve", lambda e, t4=t4, p2=p2, cosv=cosv, n=n: e.tensor_tensor(out=t4, in0=p2[:, 0:n], in1=cosv, op=ALU.mult),
                              reads=(p2b, rot_b), writes=(rt_b[3],))
                        fw.op("pool", lambda e, t1=t1, t2=t2, d=dstT[:, 0, c0:c0 + n]: e.tensor_tensor(out=d, in0=t1, in1=t2, op=ALU.subtract),
                              reads=(rt_b[0], rt_b[1]), writes=(dst_b[ci],))
                        fw.op("pool", lambda e, t3=t3, t4=t4, d=dstT[:, 1, c0:c0 + n]: e.tensor_tensor(out=d, in0=t3, in1=t4, op=ALU.add),
                              reads=(rt_b[2], rt_b[3]), writes=(dst_b[ci],))
                        if which == "q":
                            if ci < 2:
                                qd = CT("qdec", h * 128, (h + 1) * 128)
                                for fc in range(2):
                                    fw.op("dve", lambda e, fc=fc, c0=c0, qd=qd: e.tensor_tensor(
                                        out=qsT[:, fc, c0:c0 + 512].rearrange("p (a b) -> p a b", a=4),
                                        in0=qT[:, fc, c0:c0 + 512].rearrange("p (a b) -> p a b", a=4),
                                        in1=qd.unsqueeze(1).broadcast_to([128, 4, 128]), op=ALU.mult),
                                        reads=(qT_b[ci], ctab_b), writes=(qsT_b[ci],))
                            else:
                                qd = CT("qdecs", h * 32, (h + 1) * 32)
                                for fc in range(2):
                                    fw.op("dve", lambda e, fc=fc, c0=c0, qd=qd: e.tensor_tensor(
                                        out=qsT[:, fc, c0:c0 + 32], in0=qT[:, fc, c0:c0 + 32], in1=qd, op=ALU.mult),
                                        reads=(qT_b[ci], ctab_b), writes=(qsT_b[ci],))
                flush_deferred()
                fw.tag = "ret.v"
                wv = [wq.get(), wq.get()]
                for c in range(NCH + 1):
                    ci = c // 4
                    c0 = c * 128
                    m = 128 if c < NCH else SCOL
                    pv, pvb = psum()
                    for half in range(2):
                        w3 = wv[half][0].rearrange("p (k c) -> p k c", k=KC)
                        mm_group(pv[0:m, half * 256:(half + 1) * 256], pvb,
                                 [(xb_t[:, kc, c0:c0 + m], w3[:, kc, :]) for kc in range(KC)],
                                 (wv[half][1], xb_b[ci]))
                    fw.op("act", lambda e, c=c, m=m, pv=pv: e.activation(out=vt[0:m, c, :], in_=pv[0:m, :], func=AF.Copy),
                          reads=(pvb,), writes=(vt_b[c],))
                fw.tag = "ret.g"
                for half in range(2):
                    w, wbuf = wq.get()
                    w3 = w.rearrange("p (k c) -> p k c", k=KC)
                    for vc2 in range(2):
                        vc = half * 2 + vc2
                        for ci, (c0, n) in enumerate(CS):
                            pg, pgb = psum()
                            mm_group(pg[:, 0:n], pgb, [(w3[:, kc, vc2 * 128:(vc2 + 1) * 128], xb_t[:, kc, c0:c0 + n])
                                                      for kc in range(KC)], (wbuf, xb_b[ci]))
                            fw.op("act", lambda e, pg=pg, n=n, vc=vc, c0=c0: e.activation(
                                out=sg[:, vc, c0:c0 + n], in_=pg[:, 0:n], func=AF.Silu), reads=(pgb,), writes=(sg_b[ci],))
                fw.tag = "ret.kT"
                for c in range(NCH + 1):
                    ci = c // 4
                    c0 = c * 128
                    m = 128 if c < NCH else SCOL
                    pk, pkb = psum()
                    pkb16 = pk[:].bitcast(BF16)

                    def fn(e, c0=c0, m=m, pkb16=pkb16):
                        ins = None
                        for fc in range(2):
                            ins = e.transpose(pkb16[0:m, fc * 128:(fc + 1) * 128], kT[:, fc, c0:c0 + m], ident_t[:])
                        return ins
                    fw.op("pe", fn, reads=(kT_b[ci], ident_b), writes=(pkb,), npe=2)
                    sc = CT("kdec", h, h + 1) if c < NCH else CT("kdecs", h, h + 1, rows=32)
                    fw.op("act", lambda e, c=c, m=m, pkb16=pkb16, sc=sc: e.activation(
                        out=kd[0:m, c, :], in_=pkb16[0:m, 0:256], func=AF.Identity, scale=sc),
                        reads=(pkb, ctab_b), writes=(kd_b[c],))
                cm = CT("colmask")
                for fc in range(2):
                    fw.op("pool", lambda e, fc=fc, cm=cm: e.tensor_tensor(
                        out=qsm[:, fc, :, :], in0=qsT[:, fc, PG:PG + 32].unsqueeze(1).broadcast_to([128, SG, 32]),
                        in1=cm.rearrange("p (a b) -> p a b", a=SG), op=ALU.mult),
                        reads=(qsT_b[2], ctab_b), writes=(qsm_b,))
                for b in range(SG):
                    fw.op("dve", lambda e, b=b: e.tensor_scalar(
                        out=kdb[0:32, b, :], in0=kd[0:32, NCH, :], scalar1=CT("rowmask", b, b + 1, rows=32), scalar2=None,
                        op0=ALU.mult), reads=(kd_b[NCH], ctab_b), writes=(kdb_b,))
                fw.tag = "ret.chunk"
                fw.op("act", lambda e, h=h: e.activation(out=sbf[0][:, :, :], in_=s32_t[:, h, :, :], func=AF.Copy),
                      reads=(s32_b[h],), writes=(sbf_b[0],))

                def SU(c):
                    for fc in range(2):
                        pS, pSb = psum()
                        mm_group(pS[:, :], pSb, [(kd[:, c, fc * 128:(fc + 1) * 128], vt[:, c, :])], (kd_b[c], vt_b[c]))
                        fw.op("dve", lambda e, pS=pS, fc=fc, h=h: e.scalar_tensor_tensor(
                            out=s32_t[:, h, fc, :], in0=s32_t[:, h, fc, :], scalar=cd[h], in1=pS[:, :],
                            op0=ALU.mult, op1=ALU.add), reads=(pSb, s32_b[h]), writes=(s32_b[h],))
                    nxt = (c + 1) % NSB
                    fw.op("act", lambda e, h=h, nxt=nxt: e.activation(out=sbf[nxt][:, :, :], in_=s32_t[:, h, :, :], func=AF.Copy),
                          reads=(s32_b[h],), writes=(sbf_b[nxt],))

                def SC(c, slot=None):
                    ci, c0 = c // 4, c * 128
                    samp = (c == NCH)
                    m = SCOL if samp else 128
                    psc, pscb = psum()
                    mm_group(psc[0:m, 0:m], pscb, [(kT[:, fc, c0:c0 + m], qT[:, fc, c0:c0 + m]) for fc in range(2)],
                             (kT_b[ci], qT_b[ci]))
                    sTt, sTb = (sT[c % 3], sT_b[c % 3]) if slot is None else (sT[slot], sT_b[slot])
                    mk = CT("maskT", h * 128, (h + 1) * 128) if not samp else CT("maskTs", h * 32, (h + 1) * 32, rows=32)
                    fw.op("dve", lambda e, sTt=sTt, psc=psc, mk=mk, m=m: e.tensor_tensor(
                        out=sTt[0:m, 0:m], in0=psc[0:m, 0:m], in1=mk, op=ALU.mult),
                        reads=(pscb, ctab_b), writes=(sTb,))

                def GN(c, po, pob, mid=None):
                    m = SCOL if c == NCH else 128
                    gs, gsb = gst[c % 3], gst_b[c % 3]
                    eps_ap = CT("gneps", rows=m)
                    fw.op("dve", lambda e, po=po, m=m, gs=gs: e.bn_stats(out=gs[0:m, 0:6], in_=po[0:m, :]),
                          reads=(pob,), writes=(gsb,))
                    fw.op("dve", lambda e, m=m, gs=gs: e.bn_aggr(out=gs[0:m, 6:8], in_=gs[0:m, 0:6]),
                          reads=(gsb,), writes=(gsb,))
                    fw.op("act", lambda e, m=m, eps_ap=eps_ap, gs=gs: e.activation(out=gs[0:m, 8:9], in_=gs[0:m, 7:8], func=AF.Sqrt,
                                                                                   bias=eps_ap, scale=1.0),
                          reads=(gsb, ctab_b), writes=(gsb,))
                    if mid is not None:
                        mid()
                    fw.op("dve", lambda e, m=m, gs=gs: e.reciprocal(out=gs[0:m, 8:9], in_=gs[0:m, 8:9]),
                          reads=(gsb,), writes=(gsb,))
                    fw.op("dve", lambda e, m=m, gs=gs: e.scalar_tensor_tensor(out=gs[0:m, 9:10], in0=gs[0:m, 6:7], scalar=-1.0,
                                                                              in1=gs[0:m, 8:9], op0=ALU.mult, op1=ALU.mult),
                          reads=(gsb,), writes=(gsb,))
                    ot, otb = ontm[c % 3], ontm_b[c % 3]
                    fw.op("act", lambda e, ot=ot, po=po, m=m, gs=gs: e.activation(out=ot[0:m, :], in_=po[0:m, :], func=AF.Identity,
                                                                                 bias=gs[0:m, 9:10], scale=gs[0:m, 8:9]),
                          reads=(pob, gsb), writes=(otb,))

                def TR(c):
                    ci, c0 = c // 4, c * 128
                    m = SCOL if c == NCH else 128
                    ot, otb = ontm[c % 3], ontm_b[c % 3]
                    pt_, ptb = psum()
                    pt16 = pt_[:].bitcast(BF16)

                    def fnT(e, ot=ot, m=m, pt16=pt16):
                        ins = None
                        for vc in range(4):
                            ins = e.transpose(pt16[:, vc * 128:vc * 128 + m], ot[0:m, vc * 128:(vc + 1) * 128], ident_t[0:m, 0:m])
                        return ins
                    fw.op("pe", fnT, reads=(otb, ident_b), writes=(ptb,), npe=4)
                    for vc in range(4):
                        fw.op("act", lambda e, vc=vc, m=m, c0=c0, pt16=pt16, h=h: e.activation(
                            out=onT[:, vc, c0:c0 + m], in_=pt16[:, vc * 128:vc * 128 + m], func=AF.Identity,
                            scale=PT("gn_g", h * 4 + vc, h * 4 + vc + 1)), reads=(ptb, ptab_b), writes=(onT_b[ci],))

                def O(c):
                    ci, c0 = c // 4, c * 128
                    sTt, sTb = sT[c % 3], sT_b[c % 3]
                    cur = c % NSB
                    po, pob = psum()
                    pairs = [(sTt[:, 0:128], vt[:, c, :])] + \
                            [(qsT[:, fc, c0:c0 + 128], sbf[cur][:, fc, :]) for fc in range(2)]
                    mm_group(po[:, :], pob, pairs, (sTb, vt_b[c], qsT_b[ci], sbf_b[cur]))
                    return po, pob

                fw.tag = "ret.chunk"
                SC(NCH, slot=3)
                sTt, sTb = sT[3], sT_b[3]
                po_s, pob_s = psum()
                po_idx = ps_b.index(pob_s)
                ps_pin.add(po_idx)

                def SAMP(b, sTt=sTt, sTb=sTb, po=po_s, pob=pob_s):
                    k = (h * SG + b) % NSS
                    oj = b % 2
                    sidx = g * SG + b
                    pairs = []
                    if b == 0:
                        pairs.append((sTt[0:32, 0:32], vt[0:32, NCH, :]))
                    pairs += [(qsm[:, fc, b, :], ssb[k][:, fc, :]) for fc in range(2)]

                    def fn(e, pairs=pairs, b=b, po=po):
                        ins = None
                        for i, (l, r) in enumerate(pairs):
                            ins = e.matmul(po[0:32, :], l, r, start=(b == 0 and i == 0),
                                           stop=(b == SG - 1 and i == len(pairs) - 1))
                        return ins
                    fw.op("pe", fn, reads=(sTb, vt_b[NCH], qsm_b, ssb_b[k]), writes=(pob,), npe=len(pairs))
                    for fc in range(2):
                        pS, pSb = psum()
                        mm_group(pS[:, :], pSb, [(kdb[0:32, b, fc * 128:(fc + 1) * 128], vt[0:32, NCH, :])],
                                 (kdb_b, vt_b[NCH]))
                        fw.op("dve", lambda e, pS=pS, fc=fc, k=k, h=h, oj=oj: e.scalar_tensor_tensor(
                            out=sout[oj][:, fc, :], in0=sst[k][:, fc, :], scalar=cds[h], in1=pS[:, :],
                            op0=ALU.mult, op1=ALU.add), reads=(pSb, sst_b[k]), writes=(lns_b[2 * oj], lns_b[2 * oj + 1]))
                    fw.dma(rets_d[sidx, h].rearrange("(f p) v -> p f v", p=128), sout[oj], lns_b[2 * oj], write=False,
                           is_output=True, extra=(lns_b[2 * oj + 1],))
                    pend.append((h, b))
                    if len(pend) > 0:
                        ph, pb = pend.pop(0)
                        nb_, nh_ = pb + NSS, ph
                        if nb_ >= SG:
                            nb_, nh_ = nb_ - SG, ph + 1
                        if nh_ < HEADS:
                            s_in(nh_, nb_)

                SU(0)
                SU(1)
                SC(0)
                for c in range(NCH):
                    if c + 2 < NCH:
                        SU(c + 2)
                    if c + 1 < NCH:
                        SC(c + 1)
                    po, pob = O(c)
                    GN(c, po, pob, mid=(lambda c=c: TR(c - 1)) if c >= 1 else None)
                    SAMP(c)
                TR(NCH - 1)
                c = NCH
                po, pob = po_s, pob_s
                ps_pin.discard(po_idx)
                GN(c, po, pob)
                TR(c)
                if g == NG - 1:
                    fw.dma(retp_d[h].rearrange("(f p) v -> p f v", p=128), s32_t[:, h, :, :], s32_b[h], write=False,
                           is_output=True)
                for ci, (c0, n) in enumerate(CS):
                    fw.op("dve", lambda e, c0=c0, n=n: e.tensor_tensor(
                        out=onT[:, :, c0:c0 + n], in0=onT[:, :, c0:c0 + n], in1=sg[:, :, c0:c0 + n], op=ALU.mult),
                        reads=(onT_b[ci], sg_b[ci]), writes=(onT_b[ci],))
                fw.tag = "ret.out"
                for oc in range(8):
                    w, wbuf = wq.get()
                    w3 = w.rearrange("p (k c) -> p k c", k=4)
                    for ci, (c0, n) in enumerate(CS):
                        po, pob = psum()
                        mm_group(po[:, 0:n], pob, [(w3[:, k, :], onT[:, k, c0:c0 + n]) for k in range(4)], (wbuf, onT_b[ci]))
                        xs = xa_t[:, oc, c0:c0 + n]
                        fw.op("dve", lambda e, xs=xs, po=po, n=n: e.tensor_tensor(out=xs, in0=po[:, 0:n], in1=xs, op=ALU.add),
                              reads=(pob, xa_b[ci]), writes=(xa_b[ci],))

        def rglru(g):
            cast_pat[:] = cast_pats["rec"]
            arena.reset()
            fw.arena_reset()
            gh = arena.alloc([RC, T], BF16)
            gh_b = [fw.abuf(f"gh{ci}") for ci in range(len(CS))]
            xc = arena.alloc([5, T], F32)
            xcb = arena.alloc([5, T], BF16)
            XPW = 3 + PG + SG * 7
            xp = [arena.alloc([XPW], F32), lns_t[:, :, :].rearrange("p a b -> p (a b)")[:, 0:XPW]]
            tmp = [[arena.alloc([T], F32) for _ in range(2)] for _ in range(2)]
            a2buf = [arena.alloc([T], F32) for _ in range(2)]
            cvs = arena.alloc([RC, SG, 3], F32)
            lrs = arena.alloc([RC, SG], F32)
            xc_b = [fw.abuf(f"xc{i}") for i in range(5)]
            xcb_b = [fw.abuf(f"xcb{i}") for i in range(5)]
            xp_b = [(fw.abuf("xp0"),), (lns_b[0], lns_b[1], lns_b[2])]
            tmp_b = [[fw.abuf(f"tmp{i}{j}") for j in range(2)] for i in range(2)]
            a2buf_b = [fw.abuf(f"a2buf{i}") for i in range(2)]
            cvs_b, lrs_b = fw.abuf("cvs"), fw.abuf("lrs")
            fw.dma(lrs, slru_d[:, :, g * SG:(g + 1) * SG], lrs_b, write=True)
            items = []
            for hf in range(2):
                items += [(wd["recin"][hf * 5 + fc], KC * 128) for fc in range(5)]
                items += [(wd["recin"][10 + hf * 5 + fc], KC * 128) for fc in range(5)]
                for fo in range(5):
                    items += [(wd["reca"][hf, fo], 5 * 128), (wd["reci"][hf, fo], 5 * 128)]
            items += [(wd["recout"][oc], RC * 128) for oc in range(8)]
            wq = WQ(items)
            tctr = 0
            for hf in range(2):
                fw.tag = "rec.gate"
                for fc in range(5):
                    F = hf * 5 + fc
                    w, wbuf = wq.get()
                    w3 = w.rearrange("p (k c) -> p k c", k=KC)
                    for ci, (c0, n) in enumerate(CS):
                        pg, pgb = psum()
                        mm_group(pg[:, 0:n], pgb, [(w3[:, kc, :], xb_t[:, kc, c0:c0 + n]) for kc in range(KC)], (wbuf, xb_b[ci]))
                        fw.op("act", lambda e, pg=pg, n=n, F=F, c0=c0: e.activation(
                            out=gh[:, F, c0:c0 + n], in_=pg[:, 0:n], func=AF.Gelu_apprx_tanh), reads=(pgb,), writes=(gh_b[ci],))
                flush_deferred()
                fw.tag = "rec.conv"
                for fc in range(5):
                    F = hf * 5 + fc
                    w, wbuf = wq.get()
                    w3 = w.rearrange("p (k c) -> p k c", k=KC)
                    xpt, xpbs = xp[F % 2], xp_b[F % 2]
                    xps = xpt[:, 3 + PG:3 + PG + SG * 7].rearrange("p (b t) -> p b t", b=SG)
                    fw.op("dve", lambda e, xpt=xpt, F=F: e.tensor_copy(out=xpt[:, 0:3], in_=cc_t[:, F, :]),
                          reads=(cc_b,), writes=xpbs)
                    fw.dma(xps[:, :, 0:3], sconv_d[:, F, g * SG:(g + 1) * SG, :], xpbs[0], write=True, extra=xpbs[1:])
                    for ci, (c0, n) in enumerate(CS):
                        pg, pgb = psum()
                        mm_group(pg[:, 0:n], pgb, [(w3[:, kc, :], xb_t[:, kc, c0:c0 + n]) for kc in range(KC)], (wbuf, xb_b[ci]))
                        if ci < 2:
                            fw.op("act", lambda e, pg=pg, n=n, c0=c0, xpt=xpt: e.activation(
                                out=xpt[:, 3 + c0:3 + c0 + n], in_=pg[:, 0:n], func=AF.Copy), reads=(pgb,), writes=xpbs)
                        else:
                            fw.op("act", lambda e, pg=pg, xps=xps: e.activation(
                                out=xps[:, :, 3:7], in_=pg[:, 0:SCOL].rearrange("p (b t) -> p b t", b=SG), func=AF.Copy),
                                reads=(pgb,), writes=xpbs)
                    cw = lambda j, F=F: PT("conv_w", j * 10 + F, j * 10 + F + 1)
                    cbias = PT("conv_b", F, F + 1)
                    for (dst, src_of) in ((xc[:, fc, 0:PG], lambda j, xpt=xpt: xpt[:, j:j + PG]),
                                          (xc[:, fc, PG:T].rearrange("p (b t) -> p b t", b=SG), lambda j, xps=xps: xps[:, :, j:j + 4])):
                        fw.op("act", lambda e, dst=dst, src_of=src_of, cw=cw, cbias=cbias: e.activation(
                            out=dst, in_=src_of(3), func=AF.Identity, bias=cbias, scale=cw(3)),
                            reads=xpbs + (ptab_b,), writes=(xc_b[fc],))
                        for j in (2, 1, 0):
                            fw.op("dve", lambda e, dst=dst, src_of=src_of, cw=cw, j=j: e.scalar_tensor_tensor(
                                out=dst, in0=src_of(j), scalar=cw(j), in1=dst, op0=ALU.mult, op1=ALU.add),
                                reads=xpbs + (ptab_b, xc_b[fc]), writes=(xc_b[fc],))
                    fw.op("act", lambda e, fc=fc: e.activation(out=xcb[:, fc, :], in_=xc[:, fc, :], func=AF.Copy),
                          reads=(xc_b[fc],), writes=(xcb_b[fc],))
                    fw.op("dve", lambda e, xpt=xpt, F=F: e.tensor_copy(out=cc_t[:, F, :], in_=xpt[:, PG:PG + 3]),
                          reads=xpbs, writes=(cc_b,))
                    fw.op("dve", lambda e, xps=xps, F=F: e.tensor_copy(out=cvs[:, F, :, :], in_=xps[:, :, 4:7]),
                          reads=xpbs, writes=(cvs_b,))
                fw.tag = "rec.lru"
                for fo in range(5):
                    F = hf * 5 + fo
                    wa, wab = wq.get()
                    wi_, wib = wq.get()
                    wa3 = wa.rearrange("p (k c) -> p k c", k=5)
                    wi3 = wi_.rearrange("p (k c) -> p k c", k=5)
                    tm, tmb = tmp[tctr % 2], tmp_b[tctr % 2]
                    tctr += 1
                    r_t, i_t = tm
                    a2_t = a2buf[tctr % 2]
                    hs_t = a2_t
                    tmb = list(tmb) + [a2buf_b[tctr % 2], a2buf_b[tctr % 2]]
                    kcs = GATE_KCS[fo]
                    for ci, (c0, n) in enumerate(CS):
                        pr, prb = psum()
                        pi_, pib = psum()
                        mm_group(pr[:, 0:n], prb, [(wa3[:, kc, :], xcb[:, kc, c0:c0 + n]) for kc in kcs],
                                 (wab,) + tuple(xcb_b[kc] for kc in kcs))
                        mm_group(pi_[:, 0:n], pib, [(wi3[:, kc, :], xcb[:, kc, c0:c0 + n]) for kc in kcs],
                                 (wib,) + tuple(xcb_b[kc] for kc in kcs))
                        fw.op("act", lambda e, pr=pr, n=n, c0=c0, r_t=r_t, F=F: e.activation(
                            out=r_t[:, c0:c0 + n], in_=pr[:, 0:n], func=AF.Tanh, bias=pder_t[:, 116 + F:117 + F], scale=0.5),
                            reads=(prb, pder_b), writes=(tmb[0],))
                        fw.op("act", lambda e, pi_=pi_, n=n, c0=c0, i_t=i_t, F=F: e.activation(
                            out=i_t[:, c0:c0 + n], in_=pi_[:, 0:n], func=AF.Tanh, bias=pder_t[:, 126 + F:127 + F], scale=0.5),
                            reads=(pib, pder_b), writes=(tmb[1],))
                    cF = pder_t[:, 96 + F:97 + F]
                    c2F = pder_t[:, 106 + F:107 + F]
                    chF = pder_t[:, 136 + F:137 + F]
                    fw.op("act", lambda e, r_t=r_t, a2_t=a2_t, cF=cF: e.activation(out=a2_t[:, :], in_=r_t[:, :], func=AF.Exp, scale=cF, bias=cF),
                          reads=(tmb[0], pder_b), writes=(tmb[2],))
                    fw.op("act", lambda e, r_t=r_t, chF=chF: e.activation(out=r_t[:, :], in_=r_t[:, :], func=AF.Exp, scale=chF, bias=chF),
                          reads=(tmb[0], pder_b), writes=(tmb[0],))
                    fw.op("act", lambda e, a2_t=a2_t: e.activation(out=a2_t[:, :], in_=a2_t[:, :], func=AF.Sqrt,
                                                                   bias=CT("one"), scale=-1.0),
                          reads=(tmb[2], ctab_b), writes=(tmb[2],))
                    fw.op("dve", lambda e, i_t=i_t, fo=fo: e.scalar_tensor_tensor(out=i_t[:, :], in0=i_t[:, :], scalar=1.0, in1=xc[:, fo, :],
                                                                                 op0=ALU.add, op1=ALU.mult),
                          reads=(tmb[1], xc_b[fo]), writes=(tmb[1],))
                    fw.op("dve", lambda e, i_t=i_t, a2_t=a2_t: e.scalar_tensor_tensor(out=i_t[:, :], in0=i_t[:, :], scalar=0.5, in1=a2_t[:, :],
                                                                                     op0=ALU.mult, op1=ALU.mult),
                          reads=(tmb[1], tmb[2]), writes=(tmb[1],))
                    av = r_t[:, PG:T].rearrange("p (b t) -> p b t", b=SG)
                    uv = i_t[:, PG:T].rearrange("p (b t) -> p b t", b=SG)
                    hv = hs_t[:, PG:T].rearrange("p (b t) -> p b t", b=SG)
                    fw.op("dve", lambda e, av=av, hv=hv, F=F: e.tensor_tensor(out=hv[:, :, 0], in0=av[:, :, 0], in1=lrs[:, F, :], op=ALU.mult),
                          reads=(tmb[0], lrs_b), writes=(tmb[3],))
                    fw.op("dve", lambda e, uv=uv, hv=hv: e.tensor_tensor(out=uv[:, :, 0], in0=uv[:, :, 0], in1=hv[:, :, 0], op=ALU.add),
                          reads=(tmb[1], tmb[3]), writes=(tmb[1],))
                    fw.op("dve", lambda e, av=av: e.memset(av[:, :, 0], 0.0), writes=(tmb[0],))
                    fw.op("dve", lambda e, r_t=r_t, i_t=i_t, hs_t=hs_t, F=F: e.tensor_tensor_scan(
                        out=hs_t[:, 0:T], data0=r_t[:, 0:T], data1=i_t[:, 0:T], initial=hc_t[:, F:F + 1],
                        op0=ALU.mult, op1=ALU.add), reads=(tmb[0], tmb[1], hc_b), writes=(tmb[3],))
                    fw.op("dve", lambda e, hs_t=hs_t, F=F: e.tensor_copy(out=hc_t[:, F:F + 1], in_=hs_t[:, PG - 1:PG]),
                          reads=(tmb[3],), writes=(hc_b,))
                    fw.op("dve", lambda e, hv=hv, F=F: e.tensor_copy(out=lrs[:, F, :], in_=hv[:, :, DEC_T - 1]),
                          reads=(tmb[3],), writes=(lrs_b,))
                    for ci, (c0, n) in enumerate(CS):
                        fw.op("dve", lambda e, hs_t=hs_t, F=F, c0=c0, n=n: e.tensor_tensor(
                            out=gh[:, F, c0:c0 + n], in0=gh[:, F, c0:c0 + n], in1=hs_t[:, c0:c0 + n], op=ALU.mult),
                            reads=(tmb[3], gh_b[ci]), writes=(gh_b[ci],))
            fw.dma(convs_d[:, :, g * SG:(g + 1) * SG, :], cvs, cvs_b, write=False, is_output=True)
            fw.dma(lrus_d[:, :, g * SG:(g + 1) * SG], lrs, lrs_b, write=False, is_output=True)
            if g == NG - 1:
                fw.dma(convp_d[:, :, :], cc_t[:], cc_b, write=False, is_output=True)
                fw.dma(lrup_d[:, :], hc_t[:], hc_b, write=False, is_output=True)
            fw.tag = "rec.out"
            for oc in range(8):
                w, wbuf = wq.get()
                w3 = w.rearrange("p (k c) -> p k c", k=RC)
                for ci, (c0, n) in enumerate(CS):
                    po, pob = psum()
                    mm_group(po[:, 0:n], pob, [(w3[:, k, :], gh[:, k, c0:c0 + n]) for k in range(RC)], (wbuf, gh_b[ci]))
                    xs = xa_t[:, oc, c0:c0 + n]
                    fw.op("dve", lambda e, xs=xs, po=po, n=n: e.tensor_tensor(out=xs, in0=po[:, 0:n], in1=xs, op=ALU.add),
                          reads=(pob, xa_b[ci]), writes=(xa_b[ci],))

        stages = []
        for g in range(NG):
            stages.append(("load", g))
            for l in range(2):
                stages += [("ffn", "f1", l), ("ln", l * 3 + 0), ("mix", l, g), ("ln", l * 3 + 1), ("ffn", "f2", l), ("ln", l * 3 + 2)]
            stages.append(("store", g))
        nstage = 0
        for g in range(NG):
            load_x(g)
            done = False
            seq = [("ffn", "f1", 0), ("ln", 0), ("mix", 0), ("ln", 1), ("ffn", "f2", 0), ("ln", 2),
                   ("ffn", "f1", 1), ("ln", 3), ("mix", 1), ("ln", 4), ("ffn", "f2", 1), ("ln", 5)]
            for si, s in enumerate(seq):
                if stop is not None and si >= stop:
                    break
                if s[0] == "ffn":
                    ffn(s[1], s[2])
                elif s[0] == "ln":
                    layer_norm(s[1], final=(s[1] == 5))
                else:
                    if s[1] == 0:
                        retention(g)
                    else:
                        rglru(g)
            store_y(g)
        fw.finish()
        build_program.pe_tags = fw.pe_tags
        with nc.Block() as block:
            fw.replay(block)
    return nc


_CACHE = {}


def kernel(**inputs):
    stop = inputs.pop("_stop", None)
    ncores = inputs.pop("_cores", NCORES)
    ct, cd, cds = build_ctab()
    sh = prep_shared(inputs)
    in_maps = []
    for c in range(NCORES):
        d = dict(sh)
        d.update(prep_core(inputs, c))
        in_maps.append(d)
    key = ("nc", stop)
    nc = build_program(cd, cds, stop=stop)
    res = run_bass_kernel_spmd(nc, in_maps[:ncores], core_ids=list(range(ncores)))
    R = list(res.results) + [res.results[0]] * (NCORES - ncores)
    y_p = np.zeros((8, SEQ, D), np.float32)
    y_s = np.zeros((DEC_B, DEC_T, D), np.float32)
    ret_p = np.zeros((1, 8, HEADS, DK, DV), np.float32)
    conv_p = np.zeros((1, 8, 3, D_RNN), np.float32)
    lru_p = np.zeros((1, 8, D_RNN), np.float32)
    ret_s = np.zeros((1, DEC_B, HEADS, DK, DV), np.float32)
    conv_s = np.zeros((1, DEC_B, 3, D_RNN), np.float32)
    lru_s = np.zeros((1, DEC_B, D_RNN), np.float32)
    for c in range(NCORES):
        r = R[c]
        yT = np.asarray(r["yT"])
        yall = yT.transpose(2, 1, 0).reshape(SEQ + NSAMP * DEC_T, D)
        y_p[c] = yall[:SEQ]
        y_s[c * NSAMP:(c + 1) * NSAMP] = yall[SEQ:].reshape(NSAMP, DEC_T, D)
        ret_p[0, c] = np.asarray(r["ret_p"])
        ret_s[0, c * NSAMP:(c + 1) * NSAMP] = np.asarray(r["ret_s"])
        conv_p[0, c] = np.asarray(r["convT_p"]).transpose(2, 1, 0).reshape(3, D_RNN)
        conv_s[0, c * NSAMP:(c + 1) * NSAMP] = np.asarray(r["convT_s"]).transpose(2, 3, 1, 0).reshape(NSAMP, 3, D_RNN)
        lru_p[0, c] = np.asarray(r["lruT_p"]).T.reshape(D_RNN)
        lru_s[0, c * NSAMP:(c + 1) * NSAMP] = np.asarray(r["lruT_s"]).transpose(2, 1, 0).reshape(NSAMP, D_RNN)
    return (y_p, y_s, ret_p, conv_p, lru_p, ret_s, conv_s, lru_s)
```

```python
import math
from contextlib import ExitStack

import numpy as np
import concourse.bass as bass
import concourse.mybir as mybir
from concourse.bass_utils import run_bass_kernel_spmd

F32 = mybir.dt.float32
BF16 = mybir.dt.bfloat16
AF = mybir.ActivationFunctionType
ALU = mybir.AluOpType

NCORES = 8
D = 1024
KC = 8
SEQ = 2048
DEC_B = 128
DEC_T = 4
NSAMP = DEC_B // NCORES
NG = 2
PG = SEQ // NG
SG = NSAMP // NG
SCOL = SG * DEC_T
T = PG + SCOL
CS = [(0, 512), (512, 512), (1024, SCOL)]
NCH = PG // 128
HEADS = 4
DK = 256
DV = 512
D_RNN = 1280
RC = 10
D_FF = 2816
FJ = 22
PAST = 16384
ALPHA = (2.0 * 2) ** 0.25
LN_EPS = 1e-5
GN_EPS = 1e-6
LRU_C = 8.0
EPOCH_MAX = 8000

ENGS = ["sync", "act", "dve", "pool", "pe"]


class DSem:
    def __init__(self, handle):
        self.h = handle
        self.total = 0


class Buf:
    __slots__ = ("name", "w", "r", "dsem")

    def __init__(self, name):
        self.name = name
        self.w = None
        self.r = {}
        self.dsem = None


class FW:
    def __init__(self, nc, stack):
        self.nc = nc
        self.stack = stack
        self.prog = {e: [] for e in ENGS}
        self.cnt = {e: 0 for e in ENGS}
        self.epoch = {e: 0 for e in ENGS}
        self.seen = {e: {} for e in ENGS}
        self.esems = {}
        self.dsems = []
        self.out_dsems = []
        self.arena_bufs = []
        self.legacy = {}
        self.tag = "init"
        self.pe_tags = []

    def esem(self, eng, epoch):
        k = (eng, epoch)
        if k not in self.esems:
            self.esems[k] = self.stack.enter_context(self.nc.semaphore(f"e_{eng}_{epoch}"))
        return self.esems[k]

    def new_dsem(self, name):
        d = DSem(self.stack.enter_context(self.nc.semaphore(f"d_{name}_{len(self.dsems)}")))
        self.dsems.append(d)
        return d

    def _need(self, eng, tok):
        if tok is None:
            return
        if tok[0] == "e":
            _, te, tep, tc = tok
            if te == eng:
                if eng == "pe" or eng == "sync":
                    return
                if tep == self.epoch[eng] and tc + 2 <= self.cnt[eng]:
                    return
                if tep < self.epoch[eng] and self.cnt[eng] >= 2:
                    return
            key = (te, tep)
            if self.seen[eng].get(key, 0) >= tc:
                return
            self.seen[eng][key] = tc
            self.prog[eng].append(("w", self.esem(te, tep), tc))
        else:
            d = tok[1]
            key = ("d", id(d))
            if self.seen[eng].get(key, 0) >= d.total:
                return
            self.seen[eng][key] = d.total
            self.prog[eng].append(("w", d.h, d.total))

    def _deps(self, eng, reads, writes):
        for b in reads:
            self._need(eng, b.w)
        for b in writes:
            self._need(eng, b.w)
            for t in list(b.r.values()):
                self._need(eng, t)

    def op(self, eng, fn, reads=(), writes=(), npe=1):
        if eng == "pe":
            self.pe_tags.append((self.tag, npe))
        self._deps(eng, reads, writes)
        ep = self.epoch[eng]
        self.cnt[eng] += 1
        tok = ("e", eng, ep, self.cnt[eng])
        self.prog[eng].append(("o", fn, self.esem(eng, ep), 1))
        if self.cnt[eng] >= EPOCH_MAX:
            self.epoch[eng] += 1
            self.cnt[eng] = 0
        for b in reads:
            b.r[eng] = tok
        for b in writes:
            b.w = tok
            b.r = {}
        return tok

    def dma(self, out, in_, buf, write, is_output=False, extra=()):
        eng = "sync"
        if write:
            self._deps(eng, (), (buf,) + tuple(extra))
        else:
            self._deps(eng, (buf,) + tuple(extra), ())
        if buf.dsem is None:
            buf.dsem = self.new_dsem(buf.name)
        d = buf.dsem
        d.total += 16
        self.prog[eng].append(("o", lambda e, o=out, i=in_: e.dma_start(out=o, in_=i), d.h, 16))
        tok = ("d", d)
        for bb in (buf,) + tuple(extra):
            if write:
                bb.w = tok
                bb.r = {}
            else:
                bb.r["dma" + str(id(d))] = tok
        if is_output and d not in self.out_dsems:
            self.out_dsems.append(d)
        return tok

    def arena_reset(self):
        leg = dict(self.legacy)
        for b in self.arena_bufs:
            toks = list(b.r.values())
            if b.w is not None:
                toks.append(b.w)
            for t in toks:
                if t[0] == "e":
                    k = ("e", t[1])
                    o = leg.get(k)
                    if o is None or (o[2], o[3]) < (t[2], t[3]):
                        leg[k] = t
                else:
                    leg[("d", id(t[1]))] = t
        self.legacy = leg
        self.arena_bufs = []

    def abuf(self, name):
        b = Buf(name)
        b.r = dict(self.legacy)
        self.arena_bufs.append(b)
        return b

    def finish(self):
        for d in self.out_dsems:
            self._need("sync", ("d", d))

    def replay(self, block):
        prog = self.prog

        def run(name, e):
            for ent in prog[name]:
                if ent[0] == "w":
                    e.wait_ge(ent[1], ent[2])
                else:
                    ins = ent[1](e)
                    ins.then_inc(ent[2], ent[3])

        @block.sync
        def _(e):
            run("sync", e)

        @block.scalar
        def _(e):
            run("act", e)

        @block.vector
        def _(e):
            run("dve", e)

        @block.gpsimd
        def _(e):
            run("pool", e)

        @block.tensor
        def _(e):
            run("pe", e)


class Arena:
    def __init__(self, ap_f32, nbytes):
        self.ap = ap_f32
        self.nbytes = nbytes
        self.off = 0

    def reset(self):
        self.off = 0

    def alloc(self, shape_free, dtype):
        esz = 4 if dtype == F32 else 2
        n = int(np.prod(shape_free))
        nb = (n * esz + 31) // 32 * 32
        assert self.off + nb <= self.nbytes, f"arena overflow {self.off}+{nb}>{self.nbytes}"
        a = self.ap[:, self.off // 4:(self.off + nb) // 4]
        self.off += nb
        if dtype != F32:
            a = a.bitcast(dtype)
        a = a[:, 0:n]
        if len(shape_free) == 2:
            a = a.rearrange("p (a b) -> p a b", a=shape_free[0])
        elif len(shape_free) == 3:
            a = a.rearrange("p (a b c) -> p a b c", a=shape_free[0], b=shape_free[1])
        return a


def _ctab_layout():
    lay = {}
    off = 0
    for name, w in [("maskT", 4 * 128), ("qdec", 4 * 128), ("maskTs", 4 * 32), ("qdecs", 4 * 32),
                    ("colmask", SG * 32), ("rowmask", SG), ("kdec", 4), ("kdecs", 4),
                    ("lneps", 1), ("gneps", 1), ("ident", 128), ("one", 1)]:
        lay[name] = (off, w)
        off += w
    return lay, off


CT_LAY, CT_W = _ctab_layout()


def _ptab_layout():
    lay = {}
    off = 0
    for name, w in [("ln_g", 48), ("ln_b", 48), ("gn_g", 16), ("conv_w", 40), ("conv_b", 10),
                    ("b_a", 10), ("b_i", 10), ("lam", 10)]:
        lay[name] = (off, w)
        off += w
    return lay, off


PT_LAY, PT_W = _ptab_layout()


def build_ctab():
    c = np.zeros((128, CT_W), np.float64)
    lg = np.log1p(-np.exp2(-5.0 - np.arange(HEADS)))
    idx = np.arange(128)
    o, _ = CT_LAY["maskT"]
    for h in range(HEADS):
        rel = idx[None, :] - idx[:, None]
        m = np.where(rel >= 0, np.exp(lg[h] * np.maximum(rel, 0)), 0.0) / 16.0
        c[:, o + h * 128:o + (h + 1) * 128] = m
    o, _ = CT_LAY["qdec"]
    for h in range(HEADS):
        c[:, o + h * 128:o + (h + 1) * 128] = np.exp(lg[h] * (idx + 1.0))[None, :]
    m32 = np.arange(32)
    bb, tt = m32 // 4, m32 % 4
    o, _ = CT_LAY["maskTs"]
    for h in range(HEADS):
        rel = tt[None, :] - tt[:, None]
        m = np.where((rel >= 0) & (bb[None, :] == bb[:, None]), np.exp(lg[h] * np.maximum(rel, 0)), 0.0) / 16.0
        c[0:32, o + h * 32:o + (h + 1) * 32] = m
    o, _ = CT_LAY["qdecs"]
    for h in range(HEADS):
        c[:, o + h * 32:o + (h + 1) * 32] = np.exp(lg[h] * (tt + 1.0))[None, :]
    o, _ = CT_LAY["colmask"]
    for b in range(SG):
        c[:, o + b * 32:o + (b + 1) * 32] = (bb == b).astype(np.float64)[None, :]
    o, _ = CT_LAY["rowmask"]
    for b in range(SG):
        c[0:32, o + b] = (bb == b)
    o, _ = CT_LAY["kdec"]
    for h in range(HEADS):
        c[:, o + h] = np.exp(lg[h] * (127.0 - idx)) / 16.0
    o, _ = CT_LAY["kdecs"]
    for h in range(HEADS):
        c[0:32, o + h] = np.exp(lg[h] * (3.0 - tt)) / 16.0
    c[:, CT_LAY["lneps"][0]] = LN_EPS
    c[:, CT_LAY["gneps"][0]] = GN_EPS
    o, _ = CT_LAY["ident"]
    c[:, o:o + 128] = np.eye(128)
    c[:, CT_LAY["one"][0]] = 1.0
    cd = [float(np.exp(lg[h] * 128.0)) for h in range(HEADS)]
    cds = [float(np.exp(lg[h] * 4.0)) for h in range(HEADS)]
    return c.astype(np.float32), cd, cds


def build_rot():
    half = DK // 2
    inv = (10000.0 ** (-np.arange(half, dtype=np.float32) / np.float32(half))).astype(np.float32)
    rot = np.zeros((NG, 128, 2, T), np.float32)
    for g in range(NG):
        pos = np.concatenate([np.arange(g * PG, (g + 1) * PG), np.tile(PAST + np.arange(DEC_T), SG)]).astype(np.float32)
        ang = (pos[None, :] * inv[:, None]).astype(np.float32)
        rot[g, :, 0, :] = np.cos(ang)
        rot[g, :, 1, :] = np.sin(ang)
    return rot


def rec_gate_kcs():
    res = []
    for fo in range(5):
        lo, hi = fo * 128, fo * 128 + 127
        b0, b1 = lo // 160, hi // 160
        ilo, ihi = b0 * 160, b1 * 160 + 159
        res.append(list(range(ilo // 128, ihi // 128 + 1)))
    return res


GATE_KCS = rec_gate_kcs()


def fm(v, nchunk):
    return np.ascontiguousarray(v.reshape(nchunk, 128).T)


def prep_shared(inp):
    f = lambda a: np.ascontiguousarray(np.asarray(a, dtype=np.float32))
    sh = {}
    for l in range(2):
        for nm, wi, wo in (("f1", "ffn1_w_in", "ffn1_w_out"), ("f2", "ffn2_w_in", "ffn2_w_out")):
            W = f(inp[wi][l])
            Wr = W.reshape(KC, 128, 2, FJ, 128).transpose(3, 1, 0, 2, 4)
            sh[f"{nm}in{l}"] = np.ascontiguousarray(Wr).reshape(FJ, 128, KC * 256)
            Wo = f(inp[wo][l])
            Wor = Wo.reshape(2, 11, 128, 8, 128).transpose(3, 0, 2, 1, 4)
            sh[f"{nm}out{l}"] = np.ascontiguousarray(Wor).reshape(8, 2, 128, 11 * 128)
    W = f(inp["ret_w_in"][0])
    blocks = []
    for h in range(HEADS):
        cols = [W[:, h * 256:(h + 1) * 256], W[:, 1024 + h * 256:1024 + (h + 1) * 256],
                W[:, 2048 + h * 512:2048 + h * 512 + 256], W[:, 2048 + h * 512 + 256:2048 + (h + 1) * 512],
                W[:, 4096 + h * 512:4096 + h * 512 + 256], W[:, 4096 + h * 512 + 256:4096 + (h + 1) * 512]]
        for cblk in cols:
            blocks.append(cblk.reshape(KC, 128, 256).transpose(1, 0, 2).reshape(128, KC * 256))
    sh["retin"] = np.ascontiguousarray(np.stack(blocks)).reshape(HEADS, 6, 128, KC * 256)
    Wo = f(inp["ret_w_out"][0])
    sh["retout"] = np.ascontiguousarray(Wo.reshape(HEADS, 4, 128, 8, 128).transpose(0, 3, 2, 1, 4)).reshape(HEADS, 8, 128, 4 * 128)
    W = f(inp["rec_w_in"][0])
    sh["recin"] = np.ascontiguousarray(W.reshape(KC, 128, 20, 128).transpose(2, 1, 0, 3)).reshape(20, 128, KC * 128)
    for nm, key in (("reca", "rec_w_a"), ("reci", "rec_w_i")):
        Wb = f(inp[key][0])
        dense = np.zeros((2, 640, 640), np.float32)
        for n in range(8):
            hf, nl = n // 4, n % 4
            dense[hf, nl * 160:(nl + 1) * 160, nl * 160:(nl + 1) * 160] = Wb[n]
        sh[nm] = np.ascontiguousarray(dense.reshape(2, 5, 128, 5, 128).transpose(0, 3, 2, 1, 4)).reshape(2, 5, 128, 5 * 128)
    Wo = f(inp["rec_w_out"][0])
    sh["recout"] = np.ascontiguousarray(Wo.reshape(RC, 128, 8, 128).transpose(2, 1, 0, 3)).reshape(8, 128, RC * 128)
    pt = np.zeros((128, PT_W), np.float32)
    o = PT_LAY["ln_g"][0]
    for l in range(2):
        for i in range(3):
            pt[:, o + (l * 3 + i) * 8:o + (l * 3 + i + 1) * 8] = fm(f(inp["ln_g"][l, i]), 8)
    o = PT_LAY["ln_b"][0]
    for l in range(2):
        for i in range(3):
            pt[:, o + (l * 3 + i) * 8:o + (l * 3 + i + 1) * 8] = fm(f(inp["ln_b"][l, i]), 8)
    o = PT_LAY["gn_g"][0]
    pt[:, o:o + 16] = fm(f(inp["ret_gn_g"][0]), 16)
    o = PT_LAY["conv_w"][0]
    cw = f(inp["rec_conv_w"][0])
    for j in range(4):
        pt[:, o + j * 10:o + (j + 1) * 10] = fm(cw[j], RC)
    pt[:, PT_LAY["conv_b"][0]:PT_LAY["conv_b"][0] + 10] = fm(f(inp["rec_conv_b"][0]), RC)
    pt[:, PT_LAY["b_a"][0]:PT_LAY["b_a"][0] + 10] = fm(f(inp["rec_b_a"][0]), RC)
    pt[:, PT_LAY["b_i"][0]:PT_LAY["b_i"][0] + 10] = fm(f(inp["rec_b_i"][0]), RC)
    pt[:, PT_LAY["lam"][0]:PT_LAY["lam"][0] + 10] = fm(f(inp["rec_lam"][0]), RC)
    sh["ptab"] = pt
    ct, cd, cds = build_ctab()
    sh["ctab"] = ct
    sh["rot"] = build_rot()
    return sh


def prep_core(inp, c):
    f = lambda a: np.asarray(a, dtype=np.float32)
    xp = f(inp["x_prompt"][c])
    xs = f(inp["x_sample"][c * NSAMP:(c + 1) * NSAMP]).reshape(NSAMP * DEC_T, D)
    xall = np.concatenate([xp, xs], axis=0)
    xT = np.ascontiguousarray(xall.reshape(SEQ + NSAMP * DEC_T, KC, 128).transpose(2, 1, 0))
    d = {"xT": xT}
    d["sret"] = np.ascontiguousarray(f(inp["state_ret"][0, c * NSAMP:(c + 1) * NSAMP]))
    sc = f(inp["state_conv"][0, c * NSAMP:(c + 1) * NSAMP])
    d["sconvT"] = np.ascontiguousarray(sc.reshape(NSAMP, 3, RC, 128).transpose(3, 2, 0, 1))
    sl = f(inp["state_lru"][0, c * NSAMP:(c + 1) * NSAMP])
    d["slruT"] = np.ascontiguousarray(sl.reshape(NSAMP, RC, 128).transpose(2, 1, 0))
    return d


def build_program(cd, cds, stop=None):
    nc = bass.Bass("TRN2", target_bir_lowering=False)
    NTOK = SEQ + NSAMP * DEC_T

    def din(name, shape):
        return nc.dram_tensor(name, list(shape), F32, kind="ExternalInput").ap()

    def dout(name, shape):
        return nc.dram_tensor(name, list(shape), F32, kind="ExternalOutput").ap()

    xT_d = din("xT", [128, KC, NTOK])
    sret_d = din("sret", [NSAMP, HEADS, DK, DV])
    sconv_d = din("sconvT", [128, RC, NSAMP, 3])
    slru_d = din("slruT", [128, RC, NSAMP])
    ptab_d = din("ptab", [128, PT_W])
    ctab_d = din("ctab", [128, CT_W])
    rot_d = din("rot", [NG, 128, 2, T])
    wd = {}
    for l in range(2):
        for nm in ("f1", "f2"):
            wd[f"{nm}in{l}"] = din(f"{nm}in{l}", [FJ, 128, KC * 256])
            wd[f"{nm}out{l}"] = din(f"{nm}out{l}", [8, 2, 128, 11 * 128])
    wd["retin"] = din("retin", [HEADS, 6, 128, KC * 256])
    wd["retout"] = din("retout", [HEADS, 8, 128, 4 * 128])
    wd["recin"] = din("recin", [20, 128, KC * 128])
    wd["reca"] = din("reca", [2, 5, 128, 5 * 128])
    wd["reci"] = din("reci", [2, 5, 128, 5 * 128])
    wd["recout"] = din("recout", [8, 128, RC * 128])

    yT_d = dout("yT", [128, KC, NTOK])
    retp_d = dout("ret_p", [HEADS, DK, DV])
    rets_d = dout("ret_s", [NSAMP, HEADS, DK, DV])
    convp_d = dout("convT_p", [128, RC, 3])
    convs_d = dout("convT_s", [128, RC, NSAMP, 3])
    lrup_d = dout("lruT_p", [128, RC])
    lrus_d = dout("lruT_s", [128, RC, NSAMP])

    st = ExitStack()
    with st:
        fw = FW(nc, st)

        def sb(name, shape, dt):
            return st.enter_context(nc.sbuf_tensor("s_" + name, list(shape), dt))

        xa_t = sb("xa", [128, KC, T], F32)
        xb_t = sb("xb", [128, KC, T], BF16)
        NSTG, NWB = 2, 4
        stg_t = [sb(f"stg{i}", [128, 2048], F32) for i in range(NSTG)]
        wb_t = [sb(f"wb{i}", [128, 2048], BF16) for i in range(NWB)]
        s32_t = sb("s32", [128, HEADS, 2, DV], F32)
        ctab_t = sb("ctab", [128, CT_W], F32)
        ptab_t = sb("ptab", [128, PT_W], F32)
        pder_t = sb("pder", [128, 48 + 48 + 10 + 10 + 30], F32)
        ident_t = sb("ident", [128, 128], BF16)
        ones_t = sb("ones", [128, 128], BF16)
        onesf_t = sb("onesf", [128, 128], F32)
        LNW = 512
        lnz_t = sb("lnz", [128, 4 * T], BF16)
        zsq_t = lnz_t[:, 0:KC * LNW].rearrange("p (k n) -> p k n", k=KC)
        lns_t = sb("lns", [128, 4, LNW], F32)
        lnx_t = sb("lnx", [128, 2, SCOL], F32)
        hc_t = sb("hcarry", [128, RC], F32)
        cc_t = sb("ccarry", [128, RC, 3], F32)
        ARENA_BYTES = 84 * 1024
        arena_t = sb("arena", [128, ARENA_BYTES // 4], F32)
        arena = Arena(arena_t[:], ARENA_BYTES)
        psum_t = [st.enter_context(nc.psum_tensor(f"ps{i}", [128, 512], F32)) for i in range(8)]

        xa_b = [Buf(f"xa{i}") for i in range(len(CS))]
        xb_b = [Buf(f"xb{i}") for i in range(len(CS))]
        stg_b = [Buf(f"stg{i}") for i in range(NSTG)]
        wb_b = [Buf(f"wb{i}") for i in range(NWB)]
        s32_b = [Buf(f"s32_{h}") for h in range(HEADS)]
        ctab_b = Buf("ctab")
        ptab_b = Buf("ptab")
        pder_b = Buf("pder")
        ident_b = Buf("ident")
        ones_b = Buf("ones")
        onesf_b = Buf("onesf")
        zb_b = Buf("lnz")
        zsq_b = zb_b
        lns_b = [Buf(f"lns{i}") for i in range(4)]
        lnx_b = [Buf(f"lnx{i}") for i in range(2)]
        hc_b = Buf("hc")
        cc_b = Buf("cc")
        ps_b = [Buf(f"ps{i}") for i in range(8)]
        ps_rr = [0]

        ps_pin = set()

        def psum():
            while True:
                i = ps_rr[0] % 8
                ps_rr[0] += 1
                if i not in ps_pin:
                    return psum_t[i], ps_b[i]

        def CT(name, lo=0, hi=None, rows=128):
            o, w = CT_LAY[name]
            hi = w if hi is None else hi
            return ctab_t[0:rows, o + lo:o + hi]

        def PT(name, lo=0, hi=None):
            o, w = PT_LAY[name]
            hi = w if hi is None else hi
            return ptab_t[:, o + lo:o + hi]

        wctr = [0]
        cast_pat = ["pool", "pool", "act"]
        cast_pats = {"ffn": ["pool", "pool", "act"], "ret": ["act", "act", "pool"], "rec": ["pool"]}

        def wload(dram_ap, nelem):
            i = wctr[0]
            wctr[0] += 1
            s, w = i % NSTG, i % NWB
            fw.dma(stg_t[s][:, 0:nelem], dram_ap, stg_b[s], write=True)
            ce = cast_pat[i % len(cast_pat)]
            if ce == "act":
                fw.op("act", lambda e, o=wb_t[w][:, 0:nelem], a=stg_t[s][:, 0:nelem]: e.copy(out=o, in_=a),
                      reads=(stg_b[s],), writes=(wb_b[w],))
            else:
                fw.op(ce, lambda e, o=wb_t[w][:, 0:nelem], a=stg_t[s][:, 0:nelem]: e.tensor_copy(out=o, in_=a),
                      reads=(stg_b[s],), writes=(wb_b[w],))
            return wb_t[w][:, 0:nelem], wb_b[w]

        class WQ:
            def __init__(self, items, depth=2):
                self.items = list(items)
                self.loaded = []
                self.depth = depth
                self.n = 0

            def get(self):
                while len(self.loaded) < self.depth + 1 and self.n < len(self.items):
                    ap_, ne = self.items[self.n]
                    self.loaded.append(wload(ap_, ne))
                    self.n += 1
                return self.loaded.pop(0)

        fw.dma(ctab_t[:], ctab_d[:, :], ctab_b, write=True)
        fw.dma(ptab_t[:], ptab_d[:, :], ptab_b, write=True)
        fw.op("dve", lambda e: e.tensor_copy(out=ident_t[:], in_=CT("ident")), reads=(ctab_b,), writes=(ident_b,))
        fw.op("dve", lambda e: e.memset(ones_t[:], 1.0 / D), writes=(ones_b,))
        fw.op("dve", lambda e: e.memset(onesf_t[:], 1.0 / D), writes=(onesf_b,))
        fw.op("dve", lambda e: e.tensor_scalar(out=pder_t[:, 0:96], in0=ptab_t[:, 0:96], scalar1=ALPHA, scalar2=None,
                                               op0=ALU.mult), reads=(ptab_b,), writes=(pder_b,))
        fw.op("act", lambda e: e.activation(out=pder_t[:, 96:106], in_=PT("lam"), func=AF.Exp, scale=-1.0),
              reads=(ptab_b,), writes=(pder_b,))
        fw.op("act", lambda e: e.activation(out=pder_t[:, 96:106], in_=pder_t[:, 96:106], func=AF.Ln,
                                            bias=CT("one"), scale=1.0), reads=(pder_b, ctab_b), writes=(pder_b,))
        fw.op("dve", lambda e: e.tensor_scalar(out=pder_t[:, 106:116], in0=pder_t[:, 96:106], scalar1=-2.0 * LRU_C,
                                               scalar2=None, op0=ALU.mult), reads=(pder_b,), writes=(pder_b,))
        fw.op("dve", lambda e: e.tensor_scalar(out=pder_t[:, 96:106], in0=pder_t[:, 96:106], scalar1=-LRU_C,
                                               scalar2=None, op0=ALU.mult), reads=(pder_b,), writes=(pder_b,))
        fw.op("dve", lambda e: e.tensor_scalar(out=pder_t[:, 116:126], in0=PT("b_a"), scalar1=0.5, scalar2=None, op0=ALU.mult),
              reads=(ptab_b,), writes=(pder_b,))
        fw.op("dve", lambda e: e.tensor_scalar(out=pder_t[:, 126:136], in0=PT("b_i"), scalar1=0.5, scalar2=None, op0=ALU.mult),
              reads=(ptab_b,), writes=(pder_b,))
        fw.op("dve", lambda e: e.tensor_scalar(out=pder_t[:, 136:146], in0=pder_t[:, 96:106], scalar1=0.5, scalar2=None, op0=ALU.mult),
              reads=(pder_b,), writes=(pder_b,))
        fw.op("dve", lambda e: e.memset(s32_t[:], 0.0), writes=tuple(s32_b))
        fw.op("dve", lambda e: e.memset(hc_t[:], 0.0), writes=(hc_b,))
        fw.op("dve", lambda e: e.memset(cc_t[:], 0.0), writes=(cc_b,))

        def AG(idx, kc):
            return pder_t[:, idx * 8 + kc:idx * 8 + kc + 1]

        def AB(idx, kc):
            return pder_t[:, 48 + idx * 8 + kc:48 + idx * 8 + kc + 1]

        def mm_group(out_ap, out_buf, pairs, reads, npe=None):
            def fn(e, pairs=pairs, out_ap=out_ap):
                ins = None
                n = len(pairs)
                for i, (l, r) in enumerate(pairs):
                    ins = e.matmul(out_ap, l, r, start=(i == 0), stop=(i == n - 1))
                return ins
            fw.op("pe", fn, reads=reads, writes=(out_buf,), npe=(npe or len(pairs)))

        def layer_norm(idx, final=False, g=0):
            fw.tag = f"ln{idx}"
            stats = []
            for ci, (c0, n) in enumerate(CS):
                z3 = xa_t[:, :, c0:c0 + n]
                pm, pmb = psum()
                mm_group(pm[:, 0:n], pmb, [(onesf_t[:], xa_t[:, kc, c0:c0 + n]) for kc in range(KC)], (onesf_b, xa_b[ci]), npe=2 * KC)
                fw.op("act", lambda e, z3=z3, n=n: e.activation(out=zsq_t[:, :, 0:n], in_=z3, func=AF.Square),
                      reads=(xa_b[ci],), writes=(zsq_b,))
                pe2, pe2b = psum()
                mm_group(pe2[:, 0:n], pe2b, [(ones_t[:], zsq_t[:, kc, 0:n]) for kc in range(KC)], (ones_b, zsq_b))
                if ci < 2:
                    vv, nmr = lns_t[:, 2 * ci, 0:n], lns_t[:, 2 * ci + 1, 0:n]
                    vb, nb_ = lns_b[2 * ci], lns_b[2 * ci + 1]
                else:
                    vv, nmr = lnx_t[:, 0, 0:n], lnx_t[:, 1, 0:n]
                    vb, nb_ = lnx_b[0], lnx_b[1]
                fw.op("act", lambda e, vv=vv, pm=pm, n=n: e.activation(out=vv, in_=pm[:, 0:n], func=AF.Square),
                      reads=(pmb,), writes=(vb,))
                fw.op("dve", lambda e, vv=vv, pe2=pe2, n=n: e.tensor_tensor(out=vv, in0=pe2[:, 0:n], in1=vv, op=ALU.subtract),
                      reads=(pe2b, vb), writes=(vb,))
                fw.op("act", lambda e, vv=vv: e.activation(out=vv, in_=vv, func=AF.Sqrt, bias=CT("lneps"), scale=1.0),
                      reads=(vb, ctab_b), writes=(vb,))
                fw.op("dve", lambda e, vv=vv: e.reciprocal(out=vv, in_=vv), reads=(vb,), writes=(vb,))
                fw.op("dve", lambda e, pm=pm, n=n, vv=vv, nmr=nmr: e.scalar_tensor_tensor(
                    out=nmr, in0=pm[:, 0:n], scalar=-1.0, in1=vv, op0=ALU.mult, op1=ALU.mult),
                    reads=(pmb, vb), writes=(nb_,))
                stats.append((vv, nmr, vb, nb_))
            for ci, (c0, n) in enumerate(CS):
                z3 = xa_t[:, :, c0:c0 + n]
                vv, nmr, vb, nb_ = stats[ci]
                fw.op("dve", lambda e, z3=z3, vv=vv, n=n: e.tensor_tensor(
                    out=z3, in0=z3, in1=vv.unsqueeze(1).broadcast_to([128, KC, n]), op=ALU.mult),
                    reads=(xa_b[ci], vb), writes=(xa_b[ci],))
                fw.op("dve", lambda e, z3=z3, nmr=nmr, n=n: e.tensor_tensor(
                    out=z3, in0=z3, in1=nmr.unsqueeze(1).broadcast_to([128, KC, n]), op=ALU.add),
                    reads=(xa_b[ci], nb_), writes=(xa_b[ci],))
                for kc in range(KC):
                    zc = xa_t[:, kc, c0:c0 + n]
                    xo = xb_t[:, kc, c0:c0 + n]
                    if not final:
                        fw.op("act", lambda e, zc=zc, xo=xo, kc=kc: e.activation(
                            out=xo, in_=zc, func=AF.Identity, bias=PT("ln_b", idx * 8 + kc, idx * 8 + kc + 1),
                            scale=PT("ln_g", idx * 8 + kc, idx * 8 + kc + 1)),
                            reads=(xa_b[ci], ptab_b), writes=(xb_b[ci],))
                    else:
                        fw.op("act", lambda e, zc=zc, kc=kc: e.activation(
                            out=zc, in_=zc, func=AF.Identity, bias=PT("ln_b", idx * 8 + kc, idx * 8 + kc + 1),
                            scale=PT("ln_g", idx * 8 + kc, idx * 8 + kc + 1)),
                            reads=(xa_b[ci], ptab_b), writes=(xa_b[ci],))
            if not final:
                for ci, (c0, n) in enumerate(CS):
                    for kc in range(KC):
                        zc = xa_t[:, kc, c0:c0 + n]
                        fw.op("dve", lambda e, zc=zc, kc=kc: e.tensor_scalar(
                            out=zc, in0=zc, scalar1=AG(idx, kc), scalar2=AB(idx, kc), op0=ALU.mult, op1=ALU.add),
                            reads=(xa_b[ci], pder_b), writes=(xa_b[ci],))

        def ffn(nm, l):
            cast_pat[:] = cast_pats["ffn"]
            arena.reset()
            fw.arena_reset()
            h_t = arena.alloc([FJ, T], BF16)
            s_t = [arena.alloc([512], BF16) for _ in range(2)]
            h_b = [fw.abuf(f"h{ci}") for ci in range(len(CS))]
            s_b = [fw.abuf(f"s{i}") for i in range(2)]
            win, wout = wd[f"{nm}in{l}"], wd[f"{nm}out{l}"]
            items = [(win[j], KC * 256) for j in range(FJ)] + \
                    [(wout[oc, kh], 11 * 128) for oc in range(8) for kh in range(2)]
            wq = WQ(items)
            si = 0
            fw.tag = f"{nm}{l}.in"
            for j in range(FJ):
                w, wbuf = wq.get()
                w3 = w.rearrange("p (k c) -> p k c", k=KC)
                for ci, (c0, n) in enumerate(CS):
                    pg, pgb = psum()
                    pu, pub = psum()
                    mm_group(pg[:, 0:n], pgb, [(w3[:, kc, 0:128], xb_t[:, kc, c0:c0 + n]) for kc in range(KC)],
                             (wbuf, xb_b[ci]))
                    mm_group(pu[:, 0:n], pub, [(w3[:, kc, 128:256], xb_t[:, kc, c0:c0 + n]) for kc in range(KC)],
                             (wbuf, xb_b[ci]))
                    sa, sab = s_t[si % 2], s_b[si % 2]
                    si += 1
                    fw.op("act", lambda e, sa=sa, pg=pg, n=n: e.activation(out=sa[:, 0:n], in_=pg[:, 0:n], func=AF.Silu),
                          reads=(pgb,), writes=(sab,))
                    fw.op("dve", lambda e, sa=sa, pu=pu, n=n, j=j, c0=c0: e.tensor_tensor(
                        out=h_t[:, j, c0:c0 + n], in0=pu[:, 0:n], in1=sa[:, 0:n], op=ALU.mult),
                        reads=(pub, sab), writes=(h_b[ci],))
            fw.tag = f"{nm}{l}.out"
            for oc in range(8):
                wA, wAb = wq.get()
                wB, wBb = wq.get()
                wA3 = wA.rearrange("p (k c) -> p k c", k=11)
                wB3 = wB.rearrange("p (k c) -> p k c", k=11)
                for ci, (c0, n) in enumerate(CS):
                    po, pob = psum()
                    pairs = [(wA3[:, k, :], h_t[:, k, c0:c0 + n]) for k in range(11)] + \
                            [(wB3[:, k, :], h_t[:, 11 + k, c0:c0 + n]) for k in range(11)]
                    mm_group(po[:, 0:n], pob, pairs, (wAb, wBb, h_b[ci]))
                    xs = xa_t[:, oc, c0:c0 + n]
                    fw.op("dve", lambda e, xs=xs, po=po, n=n: e.scalar_tensor_tensor(
                        out=xs, in0=po[:, 0:n], scalar=0.5, in1=xs, op0=ALU.mult, op1=ALU.add),
                        reads=(pob, xa_b[ci]), writes=(xa_b[ci],))

        def load_x(g):
            for ci, (c0, n) in enumerate(CS):
                src0 = g * PG + c0 if ci < 2 else SEQ + g * SCOL
                fw.dma(xa_t[:, :, c0:c0 + n], xT_d[:, :, src0:src0 + n], xa_b[ci], write=True)
                fw.op("pool", lambda e, c0=c0, n=n: e.tensor_copy(out=xb_t[:, :, c0:c0 + n], in_=xa_t[:, :, c0:c0 + n]),
                      reads=(xa_b[ci],), writes=(xb_b[ci],))
                fw.op("act", lambda e, c0=c0, n=n: e.mul(out=xa_t[:, :, c0:c0 + n], in_=xa_t[:, :, c0:c0 + n], mul=ALPHA),
                      reads=(xa_b[ci],), writes=(xa_b[ci],))

        def store_y(g):
            for ci, (c0, n) in enumerate(CS):
                dst0 = g * PG + c0 if ci < 2 else SEQ + g * SCOL
                fw.dma(yT_d[:, :, dst0:dst0 + n], xa_t[:, :, c0:c0 + n], xa_b[ci], write=False, is_output=True)

        def retention(g):
            cast_pat[:] = cast_pats["ret"]
            arena.reset()
            fw.arena_reset()
            onT = arena.alloc([4, T], BF16)
            onT_b = [fw.abuf(f"onT{ci}") for ci in range(len(CS))]
            rot_t = lnz_t[:].bitcast(F32)[:, 0:2 * T].rearrange("p (a b) -> p a b", a=2)
            rot_b = zb_b
            fw.dma(rot_t, rot_d[g], rot_b, write=True)
            qT = arena.alloc([2, T], BF16)
            qsT = arena.alloc([2, T], BF16)
            kT = arena.alloc([2, T], BF16)
            kd = arena.alloc([NCH + 1, 256], BF16)
            vt = arena.alloc([NCH + 1, 512], BF16)
            sg = arena.alloc([4, T], BF16)
            rt = [lns_t[:, i, :] for i in range(4)]
            NSS = 3
            sst = [arena.alloc([2, 512], F32) for _ in range(NSS)]
            ssb = [arena.alloc([2, 512], BF16) for _ in range(NSS)]
            NSB = 4
            sbf = [arena.alloc([2, 512], BF16) for _ in range(NSB)]
            sout = [lns_t[:, 2 * j:2 * j + 2, :] for j in range(2)]
            ontm = [arena.alloc([512], BF16) for _ in range(3)]
            sT = [arena.alloc([128], BF16) for _ in range(4)]
            qsm = arena.alloc([2, SG, 32], BF16)
            kdb = arena.alloc([SG, 256], BF16)
            gst = [arena.alloc([16], F32) for _ in range(3)]
            nb = lambda nm, k=1: [fw.abuf(f"{nm}{i}") for i in range(k)]
            qT_b, qsT_b, kT_b = nb("qT", 3), nb("qsT", 3), nb("kT", 3)
            kd_b, vt_b = nb("kd", NCH + 1), nb("vt", NCH + 1)
            sg_b = nb("sg", 3)
            rt_b = lns_b
            sst_b, ssb_b, sbf_b, ontm_b, sT_b = nb("sst", NSS), nb("ssb", NSS), nb("sbf", NSB), nb("ontm", 3), nb("sT", 4)
            qsm_b, kdb_b, gst_b = fw.abuf("qsm"), fw.abuf("kdb"), nb("gst", 3)

            win = wd["retin"]
            items = []
            for h in range(HEADS):
                items += [(win[h, k], KC * 256) for k in range(6)]
                items += [(wd["retout"][h, oc], 4 * 128) for oc in range(8)]
            wq = WQ(items)
            sctr = [0]
            def s_in(h, b):
                k = (h * SG + b) % NSS
                fw.dma(sst[k], sret_d[g * SG + b, h].rearrange("(f p) v -> p f v", p=128), sst_b[k], write=True)
                fw.op("pool", lambda e, k=k: e.tensor_copy(out=ssb[k][:, :, :], in_=sst[k][:, :, :]),
                      reads=(sst_b[k],), writes=(ssb_b[k],))

            for b in range(NSS):
                s_in(0, b)
            pend = []
            for h in range(HEADS):
                fw.tag = "ret.qk"
                for which, dstT, dst_b in (("q", qT, qT_b), ("k", kT, kT_b)):
                    w, wbuf = wq.get()
                    w3 = w.rearrange("p (k c) -> p k c", k=KC)
                    for ci, (c0, n) in enumerate(CS):
                        p1, p1b = psum()
                        p2, p2b = psum()
                        mm_group(p1[:, 0:n], p1b, [(w3[:, kc, 0:128], xb_t[:, kc, c0:c0 + n]) for kc in range(KC)],
                                 (wbuf, xb_b[ci]))
                        mm_group(p2[:, 0:n], p2b, [(w3[:, kc, 128:256], xb_t[:, kc, c0:c0 + n]) for kc in range(KC)],
                                 (wbuf, xb_b[ci]))
                        cosv, sinv = rot_t[:, 0, c0:c0 + n], rot_t[:, 1, c0:c0 + n]
                        t1, t2, t3, t4 = (rt[i][:, 0:n] for i in range(4))
                        fw.op("dve", lambda e, t1=t1, p1=p1, cosv=cosv, n=n: e.tensor_tensor(out=t1, in0=p1[:, 0:n], in1=cosv, op=ALU.mult),
                              reads=(p1b, rot_b), writes=(rt_b[0],))
                        fw.op("dve", lambda e, t2=t2, p2=p2, sinv=sinv, n=n: e.tensor_tensor(out=t2, in0=p2[:, 0:n], in1=sinv, op=ALU.mult),
                              reads=(p2b, rot_b), writes=(rt_b[1],))
                        fw.op("dve", lambda e, t3=t3, p1=p1, sinv=sinv, n=n: e.tensor_tensor(out=t3, in0=p1[:, 0:n], in1=sinv, op=ALU.mult),
                              reads=(p1b, rot_b), writes=(rt_b[2],))
                        fw.op("dve", lambda e, t4=t4, p2=p2, cosv=cosv, n=n: e.tensor_tensor(out=t4, in0=p2[:, 0:n], in1=cosv, op=ALU.mult),
                              reads=(p2b, rot_b), writes=(rt_b[3],))
                        fw.op("pool", lambda e, t1=t1, t2=t2, d=dstT[:, 0, c0:c0 + n]: e.tensor_tensor(out=d, in0=t1, in1=t2, op=ALU.subtract),
                              reads=(rt_b[0], rt_b[1]), writes=(dst_b[ci],))
                        fw.op("pool", lambda e, t3=t3, t4=t4, d=dstT[:, 1, c0:c0 + n]: e.tensor_tensor(out=d, in0=t3, in1=t4, op=ALU.add),
                              reads=(rt_b[2], rt_b[3]), writes=(dst_b[ci],))
                        if which == "q":
                            if ci < 2:
                                qd = CT("qdec", h * 128, (h + 1) * 128)
                                for fc in range(2):
                                    fw.op("dve", lambda e, fc=fc, c0=c0, qd=qd: e.tensor_tensor(
                                        out=qsT[:, fc, c0:c0 + 512].rearrange("p (a b) -> p a b", a=4),
                                        in0=qT[:, fc, c0:c0 + 512].rearrange("p (a b) -> p a b", a=4),
                                        in1=qd.unsqueeze(1).broadcast_to([128, 4, 128]), op=ALU.mult),
                                        reads=(qT_b[ci], ctab_b), writes=(qsT_b[ci],))
                            else:
                                qd = CT("qdecs", h * 32, (h + 1) * 32)
                                for fc in range(2):
                                    fw.op("dve", lambda e, fc=fc, c0=c0, qd=qd: e.tensor_tensor(
                                        out=qsT[:, fc, c0:c0 + 32], in0=qT[:, fc, c0:c0 + 32], in1=qd, op=ALU.mult),
                                        reads=(qT_b[ci], ctab_b), writes=(qsT_b[ci],))
                fw.tag = "ret.v"
                wv = [wq.get(), wq.get()]
                for c in range(NCH + 1):
                    ci = c // 4
                    c0 = c * 128
                    m = 128 if c < NCH else SCOL
                    pv, pvb = psum()
                    for half in range(2):
                        w3 = wv[half][0].rearrange("p (k c) -> p k c", k=KC)
                        mm_group(pv[0:m, half * 256:(half + 1) * 256], pvb,
                                 [(xb_t[:, kc, c0:c0 + m], w3[:, kc, :]) for kc in range(KC)],
                                 (wv[half][1], xb_b[ci]))
                    fw.op("act", lambda e, c=c, m=m, pv=pv: e.activation(out=vt[0:m, c, :], in_=pv[0:m, :], func=AF.Copy),
                          reads=(pvb,), writes=(vt_b[c],))
                fw.tag = "ret.g"
                for half in range(2):
                    w, wbuf = wq.get()
                    w3 = w.rearrange("p (k c) -> p k c", k=KC)
                    for vc2 in range(2):
                        vc = half * 2 + vc2
                        for ci, (c0, n) in enumerate(CS):
                            pg, pgb = psum()
                            mm_group(pg[:, 0:n], pgb, [(w3[:, kc, vc2 * 128:(vc2 + 1) * 128], xb_t[:, kc, c0:c0 + n])
                                                      for kc in range(KC)], (wbuf, xb_b[ci]))
                            fw.op("act", lambda e, pg=pg, n=n, vc=vc, c0=c0: e.activation(
                                out=sg[:, vc, c0:c0 + n], in_=pg[:, 0:n], func=AF.Silu), reads=(pgb,), writes=(sg_b[ci],))
                fw.tag = "ret.kT"
                for c in range(NCH + 1):
                    ci = c // 4
                    c0 = c * 128
                    m = 128 if c < NCH else SCOL
                    pk, pkb = psum()
                    pkb16 = pk[:].bitcast(BF16)

                    def fn(e, c0=c0, m=m, pkb16=pkb16):
                        ins = None
                        for fc in range(2):
                            ins = e.transpose(pkb16[0:m, fc * 128:(fc + 1) * 128], kT[:, fc, c0:c0 + m], ident_t[:])
                        return ins
                    fw.op("pe", fn, reads=(kT_b[ci], ident_b), writes=(pkb,), npe=2)
                    sc = CT("kdec", h, h + 1) if c < NCH else CT("kdecs", h, h + 1, rows=32)
                    fw.op("act", lambda e, c=c, m=m, pkb16=pkb16, sc=sc: e.activation(
                        out=kd[0:m, c, :], in_=pkb16[0:m, 0:256], func=AF.Identity, scale=sc),
                        reads=(pkb, ctab_b), writes=(kd_b[c],))
                cm = CT("colmask")
                for fc in range(2):
                    fw.op("pool", lambda e, fc=fc, cm=cm: e.tensor_tensor(
                        out=qsm[:, fc, :, :], in0=qsT[:, fc, PG:PG + 32].unsqueeze(1).broadcast_to([128, SG, 32]),
                        in1=cm.rearrange("p (a b) -> p a b", a=SG), op=ALU.mult),
                        reads=(qsT_b[2], ctab_b), writes=(qsm_b,))
                for b in range(SG):
                    fw.op("dve", lambda e, b=b: e.tensor_scalar(
                        out=kdb[0:32, b, :], in0=kd[0:32, NCH, :], scalar1=CT("rowmask", b, b + 1, rows=32), scalar2=None,
                        op0=ALU.mult), reads=(kd_b[NCH], ctab_b), writes=(kdb_b,))
                fw.tag = "ret.chunk"
                fw.op("act", lambda e, h=h: e.activation(out=sbf[0][:, :, :], in_=s32_t[:, h, :, :], func=AF.Copy),
                      reads=(s32_b[h],), writes=(sbf_b[0],))

                def SU(c):
                    for fc in range(2):
                        pS, pSb = psum()
                        mm_group(pS[:, :], pSb, [(kd[:, c, fc * 128:(fc + 1) * 128], vt[:, c, :])], (kd_b[c], vt_b[c]))
                        fw.op("dve", lambda e, pS=pS, fc=fc, h=h: e.scalar_tensor_tensor(
                            out=s32_t[:, h, fc, :], in0=s32_t[:, h, fc, :], scalar=cd[h], in1=pS[:, :],
                            op0=ALU.mult, op1=ALU.add), reads=(pSb, s32_b[h]), writes=(s32_b[h],))
                    nxt = (c + 1) % NSB
                    fw.op("act", lambda e, h=h, nxt=nxt: e.activation(out=sbf[nxt][:, :, :], in_=s32_t[:, h, :, :], func=AF.Copy),
                          reads=(s32_b[h],), writes=(sbf_b[nxt],))

                def SC(c, slot=None):
                    ci, c0 = c // 4, c * 128
                    samp = (c == NCH)
                    m = SCOL if samp else 128
                    psc, pscb = psum()
                    mm_group(psc[0:m, 0:m], pscb, [(kT[:, fc, c0:c0 + m], qT[:, fc, c0:c0 + m]) for fc in range(2)],
                             (kT_b[ci], qT_b[ci]))
                    sTt, sTb = (sT[c % 3], sT_b[c % 3]) if slot is None else (sT[slot], sT_b[slot])
                    mk = CT("maskT", h * 128, (h + 1) * 128) if not samp else CT("maskTs", h * 32, (h + 1) * 32, rows=32)
                    fw.op("dve", lambda e, sTt=sTt, psc=psc, mk=mk, m=m: e.tensor_tensor(
                        out=sTt[0:m, 0:m], in0=psc[0:m, 0:m], in1=mk, op=ALU.mult),
                        reads=(pscb, ctab_b), writes=(sTb,))

                def GN(c, po, pob, mid=None):
                    m = SCOL if c == NCH else 128
                    gs, gsb = gst[c % 3], gst_b[c % 3]
                    eps_ap = CT("gneps", rows=m)
                    fw.op("dve", lambda e, po=po, m=m, gs=gs: e.bn_stats(out=gs[0:m, 0:6], in_=po[0:m, :]),
                          reads=(pob,), writes=(gsb,))
                    fw.op("dve", lambda e, m=m, gs=gs: e.bn_aggr(out=gs[0:m, 6:8], in_=gs[0:m, 0:6]),
                          reads=(gsb,), writes=(gsb,))
                    fw.op("act", lambda e, m=m, eps_ap=eps_ap, gs=gs: e.activation(out=gs[0:m, 8:9], in_=gs[0:m, 7:8], func=AF.Sqrt,
                                                                                   bias=eps_ap, scale=1.0),
                          reads=(gsb, ctab_b), writes=(gsb,))
                    if mid is not None:
                        mid()
                    fw.op("dve", lambda e, m=m, gs=gs: e.reciprocal(out=gs[0:m, 8:9], in_=gs[0:m, 8:9]),
                          reads=(gsb,), writes=(gsb,))
                    fw.op("dve", lambda e, m=m, gs=gs: e.scalar_tensor_tensor(out=gs[0:m, 9:10], in0=gs[0:m, 6:7], scalar=-1.0,
                                                                              in1=gs[0:m, 8:9], op0=ALU.mult, op1=ALU.mult),
                          reads=(gsb,), writes=(gsb,))
                    ot, otb = ontm[c % 3], ontm_b[c % 3]
                    fw.op("act", lambda e, ot=ot, po=po, m=m, gs=gs: e.activation(out=ot[0:m, :], in_=po[0:m, :], func=AF.Identity,
                                                                                 bias=gs[0:m, 9:10], scale=gs[0:m, 8:9]),
                          reads=(pob, gsb), writes=(otb,))

                def TR(c):
                    ci, c0 = c // 4, c * 128
                    m = SCOL if c == NCH else 128
                    ot, otb = ontm[c % 3], ontm_b[c % 3]
                    pt_, ptb = psum()
                    pt16 = pt_[:].bitcast(BF16)

                    def fnT(e, ot=ot, m=m, pt16=pt16):
                        ins = None
                        for vc in range(4):
                            ins = e.transpose(pt16[:, vc * 128:vc * 128 + m], ot[0:m, vc * 128:(vc + 1) * 128], ident_t[0:m, 0:m])
                        return ins
                    fw.op("pe", fnT, reads=(otb, ident_b), writes=(ptb,), npe=4)
                    for vc in range(4):
                        fw.op("act", lambda e, vc=vc, m=m, c0=c0, pt16=pt16, h=h: e.activation(
                            out=onT[:, vc, c0:c0 + m], in_=pt16[:, vc * 128:vc * 128 + m], func=AF.Identity,
                            scale=PT("gn_g", h * 4 + vc, h * 4 + vc + 1)), reads=(ptb, ptab_b), writes=(onT_b[ci],))

                def O(c):
                    ci, c0 = c // 4, c * 128
                    sTt, sTb = sT[c % 3], sT_b[c % 3]
                    cur = c % NSB
                    po, pob = psum()
                    pairs = [(sTt[:, 0:128], vt[:, c, :])] + \
                            [(qsT[:, fc, c0:c0 + 128], sbf[cur][:, fc, :]) for fc in range(2)]
                    mm_group(po[:, :], pob, pairs, (sTb, vt_b[c], qsT_b[ci], sbf_b[cur]))
                    return po, pob

                fw.tag = "ret.chunk"
                SC(NCH, slot=3)
                sTt, sTb = sT[3], sT_b[3]
                po_s, pob_s = psum()
                po_idx = ps_b.index(pob_s)
                ps_pin.add(po_idx)

                def SAMP(b, sTt=sTt, sTb=sTb, po=po_s, pob=pob_s):
                    k = (h * SG + b) % NSS
                    oj = b % 2
                    sidx = g * SG + b
                    pairs = []
                    if b == 0:
                        pairs.append((sTt[0:32, 0:32], vt[0:32, NCH, :]))
                    pairs += [(qsm[:, fc, b, :], ssb[k][:, fc, :]) for fc in range(2)]

                    def fn(e, pairs=pairs, b=b, po=po):
                        ins = None
                        for i, (l, r) in enumerate(pairs):
                            ins = e.matmul(po[0:32, :], l, r, start=(b == 0 and i == 0),
                                           stop=(b == SG - 1 and i == len(pairs) - 1))
                        return ins
                    fw.op("pe", fn, reads=(sTb, vt_b[NCH], qsm_b, ssb_b[k]), writes=(pob,), npe=len(pairs))
                    for fc in range(2):
                        pS, pSb = psum()
                        mm_group(pS[:, :], pSb, [(kdb[0:32, b, fc * 128:(fc + 1) * 128], vt[0:32, NCH, :])],
                                 (kdb_b, vt_b[NCH]))
                        fw.op("dve", lambda e, pS=pS, fc=fc, k=k, h=h, oj=oj: e.scalar_tensor_tensor(
                            out=sout[oj][:, fc, :], in0=sst[k][:, fc, :], scalar=cds[h], in1=pS[:, :],
                            op0=ALU.mult, op1=ALU.add), reads=(pSb, sst_b[k]), writes=(lns_b[2 * oj], lns_b[2 * oj + 1]))
                    fw.dma(rets_d[sidx, h].rearrange("(f p) v -> p f v", p=128), sout[oj], lns_b[2 * oj], write=False,
                           is_output=True, extra=(lns_b[2 * oj + 1],))
                    pend.append((h, b))
                    if len(pend) > 0:
                        ph, pb = pend.pop(0)
                        nb_, nh_ = pb + NSS, ph
                        if nb_ >= SG:
                            nb_, nh_ = nb_ - SG, ph + 1
                        if nh_ < HEADS:
                            s_in(nh_, nb_)

                SU(0)
                SU(1)
                SC(0)
                for c in range(NCH):
                    if c + 2 < NCH:
                        SU(c + 2)
                    if c + 1 < NCH:
                        SC(c + 1)
                    po, pob = O(c)
                    GN(c, po, pob, mid=(lambda c=c: TR(c - 1)) if c >= 1 else None)
                    SAMP(c)
                TR(NCH - 1)
                c = NCH
                po, pob = po_s, pob_s
                ps_pin.discard(po_idx)
                GN(c, po, pob)
                TR(c)
                if g == NG - 1:
                    fw.dma(retp_d[h].rearrange("(f p) v -> p f v", p=128), s32_t[:, h, :, :], s32_b[h], write=False,
                           is_output=True)
                for ci, (c0, n) in enumerate(CS):
                    fw.op("dve", lambda e, c0=c0, n=n: e.tensor_tensor(
                        out=onT[:, :, c0:c0 + n], in0=onT[:, :, c0:c0 + n], in1=sg[:, :, c0:c0 + n], op=ALU.mult),
                        reads=(onT_b[ci], sg_b[ci]), writes=(onT_b[ci],))
                fw.tag = "ret.out"
                for oc in range(8):
                    w, wbuf = wq.get()
                    w3 = w.rearrange("p (k c) -> p k c", k=4)
                    for ci, (c0, n) in enumerate(CS):
                        po, pob = psum()
                        mm_group(po[:, 0:n], pob, [(w3[:, k, :], onT[:, k, c0:c0 + n]) for k in range(4)], (wbuf, onT_b[ci]))
                        xs = xa_t[:, oc, c0:c0 + n]
                        fw.op("dve", lambda e, xs=xs, po=po, n=n: e.tensor_tensor(out=xs, in0=po[:, 0:n], in1=xs, op=ALU.add),
                              reads=(pob, xa_b[ci]), writes=(xa_b[ci],))

        def rglru(g):
            cast_pat[:] = cast_pats["rec"]
            arena.reset()
            fw.arena_reset()
            gh = arena.alloc([RC, T], BF16)
            gh_b = [fw.abuf(f"gh{ci}") for ci in range(len(CS))]
            xc = arena.alloc([5, T], F32)
            xcb = arena.alloc([5, T], BF16)
            XPW = 3 + PG + SG * 7
            xp = [arena.alloc([XPW], F32), lns_t[:, :, :].rearrange("p a b -> p (a b)")[:, 0:XPW]]
            tmp = [[arena.alloc([T], F32) for _ in range(2)] for _ in range(2)]
            a2buf = [arena.alloc([T], F32) for _ in range(2)]
            cvs = arena.alloc([RC, SG, 3], F32)
            lrs = arena.alloc([RC, SG], F32)
            xc_b = [fw.abuf(f"xc{i}") for i in range(5)]
            xcb_b = [fw.abuf(f"xcb{i}") for i in range(5)]
            xp_b = [(fw.abuf("xp0"),), (lns_b[0], lns_b[1], lns_b[2])]
            tmp_b = [[fw.abuf(f"tmp{i}{j}") for j in range(2)] for i in range(2)]
            a2buf_b = [fw.abuf(f"a2buf{i}") for i in range(2)]
            cvs_b, lrs_b = fw.abuf("cvs"), fw.abuf("lrs")
            fw.dma(lrs, slru_d[:, :, g * SG:(g + 1) * SG], lrs_b, write=True)
            items = []
            for hf in range(2):
                items += [(wd["recin"][hf * 5 + fc], KC * 128) for fc in range(5)]
                items += [(wd["recin"][10 + hf * 5 + fc], KC * 128) for fc in range(5)]
                for fo in range(5):
                    items += [(wd["reca"][hf, fo], 5 * 128), (wd["reci"][hf, fo], 5 * 128)]
            items += [(wd["recout"][oc], RC * 128) for oc in range(8)]
            wq = WQ(items)
            tctr = 0
            for hf in range(2):
                fw.tag = "rec.gate"
                for fc in range(5):
                    F = hf * 5 + fc
                    w, wbuf = wq.get()
                    w3 = w.rearrange("p (k c) -> p k c", k=KC)
                    for ci, (c0, n) in enumerate(CS):
                        pg, pgb = psum()
                        mm_group(pg[:, 0:n], pgb, [(w3[:, kc, :], xb_t[:, kc, c0:c0 + n]) for kc in range(KC)], (wbuf, xb_b[ci]))
                        fw.op("act", lambda e, pg=pg, n=n, F=F, c0=c0: e.activation(
                            out=gh[:, F, c0:c0 + n], in_=pg[:, 0:n], func=AF.Gelu_apprx_tanh), reads=(pgb,), writes=(gh_b[ci],))
                fw.tag = "rec.conv"
                for fc in range(5):
                    F = hf * 5 + fc
                    w, wbuf = wq.get()
                    w3 = w.rearrange("p (k c) -> p k c", k=KC)
                    xpt, xpbs = xp[F % 2], xp_b[F % 2]
                    xps = xpt[:, 3 + PG:3 + PG + SG * 7].rearrange("p (b t) -> p b t", b=SG)
                    fw.op("dve", lambda e, xpt=xpt, F=F: e.tensor_copy(out=xpt[:, 0:3], in_=cc_t[:, F, :]),
                          reads=(cc_b,), writes=xpbs)
                    fw.dma(xps[:, :, 0:3], sconv_d[:, F, g * SG:(g + 1) * SG, :], xpbs[0], write=True, extra=xpbs[1:])
                    for ci, (c0, n) in enumerate(CS):
                        pg, pgb = psum()
                        mm_group(pg[:, 0:n], pgb, [(w3[:, kc, :], xb_t[:, kc, c0:c0 + n]) for kc in range(KC)], (wbuf, xb_b[ci]))
                        if ci < 2:
                            fw.op("act", lambda e, pg=pg, n=n, c0=c0, xpt=xpt: e.activation(
                                out=xpt[:, 3 + c0:3 + c0 + n], in_=pg[:, 0:n], func=AF.Copy), reads=(pgb,), writes=xpbs)
                        else:
                            fw.op("act", lambda e, pg=pg, xps=xps: e.activation(
                                out=xps[:, :, 3:7], in_=pg[:, 0:SCOL].rearrange("p (b t) -> p b t", b=SG), func=AF.Copy),
                                reads=(pgb,), writes=xpbs)
                    cw = lambda j, F=F: PT("conv_w", j * 10 + F, j * 10 + F + 1)
                    cbias = PT("conv_b", F, F + 1)
                    for (dst, src_of) in ((xc[:, fc, 0:PG], lambda j, xpt=xpt: xpt[:, j:j + PG]),
                                          (xc[:, fc, PG:T].rearrange("p (b t) -> p b t", b=SG), lambda j, xps=xps: xps[:, :, j:j + 4])):
                        fw.op("act", lambda e, dst=dst, src_of=src_of, cw=cw, cbias=cbias: e.activation(
                            out=dst, in_=src_of(3), func=AF.Identity, bias=cbias, scale=cw(3)),
                            reads=xpbs + (ptab_b,), writes=(xc_b[fc],))
                        for j in (2, 1, 0):
                            fw.op("dve", lambda e, dst=dst, src_of=src_of, cw=cw, j=j: e.scalar_tensor_tensor(
                                out=dst, in0=src_of(j), scalar=cw(j), in1=dst, op0=ALU.mult, op1=ALU.add),
                                reads=xpbs + (ptab_b, xc_b[fc]), writes=(xc_b[fc],))
                    fw.op("act", lambda e, fc=fc: e.activation(out=xcb[:, fc, :], in_=xc[:, fc, :], func=AF.Copy),
                          reads=(xc_b[fc],), writes=(xcb_b[fc],))
                    fw.op("dve", lambda e, xpt=xpt, F=F: e.tensor_copy(out=cc_t[:, F, :], in_=xpt[:, PG:PG + 3]),
                          reads=xpbs, writes=(cc_b,))
                    fw.op("dve", lambda e, xps=xps, F=F: e.tensor_copy(out=cvs[:, F, :, :], in_=xps[:, :, 4:7]),
                          reads=xpbs, writes=(cvs_b,))
                fw.tag = "rec.lru"
                for fo in range(5):
                    F = hf * 5 + fo
                    wa, wab = wq.get()
                    wi_, wib = wq.get()
                    wa3 = wa.rearrange("p (k c) -> p k c", k=5)
                    wi3 = wi_.rearrange("p (k c) -> p k c", k=5)
                    tm, tmb = tmp[tctr % 2], tmp_b[tctr % 2]
                    tctr += 1
                    r_t, i_t = tm
                    a2_t = a2buf[tctr % 2]
                    hs_t = a2_t
                    tmb = list(tmb) + [a2buf_b[tctr % 2], a2buf_b[tctr % 2]]
                    kcs = GATE_KCS[fo]
                    for ci, (c0, n) in enumerate(CS):
                        pr, prb = psum()
                        pi_, pib = psum()
                        mm_group(pr[:, 0:n], prb, [(wa3[:, kc, :], xcb[:, kc, c0:c0 + n]) for kc in kcs],
                                 (wab,) + tuple(xcb_b[kc] for kc in kcs))
                        mm_group(pi_[:, 0:n], pib, [(wi3[:, kc, :], xcb[:, kc, c0:c0 + n]) for kc in kcs],
                                 (wib,) + tuple(xcb_b[kc] for kc in kcs))
                        fw.op("act", lambda e, pr=pr, n=n, c0=c0, r_t=r_t, F=F: e.activation(
                            out=r_t[:, c0:c0 + n], in_=pr[:, 0:n], func=AF.Tanh, bias=pder_t[:, 116 + F:117 + F], scale=0.5),
                            reads=(prb, pder_b), writes=(tmb[0],))
                        fw.op("act", lambda e, pi_=pi_, n=n, c0=c0, i_t=i_t, F=F: e.activation(
                            out=i_t[:, c0:c0 + n], in_=pi_[:, 0:n], func=AF.Tanh, bias=pder_t[:, 126 + F:127 + F], scale=0.5),
                            reads=(pib, pder_b), writes=(tmb[1],))
                    cF = pder_t[:, 96 + F:97 + F]
                    c2F = pder_t[:, 106 + F:107 + F]
                    chF = pder_t[:, 136 + F:137 + F]
                    fw.op("act", lambda e, r_t=r_t, a2_t=a2_t, cF=cF: e.activation(out=a2_t[:, :], in_=r_t[:, :], func=AF.Exp, scale=cF, bias=cF),
                          reads=(tmb[0], pder_b), writes=(tmb[2],))
                    fw.op("act", lambda e, r_t=r_t, chF=chF: e.activation(out=r_t[:, :], in_=r_t[:, :], func=AF.Exp, scale=chF, bias=chF),
                          reads=(tmb[0], pder_b), writes=(tmb[0],))
                    fw.op("act", lambda e, a2_t=a2_t: e.activation(out=a2_t[:, :], in_=a2_t[:, :], func=AF.Sqrt,
                                                                   bias=CT("one"), scale=-1.0),
                          reads=(tmb[2], ctab_b), writes=(tmb[2],))
                    fw.op("dve", lambda e, i_t=i_t, fo=fo: e.scalar_tensor_tensor(out=i_t[:, :], in0=i_t[:, :], scalar=1.0, in1=xc[:, fo, :],
                                                                                 op0=ALU.add, op1=ALU.mult),
                          reads=(tmb[1], xc_b[fo]), writes=(tmb[1],))
                    fw.op("dve", lambda e, i_t=i_t, a2_t=a2_t: e.scalar_tensor_tensor(out=i_t[:, :], in0=i_t[:, :], scalar=0.5, in1=a2_t[:, :],
                                                                                     op0=ALU.mult, op1=ALU.mult),
                          reads=(tmb[1], tmb[2]), writes=(tmb[1],))
                    av = r_t[:, PG:T].rearrange("p (b t) -> p b t", b=SG)
                    uv = i_t[:, PG:T].rearrange("p (b t) -> p b t", b=SG)
                    hv = hs_t[:, PG:T].rearrange("p (b t) -> p b t", b=SG)
                    fw.op("dve", lambda e, av=av, hv=hv, F=F: e.tensor_tensor(out=hv[:, :, 0], in0=av[:, :, 0], in1=lrs[:, F, :], op=ALU.mult),
                          reads=(tmb[0], lrs_b), writes=(tmb[3],))
                    fw.op("dve", lambda e, uv=uv, hv=hv: e.tensor_tensor(out=uv[:, :, 0], in0=uv[:, :, 0], in1=hv[:, :, 0], op=ALU.add),
                          reads=(tmb[1], tmb[3]), writes=(tmb[1],))
                    fw.op("dve", lambda e, av=av: e.memset(av[:, :, 0], 0.0), writes=(tmb[0],))
                    fw.op("dve", lambda e, r_t=r_t, i_t=i_t, hs_t=hs_t, F=F: e.tensor_tensor_scan(
                        out=hs_t[:, 0:T], data0=r_t[:, 0:T], data1=i_t[:, 0:T], initial=hc_t[:, F:F + 1],
                        op0=ALU.mult, op1=ALU.add), reads=(tmb[0], tmb[1], hc_b), writes=(tmb[3],))
                    fw.op("dve", lambda e, hs_t=hs_t, F=F: e.tensor_copy(out=hc_t[:, F:F + 1], in_=hs_t[:, PG - 1:PG]),
                          reads=(tmb[3],), writes=(hc_b,))
                    fw.op("dve", lambda e, hv=hv, F=F: e.tensor_copy(out=lrs[:, F, :], in_=hv[:, :, DEC_T - 1]),
                          reads=(tmb[3],), writes=(lrs_b,))
                    for ci, (c0, n) in enumerate(CS):
                        fw.op("dve", lambda e, hs_t=hs_t, F=F, c0=c0, n=n: e.tensor_tensor(
                            out=gh[:, F, c0:c0 + n], in0=gh[:, F, c0:c0 + n], in1=hs_t[:, c0:c0 + n], op=ALU.mult),
                            reads=(tmb[3], gh_b[ci]), writes=(gh_b[ci],))
            fw.dma(convs_d[:, :, g * SG:(g + 1) * SG, :], cvs, cvs_b, write=False, is_output=True)
            fw.dma(lrus_d[:, :, g * SG:(g + 1) * SG], lrs, lrs_b, write=False, is_output=True)
            if g == NG - 1:
                fw.dma(convp_d[:, :, :], cc_t[:], cc_b, write=False, is_output=True)
                fw.dma(lrup_d[:, :], hc_t[:], hc_b, write=False, is_output=True)
            fw.tag = "rec.out"
            for oc in range(8):
                w, wbuf = wq.get()
                w3 = w.rearrange("p (k c) -> p k c", k=RC)
                for ci, (c0, n) in enumerate(CS):
                    po, pob = psum()
                    mm_group(po[:, 0:n], pob, [(w3[:, k, :], gh[:, k, c0:c0 + n]) for k in range(RC)], (wbuf, gh_b[ci]))
                    xs = xa_t[:, oc, c0:c0 + n]
                    fw.op("dve", lambda e, xs=xs, po=po, n=n: e.tensor_tensor(out=xs, in0=po[:, 0:n], in1=xs, op=ALU.add),
                          reads=(pob, xa_b[ci]), writes=(xa_b[ci],))

        stages = []
        for g in range(NG):
            stages.append(("load", g))
            for l in range(2):
                stages += [("ffn", "f1", l), ("ln", l * 3 + 0), ("mix", l, g), ("ln", l * 3 + 1), ("ffn", "f2", l), ("ln", l * 3 + 2)]
            stages.append(("store", g))
        nstage = 0
        for g in range(NG):
            load_x(g)
            done = False
            seq = [("ffn", "f1", 0), ("ln", 0), ("mix", 0), ("ln", 1), ("ffn", "f2", 0), ("ln", 2),
                   ("ffn", "f1", 1), ("ln", 3), ("mix", 1), ("ln", 4), ("ffn", "f2", 1), ("ln", 5)]
            for si, s in enumerate(seq):
                if stop is not None and si >= stop:
                    break
                if s[0] == "ffn":
                    ffn(s[1], s[2])
                elif s[0] == "ln":
                    layer_norm(s[1], final=(s[1] == 5))
                else:
                    if s[1] == 0:
                        retention(g)
                    else:
                        rglru(g)
            store_y(g)
        fw.finish()
        build_program.pe_tags = fw.pe_tags
        with nc.Block() as block:
            fw.replay(block)
    return nc


_CACHE = {}


def kernel(**inputs):
    stop = inputs.pop("_stop", None)
    ncores = inputs.pop("_cores", NCORES)
    ct, cd, cds = build_ctab()
    sh = prep_shared(inputs)
    in_maps = []
    for c in range(NCORES):
        d = dict(sh)
        d.update(prep_core(inputs, c))
        in_maps.append(d)
    key = ("nc", stop)
    nc = build_program(cd, cds, stop=stop)
    res = run_bass_kernel_spmd(nc, in_maps[:ncores], core_ids=list(range(ncores)))
    R = list(res.results) + [res.results[0]] * (NCORES - ncores)
    y_p = np.zeros((8, SEQ, D), np.float32)
    y_s = np.zeros((DEC_B, DEC_T, D), np.float32)
    ret_p = np.zeros((1, 8, HEADS, DK, DV), np.float32)
    conv_p = np.zeros((1, 8, 3, D_RNN), np.float32)
    lru_p = np.zeros((1, 8, D_RNN), np.float32)
    ret_s = np.zeros((1, DEC_B, HEADS, DK, DV), np.float32)
    conv_s = np.zeros((1, DEC_B, 3, D_RNN), np.float32)
    lru_s = np.zeros((1, DEC_B, D_RNN), np.float32)
    for c in range(NCORES):
        r = R[c]
        yT = np.asarray(r["yT"])
        yall = yT.transpose(2, 1, 0).reshape(SEQ + NSAMP * DEC_T, D)
        y_p[c] = yall[:SEQ]
        y_s[c * NSAMP:(c + 1) * NSAMP] = yall[SEQ:].reshape(NSAMP, DEC_T, D)
        ret_p[0, c] = np.asarray(r["ret_p"])
        ret_s[0, c * NSAMP:(c + 1) * NSAMP] = np.asarray(r["ret_s"])
        conv_p[0, c] = np.asarray(r["convT_p"]).transpose(2, 1, 0).reshape(3, D_RNN)
        conv_s[0, c * NSAMP:(c + 1) * NSAMP] = np.asarray(r["convT_s"]).transpose(2, 3, 1, 0).reshape(NSAMP, 3, D_RNN)
        lru_p[0, c] = np.asarray(r["lruT_p"]).T.reshape(D_RNN)
        lru_s[0, c * NSAMP:(c + 1) * NSAMP] = np.asarray(r["lruT_s"]).transpose(2, 1, 0).reshape(NSAMP, D_RNN)
    return (y_p, y_s, ret_p, conv_p, lru_p, ret_s, conv_s, lru_s)
```

```python
import math
from contextlib import ExitStack

import numpy as np
import concourse.bass as bass
import concourse.mybir as mybir
from concourse.bass_utils import run_bass_kernel_spmd

F32 = mybir.dt.float32
BF16 = mybir.dt.bfloat16
AF = mybir.ActivationFunctionType
ALU = mybir.AluOpType

NCORES = 8
D = 1024
KC = 8
SEQ = 2048
DEC_B = 128
DEC_T = 4
NSAMP = DEC_B // NCORES
NG = 2
PG = SEQ // NG
SG = NSAMP // NG
SCOL = SG * DEC_T
T = PG + SCOL
CS = [(0, 512), (512, 512), (1024, SCOL)]
NCH = PG // 128
HEADS = 4
DK = 256
DV = 512
D_RNN = 1280
RC = 10
D_FF = 2816
FJ = 22
PAST = 16384
ALPHA = (2.0 * 2) ** 0.25
LN_EPS = 1e-5
GN_EPS = 1e-6
LRU_C = 8.0
EPOCH_MAX = 8000

ENGS = ["sync", "act", "dve", "pool", "pe"]


class DSem:
    def __init__(self, handle):
        self.h = handle
        self.total = 0


class Buf:
    __slots__ = ("name", "w", "r", "dsem")

    def __init__(self, name):
        self.name = name
        self.w = None
        self.r = {}
        self.dsem = None


class FW:
    def __init__(self, nc, stack):
        self.nc = nc
        self.stack = stack
        self.prog = {e: [] for e in ENGS}
        self.cnt = {e: 0 for e in ENGS}
        self.epoch = {e: 0 for e in ENGS}
        self.seen = {e: {} for e in ENGS}
        self.esems = {}
        self.dsems = []
        self.out_dsems = []
        self.arena_bufs = []
        self.legacy = {}
        self.tag = "init"
        self.pe_tags = []

    def esem(self, eng, epoch):
        k = (eng, epoch)
        if k not in self.esems:
            self.esems[k] = self.stack.enter_context(self.nc.semaphore(f"e_{eng}_{epoch}"))
        return self.esems[k]

    def new_dsem(self, name):
        d = DSem(self.stack.enter_context(self.nc.semaphore(f"d_{name}_{len(self.dsems)}")))
        self.dsems.append(d)
        return d

    def _need(self, eng, tok):
        if tok is None:
            return
        if tok[0] == "e":
            _, te, tep, tc = tok
            if te == eng:
                if eng == "pe" or eng == "sync":
                    return
                if tep == self.epoch[eng] and tc + 2 <= self.cnt[eng]:
                    return
                if tep < self.epoch[eng] and self.cnt[eng] >= 2:
                    return
            key = (te, tep)
            if self.seen[eng].get(key, 0) >= tc:
                return
            self.seen[eng][key] = tc
            self.prog[eng].append(("w", self.esem(te, tep), tc))
        else:
            d = tok[1]
            key = ("d", id(d))
            if self.seen[eng].get(key, 0) >= d.total:
                return
            self.seen[eng][key] = d.total
            self.prog[eng].append(("w", d.h, d.total))

    def _deps(self, eng, reads, writes):
        for b in reads:
            self._need(eng, b.w)
        for b in writes:
            self._need(eng, b.w)
            for t in list(b.r.values()):
                self._need(eng, t)

    def op(self, eng, fn, reads=(), writes=(), npe=1):
        if eng == "pe":
            self.pe_tags.append((self.tag, npe))
        self._deps(eng, reads, writes)
        ep = self.epoch[eng]
        self.cnt[eng] += 1
        tok = ("e", eng, ep, self.cnt[eng])
        self.prog[eng].append(("o", fn, self.esem(eng, ep), 1))
        if self.cnt[eng] >= EPOCH_MAX:
            self.epoch[eng] += 1
            self.cnt[eng] = 0
        for b in reads:
            b.r[eng] = tok
        for b in writes:
            b.w = tok
            b.r = {}
        return tok

    def dma(self, out, in_, buf, write, is_output=False, extra=()):
        eng = "sync"
        if write:
            self._deps(eng, (), (buf,) + tuple(extra))
        else:
            self._deps(eng, (buf,) + tuple(extra), ())
        if buf.dsem is None:
            buf.dsem = self.new_dsem(buf.name)
        d = buf.dsem
        d.total += 16
        self.prog[eng].append(("o", lambda e, o=out, i=in_: e.dma_start(out=o, in_=i), d.h, 16))
        tok = ("d", d)
        for bb in (buf,) + tuple(extra):
            if write:
                bb.w = tok
                bb.r = {}
            else:
                bb.r["dma" + str(id(d))] = tok
        if is_output and d not in self.out_dsems:
            self.out_dsems.append(d)
        return tok

    def arena_reset(self):
        leg = dict(self.legacy)
        for b in self.arena_bufs:
            toks = list(b.r.values())
            if b.w is not None:
                toks.append(b.w)
            for t in toks:
                if t[0] == "e":
                    k = ("e", t[1])
                    o = leg.get(k)
                    if o is None or (o[2], o[3]) < (t[2], t[3]):
                        leg[k] = t
                else:
                    leg[("d", id(t[1]))] = t
        self.legacy = leg
        self.arena_bufs = []

    def abuf(self, name):
        b = Buf(name)
        b.r = dict(self.legacy)
        self.arena_bufs.append(b)
        return b

    def finish(self):
        for d in self.out_dsems:
            self._need("sync", ("d", d))

    def replay(self, block):
        prog = self.prog

        def run(name, e):
            for ent in prog[name]:
                if ent[0] == "w":
                    e.wait_ge(ent[1], ent[2])
                else:
                    ins = ent[1](e)
                    ins.then_inc(ent[2], ent[3])

        @block.sync
        def _(e):
            run("sync", e)

        @block.scalar
        def _(e):
            run("act", e)

        @block.vector
        def _(e):
            run("dve", e)

        @block.gpsimd
        def _(e):
            run("pool", e)

        @block.tensor
        def _(e):
            run("pe", e)


class Arena:
    def __init__(self, ap_f32, nbytes):
        self.ap = ap_f32
        self.nbytes = nbytes
        self.off = 0

    def reset(self):
        self.off = 0

    def alloc(self, shape_free, dtype):
        esz = 4 if dtype == F32 else 2
        n = int(np.prod(shape_free))
        nb = (n * esz + 31) // 32 * 32
        assert self.off + nb <= self.nbytes, f"arena overflow {self.off}+{nb}>{self.nbytes}"
        a = self.ap[:, self.off // 4:(self.off + nb) // 4]
        self.off += nb
        if dtype != F32:
            a = a.bitcast(dtype)
        a = a[:, 0:n]
        if len(shape_free) == 2:
            a = a.rearrange("p (a b) -> p a b", a=shape_free[0])
        elif len(shape_free) == 3:
            a = a.rearrange("p (a b c) -> p a b c", a=shape_free[0], b=shape_free[1])
        return a


def _ctab_layout():
    lay = {}
    off = 0
    for name, w in [("maskT", 4 * 128), ("qdec", 4 * 128), ("maskTs", 4 * 32), ("qdecs", 4 * 32),
                    ("colmask", SG * 32), ("rowmask", SG), ("kdec", 4), ("kdecs", 4),
                    ("lneps", 1), ("gneps", 1), ("ident", 128), ("one", 1)]:
        lay[name] = (off, w)
        off += w
    return lay, off


CT_LAY, CT_W = _ctab_layout()


def _ptab_layout():
    lay = {}
    off = 0
    for name, w in [("ln_g", 48), ("ln_b", 48), ("gn_g", 16), ("conv_w", 40), ("conv_b", 10),
                    ("b_a", 10), ("b_i", 10), ("lam", 10)]:
        lay[name] = (off, w)
        off += w
    return lay, off


PT_LAY, PT_W = _ptab_layout()


def build_ctab():
    c = np.zeros((128, CT_W), np.float64)
    lg = np.log1p(-np.exp2(-5.0 - np.arange(HEADS)))
    idx = np.arange(128)
    o, _ = CT_LAY["maskT"]
    for h in range(HEADS):
        rel = idx[None, :] - idx[:, None]
        m = np.where(rel >= 0, np.exp(lg[h] * np.maximum(rel, 0)), 0.0) / 16.0
        c[:, o + h * 128:o + (h + 1) * 128] = m
    o, _ = CT_LAY["qdec"]
    for h in range(HEADS):
        c[:, o + h * 128:o + (h + 1) * 128] = np.exp(lg[h] * (idx + 1.0))[None, :]
    m32 = np.arange(32)
    bb, tt = m32 // 4, m32 % 4
    o, _ = CT_LAY["maskTs"]
    for h in range(HEADS):
        rel = tt[None, :] - tt[:, None]
        m = np.where((rel >= 0) & (bb[None, :] == bb[:, None]), np.exp(lg[h] * np.maximum(rel, 0)), 0.0) / 16.0
        c[0:32, o + h * 32:o + (h + 1) * 32] = m
    o, _ = CT_LAY["qdecs"]
    for h in range(HEADS):
        c[:, o + h * 32:o + (h + 1) * 32] = np.exp(lg[h] * (tt + 1.0))[None, :]
    o, _ = CT_LAY["colmask"]
    for b in range(SG):
        c[:, o + b * 32:o + (b + 1) * 32] = (bb == b).astype(np.float64)[None, :]
    o, _ = CT_LAY["rowmask"]
    for b in range(SG):
        c[0:32, o + b] = (bb == b)
    o, _ = CT_LAY["kdec"]
    for h in range(HEADS):
        c[:, o + h] = np.exp(lg[h] * (127.0 - idx)) / 16.0
    o, _ = CT_LAY["kdecs"]
    for h in range(HEADS):
        c[0:32, o + h] = np.exp(lg[h] * (3.0 - tt)) / 16.0
    c[:, CT_LAY["lneps"][0]] = LN_EPS
    c[:, CT_LAY["gneps"][0]] = GN_EPS
    o, _ = CT_LAY["ident"]
    c[:, o:o + 128] = np.eye(128)
    c[:, CT_LAY["one"][0]] = 1.0
    cd = [float(np.exp(lg[h] * 128.0)) for h in range(HEADS)]
    cds = [float(np.exp(lg[h] * 4.0)) for h in range(HEADS)]
    return c.astype(np.float32), cd, cds


def build_rot():
    half = DK // 2
    inv = (10000.0 ** (-np.arange(half, dtype=np.float32) / np.float32(half))).astype(np.float32)
    rot = np.zeros((NG, 128, 2, T), np.float32)
    for g in range(NG):
        pos = np.concatenate([np.arange(g * PG, (g + 1) * PG), np.tile(PAST + np.arange(DEC_T), SG)]).astype(np.float32)
        ang = (pos[None, :] * inv[:, None]).astype(np.float32)
        rot[g, :, 0, :] = np.cos(ang)
        rot[g, :, 1, :] = np.sin(ang)
    return rot


def rec_gate_kcs():
    res = []
    for fo in range(5):
        lo, hi = fo * 128, fo * 128 + 127
        b0, b1 = lo // 160, hi // 160
        ilo, ihi = b0 * 160, b1 * 160 + 159
        res.append(list(range(ilo // 128, ihi // 128 + 1)))
    return res


GATE_KCS = rec_gate_kcs()


def fm(v, nchunk):
    return np.ascontiguousarray(v.reshape(nchunk, 128).T)


def prep_shared(inp):
    f = lambda a: np.ascontiguousarray(np.asarray(a, dtype=np.float32))
    sh = {}
    for l in range(2):
        for nm, wi, wo in (("f1", "ffn1_w_in", "ffn1_w_out"), ("f2", "ffn2_w_in", "ffn2_w_out")):
            W = f(inp[wi][l])
            Wr = W.reshape(KC, 128, 2, FJ, 128).transpose(3, 1, 0, 2, 4)
            sh[f"{nm}in{l}"] = np.ascontiguousarray(Wr).reshape(FJ, 128, KC * 256)
            Wo = f(inp[wo][l])
            Wor = Wo.reshape(2, 11, 128, 8, 128).transpose(3, 0, 2, 1, 4)
            sh[f"{nm}out{l}"] = np.ascontiguousarray(Wor).reshape(8, 2, 128, 11 * 128)
    W = f(inp["ret_w_in"][0])
    blocks = []
    for h in range(HEADS):
        cols = [W[:, h * 256:(h + 1) * 256], W[:, 1024 + h * 256:1024 + (h + 1) * 256],
                W[:, 2048 + h * 512:2048 + h * 512 + 256], W[:, 2048 + h * 512 + 256:2048 + (h + 1) * 512],
                W[:, 4096 + h * 512:4096 + h * 512 + 256], W[:, 4096 + h * 512 + 256:4096 + (h + 1) * 512]]
        for cblk in cols:
            blocks.append(cblk.reshape(KC, 128, 256).transpose(1, 0, 2).reshape(128, KC * 256))
    sh["retin"] = np.ascontiguousarray(np.stack(blocks)).reshape(HEADS, 6, 128, KC * 256)
    Wo = f(inp["ret_w_out"][0])
    sh["retout"] = np.ascontiguousarray(Wo.reshape(HEADS, 4, 128, 8, 128).transpose(0, 3, 2, 1, 4)).reshape(HEADS, 8, 128, 4 * 128)
    W = f(inp["rec_w_in"][0])
    sh["recin"] = np.ascontiguousarray(W.reshape(KC, 128, 20, 128).transpose(2, 1, 0, 3)).reshape(20, 128, KC * 128)
    for nm, key in (("reca", "rec_w_a"), ("reci", "rec_w_i")):
        Wb = f(inp[key][0])
        dense = np.zeros((2, 640, 640), np.float32)
        for n in range(8):
            hf, nl = n // 4, n % 4
            dense[hf, nl * 160:(nl + 1) * 160, nl * 160:(nl + 1) * 160] = Wb[n]
        sh[nm] = np.ascontiguousarray(dense.reshape(2, 5, 128, 5, 128).transpose(0, 3, 2, 1, 4)).reshape(2, 5, 128, 5 * 128)
    Wo = f(inp["rec_w_out"][0])
    sh["recout"] = np.ascontiguousarray(Wo.reshape(RC, 128, 8, 128).transpose(2, 1, 0, 3)).reshape(8, 128, RC * 128)
    pt = np.zeros((128, PT_W), np.float32)
    o = PT_LAY["ln_g"][0]
    for l in range(2):
        for i in range(3):
            pt[:, o + (l * 3 + i) * 8:o + (l * 3 + i + 1) * 8] = fm(f(inp["ln_g"][l, i]), 8)
    o = PT_LAY["ln_b"][0]
    for l in range(2):
        for i in range(3):
            pt[:, o + (l * 3 + i) * 8:o + (l * 3 + i + 1) * 8] = fm(f(inp["ln_b"][l, i]), 8)
    o = PT_LAY["gn_g"][0]
    pt[:, o:o + 16] = fm(f(inp["ret_gn_g"][0]), 16)
    o = PT_LAY["conv_w"][0]
    cw = f(inp["rec_conv_w"][0])
    for j in range(4):
        pt[:, o + j * 10:o + (j + 1) * 10] = fm(cw[j], RC)
    pt[:, PT_LAY["conv_b"][0]:PT_LAY["conv_b"][0] + 10] = fm(f(inp["rec_conv_b"][0]), RC)
    pt[:, PT_LAY["b_a"][0]:PT_LAY["b_a"][0] + 10] = fm(f(inp["rec_b_a"][0]), RC)
    pt[:, PT_LAY["b_i"][0]:PT_LAY["b_i"][0] + 10] = fm(f(inp["rec_b_i"][0]), RC)
    pt[:, PT_LAY["lam"][0]:PT_LAY["lam"][0] + 10] = fm(f(inp["rec_lam"][0]), RC)
    sh["ptab"] = pt
    ct, cd, cds = build_ctab()
    sh["ctab"] = ct
    sh["rot"] = build_rot()
    return sh


def prep_core(inp, c):
    f = lambda a: np.asarray(a, dtype=np.float32)
    xp = f(inp["x_prompt"][c])
    xs = f(inp["x_sample"][c * NSAMP:(c + 1) * NSAMP]).reshape(NSAMP * DEC_T, D)
    xall = np.concatenate([xp, xs], axis=0)
    xT = np.ascontiguousarray(xall.reshape(SEQ + NSAMP * DEC_T, KC, 128).transpose(2, 1, 0))
    d = {"xT": xT}
    d["sret"] = np.ascontiguousarray(f(inp["state_ret"][0, c * NSAMP:(c + 1) * NSAMP]))
    sc = f(inp["state_conv"][0, c * NSAMP:(c + 1) * NSAMP])
    d["sconvT"] = np.ascontiguousarray(sc.reshape(NSAMP, 3, RC, 128).transpose(3, 2, 0, 1))
    sl = f(inp["state_lru"][0, c * NSAMP:(c + 1) * NSAMP])
    d["slruT"] = np.ascontiguousarray(sl.reshape(NSAMP, RC, 128).transpose(2, 1, 0))
    return d


def build_program(cd, cds, stop=None):
    nc = bass.Bass("TRN2", target_bir_lowering=False)
    NTOK = SEQ + NSAMP * DEC_T

    def din(name, shape):
        return nc.dram_tensor(name, list(shape), F32, kind="ExternalInput").ap()

    def dout(name, shape):
        return nc.dram_tensor(name, list(shape), F32, kind="ExternalOutput").ap()

    xT_d = din("xT", [128, KC, NTOK])
    sret_d = din("sret", [NSAMP, HEADS, DK, DV])
    sconv_d = din("sconvT", [128, RC, NSAMP, 3])
    slru_d = din("slruT", [128, RC, NSAMP])
    ptab_d = din("ptab", [128, PT_W])
    ctab_d = din("ctab", [128, CT_W])
    rot_d = din("rot", [NG, 128, 2, T])
    wd = {}
    for l in range(2):
        for nm in ("f1", "f2"):
            wd[f"{nm}in{l}"] = din(f"{nm}in{l}", [FJ, 128, KC * 256])
            wd[f"{nm}out{l}"] = din(f"{nm}out{l}", [8, 2, 128, 11 * 128])
    wd["retin"] = din("retin", [HEADS, 6, 128, KC * 256])
    wd["retout"] = din("retout", [HEADS, 8, 128, 4 * 128])
    wd["recin"] = din("recin", [20, 128, KC * 128])
    wd["reca"] = din("reca", [2, 5, 128, 5 * 128])
    wd["reci"] = din("reci", [2, 5, 128, 5 * 128])
    wd["recout"] = din("recout", [8, 128, RC * 128])

    yT_d = dout("yT", [128, KC, NTOK])
    retp_d = dout("ret_p", [HEADS, DK, DV])
    rets_d = dout("ret_s", [NSAMP, HEADS, DK, DV])
    convp_d = dout("convT_p", [128, RC, 3])
    convs_d = dout("convT_s", [128, RC, NSAMP, 3])
    lrup_d = dout("lruT_p", [128, RC])
    lrus_d = dout("lruT_s", [128, RC, NSAMP])

    st = ExitStack()
    with st:
        fw = FW(nc, st)

        def sb(name, shape, dt):
            return st.enter_context(nc.sbuf_tensor("s_" + name, list(shape), dt))

        xa_t = sb("xa", [128, KC, T], F32)
        xb_t = sb("xb", [128, KC, T], BF16)
        NSTG, NWB = 2, 4
        stg_t = [sb(f"stg{i}", [128, 2048], F32) for i in range(NSTG)]
        wb_t = [sb(f"wb{i}", [128, 2048], BF16) for i in range(NWB)]
        s32_t = sb("s32", [128, HEADS, 2, DV], F32)
        ctab_t = sb("ctab", [128, CT_W], F32)
        ptab_t = sb("ptab", [128, PT_W], F32)
        pder_t = sb("pder", [128, 48 + 48 + 10 + 10 + 30], F32)
        ident_t = sb("ident", [128, 128], BF16)
        ones_t = sb("ones", [128, 128], BF16)
        onesf_t = sb("onesf", [128, 128], F32)
        LNW = 512
        lnz_t = sb("lnz", [128, 4 * T], BF16)
        zsq_t = lnz_t[:, 0:KC * LNW].rearrange("p (k n) -> p k n", k=KC)
        lns_t = sb("lns", [128, 4, LNW], F32)
        lnx_t = sb("lnx", [128, 2, SCOL], F32)
        hc_t = sb("hcarry", [128, RC], F32)
        cc_t = sb("ccarry", [128, RC, 3], F32)
        ARENA_BYTES = 84 * 1024
        arena_t = sb("arena", [128, ARENA_BYTES // 4], F32)
        arena = Arena(arena_t[:], ARENA_BYTES)
        psum_t = [st.enter_context(nc.psum_tensor(f"ps{i}", [128, 512], F32)) for i in range(8)]

        xa_b = [Buf(f"xa{i}") for i in range(len(CS))]
        xb_b = [Buf(f"xb{i}") for i in range(len(CS))]
        stg_b = [Buf(f"stg{i}") for i in range(NSTG)]
        wb_b = [Buf(f"wb{i}") for i in range(NWB)]
        s32_b = [Buf(f"s32_{h}") for h in range(HEADS)]
        ctab_b = Buf("ctab")
        ptab_b = Buf("ptab")
        pder_b = Buf("pder")
        ident_b = Buf("ident")
        ones_b = Buf("ones")
        onesf_b = Buf("onesf")
        zb_b = Buf("lnz")
        zsq_b = zb_b
        lns_b = [Buf(f"lns{i}") for i in range(4)]
        lnx_b = [Buf(f"lnx{i}") for i in range(2)]
        hc_b = Buf("hc")
        cc_b = Buf("cc")
        ps_b = [Buf(f"ps{i}") for i in range(8)]
        ps_rr = [0]

        ps_pin = set()

        def psum():
            while True:
                i = ps_rr[0] % 8
                ps_rr[0] += 1
                if i not in ps_pin:
                    return psum_t[i], ps_b[i]

        def CT(name, lo=0, hi=None, rows=128):
            o, w = CT_LAY[name]
            hi = w if hi is None else hi
            return ctab_t[0:rows, o + lo:o + hi]

        def PT(name, lo=0, hi=None):
            o, w = PT_LAY[name]
            hi = w if hi is None else hi
            return ptab_t[:, o + lo:o + hi]

        wctr = [0]
        cast_pat = ["pool", "pool", "act"]
        cast_pats = {"ffn": ["pool", "pool", "act"], "ret": ["act", "act", "pool"], "rec": ["pool"]}

        def wload(dram_ap, nelem):
            i = wctr[0]
            wctr[0] += 1
            s, w = i % NSTG, i % NWB
            fw.dma(stg_t[s][:, 0:nelem], dram_ap, stg_b[s], write=True)
            ce = cast_pat[i % len(cast_pat)]
            if ce == "act":
                fw.op("act", lambda e, o=wb_t[w][:, 0:nelem], a=stg_t[s][:, 0:nelem]: e.copy(out=o, in_=a),
                      reads=(stg_b[s],), writes=(wb_b[w],))
            else:
                fw.op(ce, lambda e, o=wb_t[w][:, 0:nelem], a=stg_t[s][:, 0:nelem]: e.tensor_copy(out=o, in_=a),
                      reads=(stg_b[s],), writes=(wb_b[w],))
            return wb_t[w][:, 0:nelem], wb_b[w]

        class WQ:
            def __init__(self, items, depth=2):
                self.items = list(items)
                self.loaded = []
                self.depth = depth
                self.n = 0

            def get(self):
                while len(self.loaded) < self.depth + 1 and self.n < len(self.items):
                    ap_, ne = self.items[self.n]
                    self.loaded.append(wload(ap_, ne))
                    self.n += 1
                return self.loaded.pop(0)

        fw.dma(ctab_t[:], ctab_d[:, :], ctab_b, write=True)
        fw.dma(ptab_t[:], ptab_d[:, :], ptab_b, write=True)
        fw.op("dve", lambda e: e.tensor_copy(out=ident_t[:], in_=CT("ident")), reads=(ctab_b,), writes=(ident_b,))
        fw.op("dve", lambda e: e.memset(ones_t[:], 1.0 / D), writes=(ones_b,))
        fw.op("dve", lambda e: e.memset(onesf_t[:], 1.0 / D), writes=(onesf_b,))
        fw.op("dve", lambda e: e.tensor_scalar(out=pder_t[:, 0:96], in0=ptab_t[:, 0:96], scalar1=ALPHA, scalar2=None,
                                               op0=ALU.mult), reads=(ptab_b,), writes=(pder_b,))
        fw.op("act", lambda e: e.activation(out=pder_t[:, 96:106], in_=PT("lam"), func=AF.Exp, scale=-1.0),
              reads=(ptab_b,), writes=(pder_b,))
        fw.op("act", lambda e: e.activation(out=pder_t[:, 96:106], in_=pder_t[:, 96:106], func=AF.Ln,
                                            bias=CT("one"), scale=1.0), reads=(pder_b, ctab_b), writes=(pder_b,))
        fw.op("dve", lambda e: e.tensor_scalar(out=pder_t[:, 106:116], in0=pder_t[:, 96:106], scalar1=-2.0 * LRU_C,
                                               scalar2=None, op0=ALU.mult), reads=(pder_b,), writes=(pder_b,))
        fw.op("dve", lambda e: e.tensor_scalar(out=pder_t[:, 96:106], in0=pder_t[:, 96:106], scalar1=-LRU_C,
                                               scalar2=None, op0=ALU.mult), reads=(pder_b,), writes=(pder_b,))
        fw.op("dve", lambda e: e.tensor_scalar(out=pder_t[:, 116:126], in0=PT("b_a"), scalar1=0.5, scalar2=None, op0=ALU.mult),
              reads=(ptab_b,), writes=(pder_b,))
        fw.op("dve", lambda e: e.tensor_scalar(out=pder_t[:, 126:136], in0=PT("b_i"), scalar1=0.5, scalar2=None, op0=ALU.mult),
              reads=(ptab_b,), writes=(pder_b,))
        fw.op("dve", lambda e: e.tensor_scalar(out=pder_t[:, 136:146], in0=pder_t[:, 96:106], scalar1=0.5, scalar2=None, op0=ALU.mult),
              reads=(pder_b,), writes=(pder_b,))
        fw.op("dve", lambda e: e.memset(s32_t[:], 0.0), writes=tuple(s32_b))
        fw.op("dve", lambda e: e.memset(hc_t[:], 0.0), writes=(hc_b,))
        fw.op("dve", lambda e: e.memset(cc_t[:], 0.0), writes=(cc_b,))

        def AG(idx, kc):
            return pder_t[:, idx * 8 + kc:idx * 8 + kc + 1]

        def AB(idx, kc):
            return pder_t[:, 48 + idx * 8 + kc:48 + idx * 8 + kc + 1]

        def mm_group(out_ap, out_buf, pairs, reads, npe=None):
            def fn(e, pairs=pairs, out_ap=out_ap):
                ins = None
                n = len(pairs)
                for i, (l, r) in enumerate(pairs):
                    ins = e.matmul(out_ap, l, r, start=(i == 0), stop=(i == n - 1))
                return ins
            fw.op("pe", fn, reads=reads, writes=(out_buf,), npe=(npe or len(pairs)))

        def layer_norm(idx, final=False, g=0):
            fw.tag = f"ln{idx}"
            stats = []
            for ci, (c0, n) in enumerate(CS):
                z3 = xa_t[:, :, c0:c0 + n]
                pm, pmb = psum()
                mm_group(pm[:, 0:n], pmb, [(onesf_t[:], xa_t[:, kc, c0:c0 + n]) for kc in range(KC)], (onesf_b, xa_b[ci]), npe=2 * KC)
                fw.op("act", lambda e, z3=z3, n=n: e.activation(out=zsq_t[:, :, 0:n], in_=z3, func=AF.Square),
                      reads=(xa_b[ci],), writes=(zsq_b,))
                pe2, pe2b = psum()
                mm_group(pe2[:, 0:n], pe2b, [(ones_t[:], zsq_t[:, kc, 0:n]) for kc in range(KC)], (ones_b, zsq_b))
                if ci < 2:
                    vv, nmr = lns_t[:, 2 * ci, 0:n], lns_t[:, 2 * ci + 1, 0:n]
                    vb, nb_ = lns_b[2 * ci], lns_b[2 * ci + 1]
                else:
                    vv, nmr = lnx_t[:, 0, 0:n], lnx_t[:, 1, 0:n]
                    vb, nb_ = lnx_b[0], lnx_b[1]
                fw.op("act", lambda e, vv=vv, pm=pm, n=n: e.activation(out=vv, in_=pm[:, 0:n], func=AF.Square),
                      reads=(pmb,), writes=(vb,))
                fw.op("dve", lambda e, vv=vv, pe2=pe2, n=n: e.tensor_tensor(out=vv, in0=pe2[:, 0:n], in1=vv, op=ALU.subtract),
                      reads=(pe2b, vb), writes=(vb,))
                fw.op("act", lambda e, vv=vv: e.activation(out=vv, in_=vv, func=AF.Sqrt, bias=CT("lneps"), scale=1.0),
                      reads=(vb, ctab_b), writes=(vb,))
                fw.op("dve", lambda e, vv=vv: e.reciprocal(out=vv, in_=vv), reads=(vb,), writes=(vb,))
                fw.op("dve", lambda e, pm=pm, n=n, vv=vv, nmr=nmr: e.scalar_tensor_tensor(
                    out=nmr, in0=pm[:, 0:n], scalar=-1.0, in1=vv, op0=ALU.mult, op1=ALU.mult),
                    reads=(pmb, vb), writes=(nb_,))
                stats.append((vv, nmr, vb, nb_))
            for ci, (c0, n) in enumerate(CS):
                z3 = xa_t[:, :, c0:c0 + n]
                vv, nmr, vb, nb_ = stats[ci]
                fw.op("dve", lambda e, z3=z3, vv=vv, n=n: e.tensor_tensor(
                    out=z3, in0=z3, in1=vv.unsqueeze(1).broadcast_to([128, KC, n]), op=ALU.mult),
                    reads=(xa_b[ci], vb), writes=(xa_b[ci],))
                fw.op("dve", lambda e, z3=z3, nmr=nmr, n=n: e.tensor_tensor(
                    out=z3, in0=z3, in1=nmr.unsqueeze(1).broadcast_to([128, KC, n]), op=ALU.add),
                    reads=(xa_b[ci], nb_), writes=(xa_b[ci],))
                for kc in range(KC):
                    zc = xa_t[:, kc, c0:c0 + n]
                    xo = xb_t[:, kc, c0:c0 + n]
                    if not final:
                        fw.op("act", lambda e, zc=zc, xo=xo, kc=kc: e.activation(
                            out=xo, in_=zc, func=AF.Identity, bias=PT("ln_b", idx * 8 + kc, idx * 8 + kc + 1),
                            scale=PT("ln_g", idx * 8 + kc, idx * 8 + kc + 1)),
                            reads=(xa_b[ci], ptab_b), writes=(xb_b[ci],))
                    else:
                        fw.op("act", lambda e, zc=zc, kc=kc: e.activation(
                            out=zc, in_=zc, func=AF.Identity, bias=PT("ln_b", idx * 8 + kc, idx * 8 + kc + 1),
                            scale=PT("ln_g", idx * 8 + kc, idx * 8 + kc + 1)),
                            reads=(xa_b[ci], ptab_b), writes=(xa_b[ci],))
            if not final:
                for ci, (c0, n) in enumerate(CS):
                    for kc in range(KC):
                        zc = xa_t[:, kc, c0:c0 + n]
                        fw.op("dve", lambda e, zc=zc, kc=kc: e.tensor_scalar(
                            out=zc, in0=zc, scalar1=AG(idx, kc), scalar2=AB(idx, kc), op0=ALU.mult, op1=ALU.add),
                            reads=(xa_b[ci], pder_b), writes=(xa_b[ci],))

        def ffn(nm, l):
            cast_pat[:] = cast_pats["ffn"]
            arena.reset()
            fw.arena_reset()
            h_t = arena.alloc([FJ, T], BF16)
            s_t = [arena.alloc([512], BF16) for _ in range(2)]
            h_b = [fw.abuf(f"h{ci}") for ci in range(len(CS))]
            s_b = [fw.abuf(f"s{i}") for i in range(2)]
            win, wout = wd[f"{nm}in{l}"], wd[f"{nm}out{l}"]
            items = [(win[j], KC * 256) for j in range(FJ)] + \
                    [(wout[oc, kh], 11 * 128) for oc in range(8) for kh in range(2)]
            wq = WQ(items)
            si = 0
            fw.tag = f"{nm}{l}.in"
            for j in range(FJ):
                w, wbuf = wq.get()
                w3 = w.rearrange("p (k c) -> p k c", k=KC)
                for ci, (c0, n) in enumerate(CS):
                    pg, pgb = psum()
                    pu, pub = psum()
                    mm_group(pg[:, 0:n], pgb, [(w3[:, kc, 0:128], xb_t[:, kc, c0:c0 + n]) for kc in range(KC)],
                             (wbuf, xb_b[ci]))
                    mm_group(pu[:, 0:n], pub, [(w3[:, kc, 128:256], xb_t[:, kc, c0:c0 + n]) for kc in range(KC)],
                             (wbuf, xb_b[ci]))
                    sa, sab = s_t[si % 2], s_b[si % 2]
                    si += 1
                    fw.op("act", lambda e, sa=sa, pg=pg, n=n: e.activation(out=sa[:, 0:n], in_=pg[:, 0:n], func=AF.Silu),
                          reads=(pgb,), writes=(sab,))
                    fw.op("dve", lambda e, sa=sa, pu=pu, n=n, j=j, c0=c0: e.tensor_tensor(
                        out=h_t[:, j, c0:c0 + n], in0=pu[:, 0:n], in1=sa[:, 0:n], op=ALU.mult),
                        reads=(pub, sab), writes=(h_b[ci],))
            fw.tag = f"{nm}{l}.out"
            for oc in range(8):
                wA, wAb = wq.get()
                wB, wBb = wq.get()
                wA3 = wA.rearrange("p (k c) -> p k c", k=11)
                wB3 = wB.rearrange("p (k c) -> p k c", k=11)
                for ci, (c0, n) in enumerate(CS):
                    po, pob = psum()
                    pairs = [(wA3[:, k, :], h_t[:, k, c0:c0 + n]) for k in range(11)] + \
                            [(wB3[:, k, :], h_t[:, 11 + k, c0:c0 + n]) for k in range(11)]
                    mm_group(po[:, 0:n], pob, pairs, (wAb, wBb, h_b[ci]))
                    xs = xa_t[:, oc, c0:c0 + n]
                    fw.op("dve", lambda e, xs=xs, po=po, n=n: e.scalar_tensor_tensor(
                        out=xs, in0=po[:, 0:n], scalar=0.5, in1=xs, op0=ALU.mult, op1=ALU.add),
                        reads=(pob, xa_b[ci]), writes=(xa_b[ci],))

        def load_x(g):
            for ci, (c0, n) in enumerate(CS):
                src0 = g * PG + c0 if ci < 2 else SEQ + g * SCOL
                fw.dma(xa_t[:, :, c0:c0 + n], xT_d[:, :, src0:src0 + n], xa_b[ci], write=True)
                fw.op("pool", lambda e, c0=c0, n=n: e.tensor_copy(out=xb_t[:, :, c0:c0 + n], in_=xa_t[:, :, c0:c0 + n]),
                      reads=(xa_b[ci],), writes=(xb_b[ci],))
                fw.op("act", lambda e, c0=c0, n=n: e.mul(out=xa_t[:, :, c0:c0 + n], in_=xa_t[:, :, c0:c0 + n], mul=ALPHA),
                      reads=(xa_b[ci],), writes=(xa_b[ci],))

        def store_y(g):
            for ci, (c0, n) in enumerate(CS):
                dst0 = g * PG + c0 if ci < 2 else SEQ + g * SCOL
                fw.dma(yT_d[:, :, dst0:dst0 + n], xa_t[:, :, c0:c0 + n], xa_b[ci], write=False, is_output=True)

        def retention(g):
            cast_pat[:] = cast_pats["ret"]
            arena.reset()
            fw.arena_reset()
            onT = arena.alloc([4, T], BF16)
            onT_b = [fw.abuf(f"onT{ci}") for ci in range(len(CS))]
            rot_t = lnz_t[:].bitcast(F32)[:, 0:2 * T].rearrange("p (a b) -> p a b", a=2)
            rot_b = zb_b
            fw.dma(rot_t, rot_d[g], rot_b, write=True)
            qT = arena.alloc([2, T], BF16)
            qsT = arena.alloc([2, T], BF16)
            kT = arena.alloc([2, T], BF16)
            kd = arena.alloc([NCH + 1, 256], BF16)
            vt = arena.alloc([NCH + 1, 512], BF16)
            sg = arena.alloc([4, T], BF16)
            rt = [lns_t[:, i, :] for i in range(4)]
            NSS = 3
            sst = [arena.alloc([2, 512], F32) for _ in range(NSS)]
            ssb = [arena.alloc([2, 512], BF16) for _ in range(NSS)]
            NSB = 4
            sbf = [arena.alloc([2, 512], BF16) for _ in range(NSB)]
            sout = [lns_t[:, 2 * j:2 * j + 2, :] for j in range(2)]
            ontm = [arena.alloc([512], BF16) for _ in range(3)]
            sT = [arena.alloc([128], BF16) for _ in range(4)]
            qsm = arena.alloc([2, SG, 32], BF16)
            kdb = arena.alloc([SG, 256], BF16)
            gst = [arena.alloc([16], F32) for _ in range(3)]
            nb = lambda nm, k=1: [fw.abuf(f"{nm}{i}") for i in range(k)]
            qT_b, qsT_b, kT_b = nb("qT", 3), nb("qsT", 3), nb("kT", 3)
            kd_b, vt_b = nb("kd", NCH + 1), nb("vt", NCH + 1)
            sg_b = nb("sg", 3)
            rt_b = lns_b
            sst_b, ssb_b, sbf_b, ontm_b, sT_b = nb("sst", NSS), nb("ssb", NSS), nb("sbf", NSB), nb("ontm", 3), nb("sT", 4)
            qsm_b, kdb_b, gst_b = fw.abuf("qsm"), fw.abuf("kdb"), nb("gst", 3)

            win = wd["retin"]
            items = []
            for h in range(HEADS):
                items += [(win[h, k], KC * 256) for k in range(6)]
                items += [(wd["retout"][h, oc], 4 * 128) for oc in range(8)]
            wq = WQ(items)
            sctr = [0]
            def s_in(h, b):
                k = (h * SG + b) % NSS
                fw.dma(sst[k], sret_d[g * SG + b, h].rearrange("(f p) v -> p f v", p=128), sst_b[k], write=True)
                fw.op("pool", lambda e, k=k: e.tensor_copy(out=ssb[k][:, :, :], in_=sst[k][:, :, :]),
                      reads=(sst_b[k],), writes=(ssb_b[k],))

            for b in range(NSS):
                s_in(0, b)
            pend = []
            for h in range(HEADS):
                fw.tag = "ret.qk"
                for which, dstT, dst_b in (("q", qT, qT_b), ("k", kT, kT_b)):
                    w, wbuf = wq.get()
                    w3 = w.rearrange("p (k c) -> p k c", k=KC)
                    for ci, (c0, n) in enumerate(CS):
                        p1, p1b = psum()
                        p2, p2b = psum()
                        mm_group(p1[:, 0:n], p1b, [(w3[:, kc, 0:128], xb_t[:, kc, c0:c0 + n]) for kc in range(KC)],
                                 (wbuf, xb_b[ci]))
                        mm_group(p2[:, 0:n], p2b, [(w3[:, kc, 128:256], xb_t[:, kc, c0:c0 + n]) for kc in range(KC)],
                                 (wbuf, xb_b[ci]))
                        cosv, sinv = rot_t[:, 0, c0:c0 + n], rot_t[:, 1, c0:c0 + n]
                        t1, t2, t3, t4 = (rt[i][:, 0:n] for i in range(4))
                        fw.op("dve", lambda e, t1=t1, p1=p1, cosv=cosv, n=n: e.tensor_tensor(out=t1, in0=p1[:, 0:n], in1=cosv, op=ALU.mult),
                              reads=(p1b, rot_b), writes=(rt_b[0],))
                        fw.op("dve", lambda e, t2=t2, p2=p2, sinv=sinv, n=n: e.tensor_tensor(out=t2, in0=p2[:, 0:n], in1=sinv, op=ALU.mult),
                              reads=(p2b, rot_b), writes=(rt_b[1],))
                        fw.op("dve", lambda e, t3=t3, p1=p1, sinv=sinv, n=n: e.tensor_tensor(out=t3, in0=p1[:, 0:n], in1=sinv, op=ALU.mult),
                              reads=(p1b, rot_b), writes=(rt_b[2],))
                        fw.op("dve", lambda e, t4=t4, p2=p2, cosv=cosv, n=n: e.tensor_tensor(out=t4, in0=p2[:, 0:n], in1=cosv, op=ALU.mult),
                              reads=(p2b, rot_b), writes=(rt_b[3],))
                        fw.op("pool", lambda e, t1=t1, t2=t2, d=dstT[:, 0, c0:c0 + n]: e.tensor_tensor(out=d, in0=t1, in1=t2, op=ALU.subtract),
                              reads=(rt_b[0], rt_b[1]), writes=(dst_b[ci],))
                        fw.op("pool", lambda e, t3=t3, t4=t4, d=dstT[:, 1, c0:c0 + n]: e.tensor_tensor(out=d, in0=t3, in1=t4, op=ALU.add),
                              reads=(rt_b[2], rt_b[3]), writes=(dst_b[ci],))
                        if which == "q":
                            if ci < 2:
                                qd = CT("qdec", h * 128, (h + 1) * 128)
                                for fc in range(2):
                                    fw.op("pool", lambda e, fc=fc, c0=c0, qd=qd: e.tensor_tensor(
                                        out=qsT[:, fc, c0:c0 + 512].rearrange("p (a b) -> p a b", a=4),
                                        in0=qT[:, fc, c0:c0 + 512].rearrange("p (a b) -> p a b", a=4),
                                        in1=qd.unsqueeze(1).broadcast_to([128, 4, 128]), op=ALU.mult),
                                        reads=(qT_b[ci], ctab_b), writes=(qsT_b[ci],))
                            else:
                                qd = CT("qdecs", h * 32, (h + 1) * 32)
                                for fc in range(2):
                                    fw.op("pool", lambda e, fc=fc, c0=c0, qd=qd: e.tensor_tensor(
                                        out=qsT[:, fc, c0:c0 + 32], in0=qT[:, fc, c0:c0 + 32], in1=qd, op=ALU.mult),
                                        reads=(qT_b[ci], ctab_b), writes=(qsT_b[ci],))
                fw.tag = "ret.v"
                wv = [wq.get(), wq.get()]
                for c in range(NCH + 1):
                    ci = c // 4
                    c0 = c * 128
                    m = 128 if c < NCH else SCOL
                    pv, pvb = psum()
                    for half in range(2):
                        w3 = wv[half][0].rearrange("p (k c) -> p k c", k=KC)
                        mm_group(pv[0:m, half * 256:(half + 1) * 256], pvb,
                                 [(xb_t[:, kc, c0:c0 + m], w3[:, kc, :]) for kc in range(KC)],
                                 (wv[half][1], xb_b[ci]))
                    fw.op("act", lambda e, c=c, m=m, pv=pv: e.activation(out=vt[0:m, c, :], in_=pv[0:m, :], func=AF.Copy),
                          reads=(pvb,), writes=(vt_b[c],))
                fw.tag = "ret.g"
                for half in range(2):
                    w, wbuf = wq.get()
                    w3 = w.rearrange("p (k c) -> p k c", k=KC)
                    for vc2 in range(2):
                        vc = half * 2 + vc2
                        for ci, (c0, n) in enumerate(CS):
                            pg, pgb = psum()
                            mm_group(pg[:, 0:n], pgb, [(w3[:, kc, vc2 * 128:(vc2 + 1) * 128], xb_t[:, kc, c0:c0 + n])
                                                      for kc in range(KC)], (wbuf, xb_b[ci]))
                            fw.op("act", lambda e, pg=pg, n=n, vc=vc, c0=c0: e.activation(
                                out=sg[:, vc, c0:c0 + n], in_=pg[:, 0:n], func=AF.Silu), reads=(pgb,), writes=(sg_b[ci],))
                fw.tag = "ret.kT"
                for c in range(NCH + 1):
                    ci = c // 4
                    c0 = c * 128
                    m = 128 if c < NCH else SCOL
                    pk, pkb = psum()
                    pkb16 = pk[:].bitcast(BF16)

                    def fn(e, c0=c0, m=m, pkb16=pkb16):
                        ins = None
                        for fc in range(2):
                            ins = e.transpose(pkb16[0:m, fc * 128:(fc + 1) * 128], kT[:, fc, c0:c0 + m], ident_t[:])
                        return ins
                    fw.op("pe", fn, reads=(kT_b[ci], ident_b), writes=(pkb,), npe=2)
                    sc = CT("kdec", h, h + 1) if c < NCH else CT("kdecs", h, h + 1, rows=32)
                    fw.op("act", lambda e, c=c, m=m, pkb16=pkb16, sc=sc: e.activation(
                        out=kd[0:m, c, :], in_=pkb16[0:m, 0:256], func=AF.Identity, scale=sc),
                        reads=(pkb, ctab_b), writes=(kd_b[c],))
                cm = CT("colmask")
                for fc in range(2):
                    fw.op("pool", lambda e, fc=fc, cm=cm: e.tensor_tensor(
                        out=qsm[:, fc, :, :], in0=qsT[:, fc, PG:PG + 32].unsqueeze(1).broadcast_to([128, SG, 32]),
                        in1=cm.rearrange("p (a b) -> p a b", a=SG), op=ALU.mult),
                        reads=(qsT_b[2], ctab_b), writes=(qsm_b,))
                for b in range(SG):
                    fw.op("dve", lambda e, b=b: e.tensor_scalar(
                        out=kdb[0:32, b, :], in0=kd[0:32, NCH, :], scalar1=CT("rowmask", b, b + 1, rows=32), scalar2=None,
                        op0=ALU.mult), reads=(kd_b[NCH], ctab_b), writes=(kdb_b,))
                fw.tag = "ret.chunk"
                fw.op("act", lambda e, h=h: e.activation(out=sbf[0][:, :, :], in_=s32_t[:, h, :, :], func=AF.Copy),
                      reads=(s32_b[h],), writes=(sbf_b[0],))

                def SU(c):
                    for fc in range(2):
                        pS, pSb = psum()
                        mm_group(pS[:, :], pSb, [(kd[:, c, fc * 128:(fc + 1) * 128], vt[:, c, :])], (kd_b[c], vt_b[c]))
                        fw.op("dve", lambda e, pS=pS, fc=fc, h=h: e.scalar_tensor_tensor(
                            out=s32_t[:, h, fc, :], in0=s32_t[:, h, fc, :], scalar=cd[h], in1=pS[:, :],
                            op0=ALU.mult, op1=ALU.add), reads=(pSb, s32_b[h]), writes=(s32_b[h],))
                    nxt = (c + 1) % NSB
                    fw.op("act", lambda e, h=h, nxt=nxt: e.activation(out=sbf[nxt][:, :, :], in_=s32_t[:, h, :, :], func=AF.Copy),
                          reads=(s32_b[h],), writes=(sbf_b[nxt],))

                def SC(c, slot=None):
                    ci, c0 = c // 4, c * 128
                    samp = (c == NCH)
                    m = SCOL if samp else 128
                    psc, pscb = psum()
                    mm_group(psc[0:m, 0:m], pscb, [(kT[:, fc, c0:c0 + m], qT[:, fc, c0:c0 + m]) for fc in range(2)],
                             (kT_b[ci], qT_b[ci]))
                    sTt, sTb = (sT[c % 3], sT_b[c % 3]) if slot is None else (sT[slot], sT_b[slot])
                    mk = CT("maskT", h * 128, (h + 1) * 128) if not samp else CT("maskTs", h * 32, (h + 1) * 32, rows=32)
                    fw.op("dve", lambda e, sTt=sTt, psc=psc, mk=mk, m=m: e.tensor_tensor(
                        out=sTt[0:m, 0:m], in0=psc[0:m, 0:m], in1=mk, op=ALU.mult),
                        reads=(pscb, ctab_b), writes=(sTb,))

                def GN(c, po, pob, mid=None):
                    m = SCOL if c == NCH else 128
                    gs, gsb = gst[c % 3], gst_b[c % 3]
                    eps_ap = CT("gneps", rows=m)
                    fw.op("dve", lambda e, po=po, m=m, gs=gs: e.bn_stats(out=gs[0:m, 0:6], in_=po[0:m, :]),
                          reads=(pob,), writes=(gsb,))
                    fw.op("dve", lambda e, m=m, gs=gs: e.bn_aggr(out=gs[0:m, 6:8], in_=gs[0:m, 0:6]),
                          reads=(gsb,), writes=(gsb,))
                    fw.op("act", lambda e, m=m, eps_ap=eps_ap, gs=gs: e.activation(out=gs[0:m, 8:9], in_=gs[0:m, 7:8], func=AF.Sqrt,
                                                                                   bias=eps_ap, scale=1.0),
                          reads=(gsb, ctab_b), writes=(gsb,))
                    if mid is not None:
                        mid()
                    fw.op("dve", lambda e, m=m, gs=gs: e.reciprocal(out=gs[0:m, 8:9], in_=gs[0:m, 8:9]),
                          reads=(gsb,), writes=(gsb,))
                    fw.op("dve", lambda e, m=m, gs=gs: e.scalar_tensor_tensor(out=gs[0:m, 9:10], in0=gs[0:m, 6:7], scalar=-1.0,
                                                                              in1=gs[0:m, 8:9], op0=ALU.mult, op1=ALU.mult),
                          reads=(gsb,), writes=(gsb,))
                    ot, otb = ontm[c % 3], ontm_b[c % 3]
                    fw.op("act", lambda e, ot=ot, po=po, m=m, gs=gs: e.activation(out=ot[0:m, :], in_=po[0:m, :], func=AF.Identity,
                                                                                 bias=gs[0:m, 9:10], scale=gs[0:m, 8:9]),
                          reads=(pob, gsb), writes=(otb,))

                def TR(c):
                    ci, c0 = c // 4, c * 128
                    m = SCOL if c == NCH else 128
                    ot, otb = ontm[c % 3], ontm_b[c % 3]
                    pt_, ptb = psum()
                    pt16 = pt_[:].bitcast(BF16)

                    def fnT(e, ot=ot, m=m, pt16=pt16):
                        ins = None
                        for vc in range(4):
                            ins = e.transpose(pt16[:, vc * 128:vc * 128 + m], ot[0:m, vc * 128:(vc + 1) * 128], ident_t[0:m, 0:m])
                        return ins
                    fw.op("pe", fnT, reads=(otb, ident_b), writes=(ptb,), npe=4)
                    for vc in range(4):
                        fw.op("act", lambda e, vc=vc, m=m, c0=c0, pt16=pt16, h=h: e.activation(
                            out=onT[:, vc, c0:c0 + m], in_=pt16[:, vc * 128:vc * 128 + m], func=AF.Identity,
                            scale=PT("gn_g", h * 4 + vc, h * 4 + vc + 1)), reads=(ptb, ptab_b), writes=(onT_b[ci],))

                def O(c):
                    ci, c0 = c // 4, c * 128
                    sTt, sTb = sT[c % 3], sT_b[c % 3]
                    cur = c % NSB
                    po, pob = psum()
                    pairs = [(sTt[:, 0:128], vt[:, c, :])] + \
                            [(qsT[:, fc, c0:c0 + 128], sbf[cur][:, fc, :]) for fc in range(2)]
                    mm_group(po[:, :], pob, pairs, (sTb, vt_b[c], qsT_b[ci], sbf_b[cur]))
                    return po, pob

                fw.tag = "ret.chunk"
                SC(NCH, slot=3)
                sTt, sTb = sT[3], sT_b[3]
                po_s, pob_s = psum()
                po_idx = ps_b.index(pob_s)
                ps_pin.add(po_idx)

                def SAMP(b, sTt=sTt, sTb=sTb, po=po_s, pob=pob_s):
                    k = (h * SG + b) % NSS
                    oj = b % 2
                    sidx = g * SG + b
                    pairs = []
                    if b == 0:
                        pairs.append((sTt[0:32, 0:32], vt[0:32, NCH, :]))
                    pairs += [(qsm[:, fc, b, :], ssb[k][:, fc, :]) for fc in range(2)]

                    def fn(e, pairs=pairs, b=b, po=po):
                        ins = None
                        for i, (l, r) in enumerate(pairs):
                            ins = e.matmul(po[0:32, :], l, r, start=(b == 0 and i == 0),
                                           stop=(b == SG - 1 and i == len(pairs) - 1))
                        return ins
                    fw.op("pe", fn, reads=(sTb, vt_b[NCH], qsm_b, ssb_b[k]), writes=(pob,), npe=len(pairs))
                    for fc in range(2):
                        pS, pSb = psum()
                        mm_group(pS[:, :], pSb, [(kdb[0:32, b, fc * 128:(fc + 1) * 128], vt[0:32, NCH, :])],
                                 (kdb_b, vt_b[NCH]))
                        fw.op("dve", lambda e, pS=pS, fc=fc, k=k, h=h, oj=oj: e.scalar_tensor_tensor(
                            out=sout[oj][:, fc, :], in0=sst[k][:, fc, :], scalar=cds[h], in1=pS[:, :],
                            op0=ALU.mult, op1=ALU.add), reads=(pSb, sst_b[k]), writes=(lns_b[2 * oj], lns_b[2 * oj + 1]))
                    fw.dma(rets_d[sidx, h].rearrange("(f p) v -> p f v", p=128), sout[oj], lns_b[2 * oj], write=False,
                           is_output=True, extra=(lns_b[2 * oj + 1],))
                    pend.append((h, b))
                    if len(pend) > 0:
                        ph, pb = pend.pop(0)
                        nb_, nh_ = pb + NSS, ph
                        if nb_ >= SG:
                            nb_, nh_ = nb_ - SG, ph + 1
                        if nh_ < HEADS:
                            s_in(nh_, nb_)

                SU(0)
                SU(1)
                SC(0)
                for c in range(NCH):
                    if c + 2 < NCH:
                        SU(c + 2)
                    if c + 1 < NCH:
                        SC(c + 1)
                    po, pob = O(c)
                    GN(c, po, pob, mid=(lambda c=c: TR(c - 1)) if c >= 1 else None)
                    SAMP(c)
                TR(NCH - 1)
                c = NCH
                po, pob = po_s, pob_s
                ps_pin.discard(po_idx)
                GN(c, po, pob)
                TR(c)
                if g == NG - 1:
                    fw.dma(retp_d[h].rearrange("(f p) v -> p f v", p=128), s32_t[:, h, :, :], s32_b[h], write=False,
                           is_output=True)
                for ci, (c0, n) in enumerate(CS):
                    fw.op("dve", lambda e, c0=c0, n=n: e.tensor_tensor(
                        out=onT[:, :, c0:c0 + n], in0=onT[:, :, c0:c0 + n], in1=sg[:, :, c0:c0 + n], op=ALU.mult),
                        reads=(onT_b[ci], sg_b[ci]), writes=(onT_b[ci],))
                fw.tag = "ret.out"
                for oc in range(8):
                    w, wbuf = wq.get()
                    w3 = w.rearrange("p (k c) -> p k c", k=4)
                    for ci, (c0, n) in enumerate(CS):
                        po, pob = psum()
                        mm_group(po[:, 0:n], pob, [(w3[:, k, :], onT[:, k, c0:c0 + n]) for k in range(4)], (wbuf, onT_b[ci]))
                        xs = xa_t[:, oc, c0:c0 + n]
                        fw.op("dve", lambda e, xs=xs, po=po, n=n: e.tensor_tensor(out=xs, in0=po[:, 0:n], in1=xs, op=ALU.add),
                              reads=(pob, xa_b[ci]), writes=(xa_b[ci],))

        def rglru(g):
            cast_pat[:] = cast_pats["rec"]
            arena.reset()
            fw.arena_reset()
            gh = arena.alloc([RC, T], BF16)
            gh_b = [fw.abuf(f"gh{ci}") for ci in range(len(CS))]
            xc = arena.alloc([5, T], F32)
            xcb = arena.alloc([5, T], BF16)
            XPW = 3 + PG + SG * 7
            xp = [arena.alloc([XPW], F32), lns_t[:, :, :].rearrange("p a b -> p (a b)")[:, 0:XPW]]
            tmp = [[arena.alloc([T], F32) for _ in range(2)] for _ in range(2)]
            a2buf = [arena.alloc([T], F32) for _ in range(2)]
            cvs = arena.alloc([RC, SG, 3], F32)
            lrs = arena.alloc([RC, SG], F32)
            xc_b = [fw.abuf(f"xc{i}") for i in range(5)]
            xcb_b = [fw.abuf(f"xcb{i}") for i in range(5)]
            xp_b = [(fw.abuf("xp0"),), (lns_b[0], lns_b[1], lns_b[2])]
            tmp_b = [[fw.abuf(f"tmp{i}{j}") for j in range(2)] for i in range(2)]
            a2buf_b = [fw.abuf(f"a2buf{i}") for i in range(2)]
            cvs_b, lrs_b = fw.abuf("cvs"), fw.abuf("lrs")
            fw.dma(lrs, slru_d[:, :, g * SG:(g + 1) * SG], lrs_b, write=True)
            items = []
            for hf in range(2):
                items += [(wd["recin"][hf * 5 + fc], KC * 128) for fc in range(5)]
                items += [(wd["recin"][10 + hf * 5 + fc], KC * 128) for fc in range(5)]
                for fo in range(5):
                    items += [(wd["reca"][hf, fo], 5 * 128), (wd["reci"][hf, fo], 5 * 128)]
            items += [(wd["recout"][oc], RC * 128) for oc in range(8)]
            wq = WQ(items)
            tctr = 0
            for hf in range(2):
                fw.tag = "rec.gate"
                for fc in range(5):
                    F = hf * 5 + fc
                    w, wbuf = wq.get()
                    w3 = w.rearrange("p (k c) -> p k c", k=KC)
                    for ci, (c0, n) in enumerate(CS):
                        pg, pgb = psum()
                        mm_group(pg[:, 0:n], pgb, [(w3[:, kc, :], xb_t[:, kc, c0:c0 + n]) for kc in range(KC)], (wbuf, xb_b[ci]))
                        fw.op("act", lambda e, pg=pg, n=n, F=F, c0=c0: e.activation(
                            out=gh[:, F, c0:c0 + n], in_=pg[:, 0:n], func=AF.Gelu_apprx_tanh), reads=(pgb,), writes=(gh_b[ci],))
                fw.tag = "rec.conv"
                for fc in range(5):
                    F = hf * 5 + fc
                    w, wbuf = wq.get()
                    w3 = w.rearrange("p (k c) -> p k c", k=KC)
                    xpt, xpbs = xp[F % 2], xp_b[F % 2]
                    xps = xpt[:, 3 + PG:3 + PG + SG * 7].rearrange("p (b t) -> p b t", b=SG)
                    fw.op("dve", lambda e, xpt=xpt, F=F: e.tensor_copy(out=xpt[:, 0:3], in_=cc_t[:, F, :]),
                          reads=(cc_b,), writes=xpbs)
                    fw.dma(xps[:, :, 0:3], sconv_d[:, F, g * SG:(g + 1) * SG, :], xpbs[0], write=True, extra=xpbs[1:])
                    for ci, (c0, n) in enumerate(CS):
                        pg, pgb = psum()
                        mm_group(pg[:, 0:n], pgb, [(w3[:, kc, :], xb_t[:, kc, c0:c0 + n]) for kc in range(KC)], (wbuf, xb_b[ci]))
                        if ci < 2:
                            fw.op("act", lambda e, pg=pg, n=n, c0=c0, xpt=xpt: e.activation(
                                out=xpt[:, 3 + c0:3 + c0 + n], in_=pg[:, 0:n], func=AF.Copy), reads=(pgb,), writes=xpbs)
                        else:
                            fw.op("act", lambda e, pg=pg, xps=xps: e.activation(
                                out=xps[:, :, 3:7], in_=pg[:, 0:SCOL].rearrange("p (b t) -> p b t", b=SG), func=AF.Copy),
                                reads=(pgb,), writes=xpbs)
                    cw = lambda j, F=F: PT("conv_w", j * 10 + F, j * 10 + F + 1)
                    cbias = PT("conv_b", F, F + 1)
                    for (dst, src_of) in ((xc[:, fc, 0:PG], lambda j, xpt=xpt: xpt[:, j:j + PG]),
                                          (xc[:, fc, PG:T].rearrange("p (b t) -> p b t", b=SG), lambda j, xps=xps: xps[:, :, j:j + 4])):
                        fw.op("act", lambda e, dst=dst, src_of=src_of, cw=cw, cbias=cbias: e.activation(
                            out=dst, in_=src_of(3), func=AF.Identity, bias=cbias, scale=cw(3)),
                            reads=xpbs + (ptab_b,), writes=(xc_b[fc],))
                        for j in (2, 1, 0):
                            fw.op("dve", lambda e, dst=dst, src_of=src_of, cw=cw, j=j: e.scalar_tensor_tensor(
                                out=dst, in0=src_of(j), scalar=cw(j), in1=dst, op0=ALU.mult, op1=ALU.add),
                                reads=xpbs + (ptab_b, xc_b[fc]), writes=(xc_b[fc],))
                    fw.op("act", lambda e, fc=fc: e.activation(out=xcb[:, fc, :], in_=xc[:, fc, :], func=AF.Copy),
                          reads=(xc_b[fc],), writes=(xcb_b[fc],))
                    fw.op("dve", lambda e, xpt=xpt, F=F: e.tensor_copy(out=cc_t[:, F, :], in_=xpt[:, PG:PG + 3]),
                          reads=xpbs, writes=(cc_b,))
                    fw.op("dve", lambda e, xps=xps, F=F: e.tensor_copy(out=cvs[:, F, :, :], in_=xps[:, :, 4:7]),
                          reads=xpbs, writes=(cvs_b,))
                fw.tag = "rec.lru"
                for fo in range(5):
                    F = hf * 5 + fo
                    wa, wab = wq.get()
                    wi_, wib = wq.get()
                    wa3 = wa.rearrange("p (k c) -> p k c", k=5)
                    wi3 = wi_.rearrange("p (k c) -> p k c", k=5)
                    tm, tmb = tmp[tctr % 2], tmp_b[tctr % 2]
                    tctr += 1
                    r_t, i_t = tm
                    a2_t = a2buf[tctr % 2]
                    hs_t = a2_t
                    tmb = list(tmb) + [a2buf_b[tctr % 2], a2buf_b[tctr % 2]]
                    kcs = GATE_KCS[fo]
                    for ci, (c0, n) in enumerate(CS):
                        pr, prb = psum()
                        pi_, pib = psum()
                        mm_group(pr[:, 0:n], prb, [(wa3[:, kc, :], xcb[:, kc, c0:c0 + n]) for kc in kcs],
                                 (wab,) + tuple(xcb_b[kc] for kc in kcs))
                        mm_group(pi_[:, 0:n], pib, [(wi3[:, kc, :], xcb[:, kc, c0:c0 + n]) for kc in kcs],
                                 (wib,) + tuple(xcb_b[kc] for kc in kcs))
                        fw.op("act", lambda e, pr=pr, n=n, c0=c0, r_t=r_t, F=F: e.activation(
                            out=r_t[:, c0:c0 + n], in_=pr[:, 0:n], func=AF.Tanh, bias=pder_t[:, 116 + F:117 + F], scale=0.5),
                            reads=(prb, pder_b), writes=(tmb[0],))
                        fw.op("act", lambda e, pi_=pi_, n=n, c0=c0, i_t=i_t, F=F: e.activation(
                            out=i_t[:, c0:c0 + n], in_=pi_[:, 0:n], func=AF.Tanh, bias=pder_t[:, 126 + F:127 + F], scale=0.5),
                            reads=(pib, pder_b), writes=(tmb[1],))
                    cF = pder_t[:, 96 + F:97 + F]
                    c2F = pder_t[:, 106 + F:107 + F]
                    chF = pder_t[:, 136 + F:137 + F]
                    fw.op("act", lambda e, r_t=r_t, a2_t=a2_t, cF=cF: e.activation(out=a2_t[:, :], in_=r_t[:, :], func=AF.Exp, scale=cF, bias=cF),
                          reads=(tmb[0], pder_b), writes=(tmb[2],))
                    fw.op("act", lambda e, r_t=r_t, chF=chF: e.activation(out=r_t[:, :], in_=r_t[:, :], func=AF.Exp, scale=chF, bias=chF),
                          reads=(tmb[0], pder_b), writes=(tmb[0],))
                    fw.op("act", lambda e, a2_t=a2_t: e.activation(out=a2_t[:, :], in_=a2_t[:, :], func=AF.Sqrt,
                                                                   bias=CT("one"), scale=-1.0),
                          reads=(tmb[2], ctab_b), writes=(tmb[2],))
                    fw.op("dve", lambda e, i_t=i_t, fo=fo: e.scalar_tensor_tensor(out=i_t[:, :], in0=i_t[:, :], scalar=1.0, in1=xc[:, fo, :],
                                                                                 op0=ALU.add, op1=ALU.mult),
                          reads=(tmb[1], xc_b[fo]), writes=(tmb[1],))
                    fw.op("dve", lambda e, i_t=i_t, a2_t=a2_t: e.scalar_tensor_tensor(out=i_t[:, :], in0=i_t[:, :], scalar=0.5, in1=a2_t[:, :],
                                                                                     op0=ALU.mult, op1=ALU.mult),
                          reads=(tmb[1], tmb[2]), writes=(tmb[1],))
                    av = r_t[:, PG:T].rearrange("p (b t) -> p b t", b=SG)
                    uv = i_t[:, PG:T].rearrange("p (b t) -> p b t", b=SG)
                    hv = hs_t[:, PG:T].rearrange("p (b t) -> p b t", b=SG)
                    fw.op("dve", lambda e, av=av, hv=hv, F=F: e.tensor_tensor(out=hv[:, :, 0], in0=av[:, :, 0], in1=lrs[:, F, :], op=ALU.mult),
                          reads=(tmb[0], lrs_b), writes=(tmb[3],))
                    fw.op("dve", lambda e, uv=uv, hv=hv: e.tensor_tensor(out=uv[:, :, 0], in0=uv[:, :, 0], in1=hv[:, :, 0], op=ALU.add),
                          reads=(tmb[1], tmb[3]), writes=(tmb[1],))
                    fw.op("dve", lambda e, av=av: e.memset(av[:, :, 0], 0.0), writes=(tmb[0],))
                    fw.op("dve", lambda e, r_t=r_t, i_t=i_t, hs_t=hs_t, F=F: e.tensor_tensor_scan(
                        out=hs_t[:, 0:T], data0=r_t[:, 0:T], data1=i_t[:, 0:T], initial=hc_t[:, F:F + 1],
                        op0=ALU.mult, op1=ALU.add), reads=(tmb[0], tmb[1], hc_b), writes=(tmb[3],))
                    fw.op("dve", lambda e, hs_t=hs_t, F=F: e.tensor_copy(out=hc_t[:, F:F + 1], in_=hs_t[:, PG - 1:PG]),
                          reads=(tmb[3],), writes=(hc_b,))
                    fw.op("dve", lambda e, hv=hv, F=F: e.tensor_copy(out=lrs[:, F, :], in_=hv[:, :, DEC_T - 1]),
                          reads=(tmb[3],), writes=(lrs_b,))
                    for ci, (c0, n) in enumerate(CS):
                        fw.op("dve", lambda e, hs_t=hs_t, F=F, c0=c0, n=n: e.tensor_tensor(
                            out=gh[:, F, c0:c0 + n], in0=gh[:, F, c0:c0 + n], in1=hs_t[:, c0:c0 + n], op=ALU.mult),
                            reads=(tmb[3], gh_b[ci]), writes=(gh_b[ci],))
            fw.dma(convs_d[:, :, g * SG:(g + 1) * SG, :], cvs, cvs_b, write=False, is_output=True)
            fw.dma(lrus_d[:, :, g * SG:(g + 1) * SG], lrs, lrs_b, write=False, is_output=True)
            if g == NG - 1:
                fw.dma(convp_d[:, :, :], cc_t[:], cc_b, write=False, is_output=True)
                fw.dma(lrup_d[:, :], hc_t[:], hc_b, write=False, is_output=True)
            fw.tag = "rec.out"
            for oc in range(8):
                w, wbuf = wq.get()
                w3 = w.rearrange("p (k c) -> p k c", k=RC)
                for ci, (c0, n) in enumerate(CS):
                    po, pob = psum()
                    mm_group(po[:, 0:n], pob, [(w3[:, k, :], gh[:, k, c0:c0 + n]) for k in range(RC)], (wbuf, gh_b[ci]))
                    xs = xa_t[:, oc, c0:c0 + n]
                    fw.op("dve", lambda e, xs=xs, po=po, n=n: e.tensor_tensor(out=xs, in0=po[:, 0:n], in1=xs, op=ALU.add),
                          reads=(pob, xa_b[ci]), writes=(xa_b[ci],))

        stages = []
        for g in range(NG):
            stages.append(("load", g))
            for l in range(2):
                stages += [("ffn", "f1", l), ("ln", l * 3 + 0), ("mix", l, g), ("ln", l * 3 + 1), ("ffn", "f2", l), ("ln", l * 3 + 2)]
            stages.append(("store", g))
        nstage = 0
        for g in range(NG):
            load_x(g)
            done = False
            seq = [("ffn", "f1", 0), ("ln", 0), ("mix", 0), ("ln", 1), ("ffn", "f2", 0), ("ln", 2),
                   ("ffn", "f1", 1), ("ln", 3), ("mix", 1), ("ln", 4), ("ffn", "f2", 1), ("ln", 5)]
            for si, s in enumerate(seq):
                if stop is not None and si >= stop:
                    break
                if s[0] == "ffn":
                    ffn(s[1], s[2])
                elif s[0] == "ln":
                    layer_norm(s[1], final=(s[1] == 5))
                else:
                    if s[1] == 0:
                        retention(g)
                    else:
                        rglru(g)
            store_y(g)
        fw.finish()
        build_program.pe_tags = fw.pe_tags
        with nc.Block() as block:
            fw.replay(block)
    return nc


_CACHE = {}


def kernel(**inputs):
    stop = inputs.pop("_stop", None)
    ncores = inputs.pop("_cores", NCORES)
    ct, cd, cds = build_ctab()
    sh = prep_shared(inputs)
    in_maps = []
    for c in range(NCORES):
        d = dict(sh)
        d.update(prep_core(inputs, c))
        in_maps.append(d)
    key = ("nc", stop)
    nc = build_program(cd, cds, stop=stop)
    res = run_bass_kernel_spmd(nc, in_maps[:ncores], core_ids=list(range(ncores)))
    R = list(res.results) + [res.results[0]] * (NCORES - ncores)
    y_p = np.zeros((8, SEQ, D), np.float32)
    y_s = np.zeros((DEC_B, DEC_T, D), np.float32)
    ret_p = np.zeros((1, 8, HEADS, DK, DV), np.float32)
    conv_p = np.zeros((1, 8, 3, D_RNN), np.float32)
    lru_p = np.zeros((1, 8, D_RNN), np.float32)
    ret_s = np.zeros((1, DEC_B, HEADS, DK, DV), np.float32)
    conv_s = np.zeros((1, DEC_B, 3, D_RNN), np.float32)
    lru_s = np.zeros((1, DEC_B, D_RNN), np.float32)
    for c in range(NCORES):
        r = R[c]
        yT = np.asarray(r["yT"])
        yall = yT.transpose(2, 1, 0).reshape(SEQ + NSAMP * DEC_T, D)
        y_p[c] = yall[:SEQ]
        y_s[c * NSAMP:(c + 1) * NSAMP] = yall[SEQ:].reshape(NSAMP, DEC_T, D)
        ret_p[0, c] = np.asarray(r["ret_p"])
        ret_s[0, c * NSAMP:(c + 1) * NSAMP] = np.asarray(r["ret_s"])
        conv_p[0, c] = np.asarray(r["convT_p"]).transpose(2, 1, 0).reshape(3, D_RNN)
        conv_s[0, c * NSAMP:(c + 1) * NSAMP] = np.asarray(r["convT_s"]).transpose(2, 3, 1, 0).reshape(NSAMP, 3, D_RNN)
        lru_p[0, c] = np.asarray(r["lruT_p"]).T.reshape(D_RNN)
        lru_s[0, c * NSAMP:(c + 1) * NSAMP] = np.asarray(r["lruT_s"]).transpose(2, 1, 0).reshape(NSAMP, D_RNN)
    return (y_p, y_s, ret_p, conv_p, lru_p, ret_s, conv_s, lru_s)
```

```python
import math
from contextlib import ExitStack

import numpy as np
import concourse.bass as bass
import concourse.mybir as mybir
from concourse.bass_utils import run_bass_kernel_spmd

F32 = mybir.dt.float32
BF16 = mybir.dt.bfloat16
AF = mybir.ActivationFunctionType
ALU = mybir.AluOpType

NCORES = 8
D = 1024
KC = 8
SEQ = 2048
DEC_B = 128
DEC_T = 4
NSAMP = DEC_B // NCORES
NG = 2
PG = SEQ // NG
SG = NSAMP // NG
SCOL = SG * DEC_T
T = PG + SCOL
CS = [(0, 512), (512, 512), (1024, SCOL)]
NCH = PG // 128
HEADS = 4
DK = 256
DV = 512
D_RNN = 1280
RC = 10
D_FF = 2816
FJ = 22
PAST = 16384
ALPHA = (2.0 * 2) ** 0.25
LN_EPS = 1e-5
GN_EPS = 1e-6
LRU_C = 8.0
EPOCH_MAX = 8000

ENGS = ["sync", "act", "dve", "pool", "pe"]


class DSem:
    def __init__(self, handle):
        self.h = handle
        self.total = 0


class Buf:
    __slots__ = ("name", "w", "r", "dsem")

    def __init__(self, name):
        self.name = name
        self.w = None
        self.r = {}
        self.dsem = None


class FW:
    def __init__(self, nc, stack):
        self.nc = nc
        self.stack = stack
        self.prog = {e: [] for e in ENGS}
        self.cnt = {e: 0 for e in ENGS}
        self.epoch = {e: 0 for e in ENGS}
        self.seen = {e: {} for e in ENGS}
        self.esems = {}
        self.dsems = []
        self.out_dsems = []
        self.arena_bufs = []
        self.legacy = {}
        self.tag = "init"
        self.pe_tags = []

    def esem(self, eng, epoch):
        k = (eng, epoch)
        if k not in self.esems:
            self.esems[k] = self.stack.enter_context(self.nc.semaphore(f"e_{eng}_{epoch}"))
        return self.esems[k]

    def new_dsem(self, name):
        d = DSem(self.stack.enter_context(self.nc.semaphore(f"d_{name}_{len(self.dsems)}")))
        self.dsems.append(d)
        return d

    def _need(self, eng, tok):
        if tok is None:
            return
        if tok[0] == "e":
            _, te, tep, tc = tok
            if te == eng:
                if eng == "pe" or eng == "sync":
                    return
                if tep == self.epoch[eng] and tc + 2 <= self.cnt[eng]:
                    return
                if tep < self.epoch[eng] and self.cnt[eng] >= 2:
                    return
            key = (te, tep)
            if self.seen[eng].get(key, 0) >= tc:
                return
            self.seen[eng][key] = tc
            self.prog[eng].append(("w", self.esem(te, tep), tc))
        else:
            d = tok[1]
            key = ("d", id(d))
            if self.seen[eng].get(key, 0) >= d.total:
                return
            self.seen[eng][key] = d.total
            self.prog[eng].append(("w", d.h, d.total))

    def _deps(self, eng, reads, writes):
        for b in reads:
            self._need(eng, b.w)
        for b in writes:
            self._need(eng, b.w)
            for t in list(b.r.values()):
                self._need(eng, t)

    def op(self, eng, fn, reads=(), writes=(), npe=1):
        if eng == "pe":
            self.pe_tags.append((self.tag, npe))
        self._deps(eng, reads, writes)
        ep = self.epoch[eng]
        self.cnt[eng] += 1
        tok = ("e", eng, ep, self.cnt[eng])
        self.prog[eng].append(("o", fn, self.esem(eng, ep), 1))
        if self.cnt[eng] >= EPOCH_MAX:
            self.epoch[eng] += 1
            self.cnt[eng] = 0
        for b in reads:
            b.r[eng] = tok
        for b in writes:
            b.w = tok
            b.r = {}
        return tok

    def dma(self, out, in_, buf, write, is_output=False, extra=()):
        eng = "sync"
        if write:
            self._deps(eng, (), (buf,) + tuple(extra))
        else:
            self._deps(eng, (buf,) + tuple(extra), ())
        if buf.dsem is None:
            buf.dsem = self.new_dsem(buf.name)
        d = buf.dsem
        d.total += 16
        self.prog[eng].append(("o", lambda e, o=out, i=in_: e.dma_start(out=o, in_=i), d.h, 16))
        tok = ("d", d)
        for bb in (buf,) + tuple(extra):
            if write:
                bb.w = tok
                bb.r = {}
            else:
                bb.r["dma" + str(id(d))] = tok
        if is_output and d not in self.out_dsems:
            self.out_dsems.append(d)
        return tok

    def arena_reset(self):
        leg = dict(self.legacy)
        for b in self.arena_bufs:
            toks = list(b.r.values())
            if b.w is not None:
                toks.append(b.w)
            for t in toks:
                if t[0] == "e":
                    k = ("e", t[1])
                    o = leg.get(k)
                    if o is None or (o[2], o[3]) < (t[2], t[3]):
                        leg[k] = t
                else:
                    leg[("d", id(t[1]))] = t
        self.legacy = leg
        self.arena_bufs = []

    def abuf(self, name):
        b = Buf(name)
        b.r = dict(self.legacy)
        self.arena_bufs.append(b)
        return b

    def finish(self):
        for d in self.out_dsems:
            self._need("sync", ("d", d))

    def replay(self, block):
        prog = self.prog

        def run(name, e):
            for ent in prog[name]:
                if ent[0] == "w":
                    e.wait_ge(ent[1], ent[2])
                else:
                    ins = ent[1](e)
                    ins.then_inc(ent[2], ent[3])

        @block.sync
        def _(e):
            run("sync", e)

        @block.scalar
        def _(e):
            run("act", e)

        @block.vector
        def _(e):
            run("dve", e)

        @block.gpsimd
        def _(e):
            run("pool", e)

        @block.tensor
        def _(e):
            run("pe", e)


class Arena:
    def __init__(self, ap_f32, nbytes):
        self.ap = ap_f32
        self.nbytes = nbytes
        self.off = 0

    def reset(self):
        self.off = 0

    def alloc(self, shape_free, dtype):
        esz = 4 if dtype == F32 else 2
        n = int(np.prod(shape_free))
        nb = (n * esz + 31) // 32 * 32
        assert self.off + nb <= self.nbytes, f"arena overflow {self.off}+{nb}>{self.nbytes}"
        a = self.ap[:, self.off // 4:(self.off + nb) // 4]
        self.off += nb
        if dtype != F32:
            a = a.bitcast(dtype)
        a = a[:, 0:n]
        if len(shape_free) == 2:
            a = a.rearrange("p (a b) -> p a b", a=shape_free[0])
        elif len(shape_free) == 3:
            a = a.rearrange("p (a b c) -> p a b c", a=shape_free[0], b=shape_free[1])
        return a


def _ctab_layout():
    lay = {}
    off = 0
    for name, w in [("maskT", 4 * 128), ("qdec", 4 * 128), ("maskTs", 4 * 32), ("qdecs", 4 * 32),
                    ("colmask", SG * 32), ("rowmask", SG), ("kdec", 4), ("kdecs", 4),
                    ("lneps", 1), ("gneps", 1), ("ident", 128), ("one", 1)]:
        lay[name] = (off, w)
        off += w
    return lay, off


CT_LAY, CT_W = _ctab_layout()


def _ptab_layout():
    lay = {}
    off = 0
    for name, w in [("ln_g", 48), ("ln_b", 48), ("gn_g", 16), ("conv_w", 40), ("conv_b", 10),
                    ("b_a", 10), ("b_i", 10), ("lam", 10)]:
        lay[name] = (off, w)
        off += w
    return lay, off


PT_LAY, PT_W = _ptab_layout()


def build_ctab():
    c = np.zeros((128, CT_W), np.float64)
    lg = np.log1p(-np.exp2(-5.0 - np.arange(HEADS)))
    idx = np.arange(128)
    o, _ = CT_LAY["maskT"]
    for h in range(HEADS):
        rel = idx[None, :] - idx[:, None]
        m = np.where(rel >= 0, np.exp(lg[h] * np.maximum(rel, 0)), 0.0) / 16.0
        c[:, o + h * 128:o + (h + 1) * 128] = m
    o, _ = CT_LAY["qdec"]
    for h in range(HEADS):
        c[:, o + h * 128:o + (h + 1) * 128] = np.exp(lg[h] * (idx + 1.0))[None, :]
    m32 = np.arange(32)
    bb, tt = m32 // 4, m32 % 4
    o, _ = CT_LAY["maskTs"]
    for h in range(HEADS):
        rel = tt[None, :] - tt[:, None]
        m = np.where((rel >= 0) & (bb[None, :] == bb[:, None]), np.exp(lg[h] * np.maximum(rel, 0)), 0.0) / 16.0
        c[0:32, o + h * 32:o + (h + 1) * 32] = m
    o, _ = CT_LAY["qdecs"]
    for h in range(HEADS):
        c[:, o + h * 32:o + (h + 1) * 32] = np.exp(lg[h] * (tt + 1.0))[None, :]
    o, _ = CT_LAY["colmask"]
    for b in range(SG):
        c[:, o + b * 32:o + (b + 1) * 32] = (bb == b).astype(np.float64)[None, :]
    o, _ = CT_LAY["rowmask"]
    for b in range(SG):
        c[0:32, o + b] = (bb == b)
    o, _ = CT_LAY["kdec"]
    for h in range(HEADS):
        c[:, o + h] = np.exp(lg[h] * (127.0 - idx)) / 16.0
    o, _ = CT_LAY["kdecs"]
    for h in range(HEADS):
        c[0:32, o + h] = np.exp(lg[h] * (3.0 - tt)) / 16.0
    c[:, CT_LAY["lneps"][0]] = LN_EPS
    c[:, CT_LAY["gneps"][0]] = GN_EPS
    o, _ = CT_LAY["ident"]
    c[:, o:o + 128] = np.eye(128)
    c[:, CT_LAY["one"][0]] = 1.0
    cd = [float(np.exp(lg[h] * 128.0)) for h in range(HEADS)]
    cds = [float(np.exp(lg[h] * 4.0)) for h in range(HEADS)]
    return c.astype(np.float32), cd, cds


def build_rot():
    half = DK // 2
    inv = (10000.0 ** (-np.arange(half, dtype=np.float32) / np.float32(half))).astype(np.float32)
    rot = np.zeros((NG, 128, 2, T), np.float32)
    for g in range(NG):
        pos = np.concatenate([np.arange(g * PG, (g + 1) * PG), np.tile(PAST + np.arange(DEC_T), SG)]).astype(np.float32)
        ang = (pos[None, :] * inv[:, None]).astype(np.float32)
        rot[g, :, 0, :] = np.cos(ang)
        rot[g, :, 1, :] = np.sin(ang)
    return rot


def rec_gate_kcs():
    res = []
    for fo in range(5):
        lo, hi = fo * 128, fo * 128 + 127
        b0, b1 = lo // 160, hi // 160
        ilo, ihi = b0 * 160, b1 * 160 + 159
        res.append(list(range(ilo // 128, ihi // 128 + 1)))
    return res


GATE_KCS = rec_gate_kcs()


def fm(v, nchunk):
    return np.ascontiguousarray(v.reshape(nchunk, 128).T)


def prep_shared(inp):
    f = lambda a: np.ascontiguousarray(np.asarray(a, dtype=np.float32))
    sh = {}
    for l in range(2):
        for nm, wi, wo in (("f1", "ffn1_w_in", "ffn1_w_out"), ("f2", "ffn2_w_in", "ffn2_w_out")):
            W = f(inp[wi][l])
            Wr = W.reshape(KC, 128, 2, FJ, 128).transpose(3, 1, 0, 2, 4)
            sh[f"{nm}in{l}"] = np.ascontiguousarray(Wr).reshape(FJ, 128, KC * 256)
            Wo = f(inp[wo][l])
            Wor = Wo.reshape(2, 11, 128, 8, 128).transpose(3, 0, 2, 1, 4)
            sh[f"{nm}out{l}"] = np.ascontiguousarray(Wor).reshape(8, 2, 128, 11 * 128)
    W = f(inp["ret_w_in"][0])
    blocks = []
    for h in range(HEADS):
        cols = [W[:, h * 256:(h + 1) * 256], W[:, 1024 + h * 256:1024 + (h + 1) * 256],
                W[:, 2048 + h * 512:2048 + h * 512 + 256], W[:, 2048 + h * 512 + 256:2048 + (h + 1) * 512],
                W[:, 4096 + h * 512:4096 + h * 512 + 256], W[:, 4096 + h * 512 + 256:4096 + (h + 1) * 512]]
        for cblk in cols:
            blocks.append(cblk.reshape(KC, 128, 256).transpose(1, 0, 2).reshape(128, KC * 256))
    sh["retin"] = np.ascontiguousarray(np.stack(blocks)).reshape(HEADS, 6, 128, KC * 256)
    Wo = f(inp["ret_w_out"][0])
    sh["retout"] = np.ascontiguousarray(Wo.reshape(HEADS, 4, 128, 8, 128).transpose(0, 3, 2, 1, 4)).reshape(HEADS, 8, 128, 4 * 128)
    W = f(inp["rec_w_in"][0])
    sh["recin"] = np.ascontiguousarray(W.reshape(KC, 128, 20, 128).transpose(2, 1, 0, 3)).reshape(20, 128, KC * 128)
    for nm, key in (("reca", "rec_w_a"), ("reci", "rec_w_i")):
        Wb = f(inp[key][0])
        dense = np.zeros((2, 640, 640), np.float32)
        for n in range(8):
            hf, nl = n // 4, n % 4
            dense[hf, nl * 160:(nl + 1) * 160, nl * 160:(nl + 1) * 160] = Wb[n]
        sh[nm] = np.ascontiguousarray(dense.reshape(2, 5, 128, 5, 128).transpose(0, 3, 2, 1, 4)).reshape(2, 5, 128, 5 * 128)
    Wo = f(inp["rec_w_out"][0])
    sh["recout"] = np.ascontiguousarray(Wo.reshape(RC, 128, 8, 128).transpose(2, 1, 0, 3)).reshape(8, 128, RC * 128)
    pt = np.zeros((128, PT_W), np.float32)
    o = PT_LAY["ln_g"][0]
    for l in range(2):
        for i in range(3):
            pt[:, o + (l * 3 + i) * 8:o + (l * 3 + i + 1) * 8] = fm(f(inp["ln_g"][l, i]), 8)
    o = PT_LAY["ln_b"][0]
    for l in range(2):
        for i in range(3):
            pt[:, o + (l * 3 + i) * 8:o + (l * 3 + i + 1) * 8] = fm(f(inp["ln_b"][l, i]), 8)
    o = PT_LAY["gn_g"][0]
    pt[:, o:o + 16] = fm(f(inp["ret_gn_g"][0]), 16)
    o = PT_LAY["conv_w"][0]
    cw = f(inp["rec_conv_w"][0])
    for j in range(4):
        pt[:, o + j * 10:o + (j + 1) * 10] = fm(cw[j], RC)
    pt[:, PT_LAY["conv_b"][0]:PT_LAY["conv_b"][0] + 10] = fm(f(inp["rec_conv_b"][0]), RC)
    pt[:, PT_LAY["b_a"][0]:PT_LAY["b_a"][0] + 10] = fm(f(inp["rec_b_a"][0]), RC)
    pt[:, PT_LAY["b_i"][0]:PT_LAY["b_i"][0] + 10] = fm(f(inp["rec_b_i"][0]), RC)
    pt[:, PT_LAY["lam"][0]:PT_LAY["lam"][0] + 10] = fm(f(inp["rec_lam"][0]), RC)
    sh["ptab"] = pt
    ct, cd, cds = build_ctab()
    sh["ctab"] = ct
    sh["rot"] = build_rot()
    return sh


def prep_core(inp, c):
    f = lambda a: np.asarray(a, dtype=np.float32)
    xp = f(inp["x_prompt"][c])
    xs = f(inp["x_sample"][c * NSAMP:(c + 1) * NSAMP]).reshape(NSAMP * DEC_T, D)
    xall = np.concatenate([xp, xs], axis=0)
    xT = np.ascontiguousarray(xall.reshape(SEQ + NSAMP * DEC_T, KC, 128).transpose(2, 1, 0))
    d = {"xT": xT}
    d["sret"] = np.ascontiguousarray(f(inp["state_ret"][0, c * NSAMP:(c + 1) * NSAMP]))
    sc = f(inp["state_conv"][0, c * NSAMP:(c + 1) * NSAMP])
    d["sconvT"] = np.ascontiguousarray(sc.reshape(NSAMP, 3, RC, 128).transpose(3, 2, 0, 1))
    sl = f(inp["state_lru"][0, c * NSAMP:(c + 1) * NSAMP])
    d["slruT"] = np.ascontiguousarray(sl.reshape(NSAMP, RC, 128).transpose(2, 1, 0))
    return d


def build_program(cd, cds, stop=None):
    nc = bass.Bass("TRN2", target_bir_lowering=False)
    NTOK = SEQ + NSAMP * DEC_T

    def din(name, shape):
        return nc.dram_tensor(name, list(shape), F32, kind="ExternalInput").ap()

    def dout(name, shape):
        return nc.dram_tensor(name, list(shape), F32, kind="ExternalOutput").ap()

    xT_d = din("xT", [128, KC, NTOK])
    sret_d = din("sret", [NSAMP, HEADS, DK, DV])
    sconv_d = din("sconvT", [128, RC, NSAMP, 3])
    slru_d = din("slruT", [128, RC, NSAMP])
    ptab_d = din("ptab", [128, PT_W])
    ctab_d = din("ctab", [128, CT_W])
    rot_d = din("rot", [NG, 128, 2, T])
    wd = {}
    for l in range(2):
        for nm in ("f1", "f2"):
            wd[f"{nm}in{l}"] = din(f"{nm}in{l}", [FJ, 128, KC * 256])
            wd[f"{nm}out{l}"] = din(f"{nm}out{l}", [8, 2, 128, 11 * 128])
    wd["retin"] = din("retin", [HEADS, 6, 128, KC * 256])
    wd["retout"] = din("retout", [HEADS, 8, 128, 4 * 128])
    wd["recin"] = din("recin", [20, 128, KC * 128])
    wd["reca"] = din("reca", [2, 5, 128, 5 * 128])
    wd["reci"] = din("reci", [2, 5, 128, 5 * 128])
    wd["recout"] = din("recout", [8, 128, RC * 128])

    yT_d = dout("yT", [128, KC, NTOK])
    retp_d = dout("ret_p", [HEADS, DK, DV])
    rets_d = dout("ret_s", [NSAMP, HEADS, DK, DV])
    convp_d = dout("convT_p", [128, RC, 3])
    convs_d = dout("convT_s", [128, RC, NSAMP, 3])
    lrup_d = dout("lruT_p", [128, RC])
    lrus_d = dout("lruT_s", [128, RC, NSAMP])

    st = ExitStack()
    with st:
        fw = FW(nc, st)

        def sb(name, shape, dt):
            return st.enter_context(nc.sbuf_tensor("s_" + name, list(shape), dt))

        xa_t = sb("xa", [128, KC, T], F32)
        xb_t = sb("xb", [128, KC, T], BF16)
        NSTG, NWB = 2, 4
        stg_t = [sb(f"stg{i}", [128, 2048], F32) for i in range(NSTG)]
        wb_t = [sb(f"wb{i}", [128, 2048], BF16) for i in range(NWB)]
        s32_t = sb("s32", [128, HEADS, 2, DV], F32)
        ctab_t = sb("ctab", [128, CT_W], F32)
        ptab_t = sb("ptab", [128, PT_W], F32)
        pder_t = sb("pder", [128, 48 + 48 + 10 + 10 + 30], F32)
        ident_t = sb("ident", [128, 128], BF16)
        ones_t = sb("ones", [128, 128], BF16)
        onesf_t = sb("onesf", [128, 128], F32)
        LNW = 512
        lnz_t = sb("lnz", [128, 4 * T], BF16)
        zsq_t = lnz_t[:, 0:KC * LNW].rearrange("p (k n) -> p k n", k=KC)
        lns_t = sb("lns", [128, 4, LNW], F32)
        lnx_t = sb("lnx", [128, 2, SCOL], F32)
        hc_t = sb("hcarry", [128, RC], F32)
        cc_t = sb("ccarry", [128, RC, 3], F32)
        ARENA_BYTES = 84 * 1024
        arena_t = sb("arena", [128, ARENA_BYTES // 4], F32)
        arena = Arena(arena_t[:], ARENA_BYTES)
        psum_t = [st.enter_context(nc.psum_tensor(f"ps{i}", [128, 512], F32)) for i in range(8)]

        xa_b = [Buf(f"xa{i}") for i in range(len(CS))]
        xb_b = [Buf(f"xb{i}") for i in range(len(CS))]
        stg_b = [Buf(f"stg{i}") for i in range(NSTG)]
        wb_b = [Buf(f"wb{i}") for i in range(NWB)]
        s32_b = [Buf(f"s32_{h}") for h in range(HEADS)]
        ctab_b = Buf("ctab")
        ptab_b = Buf("ptab")
        pder_b = Buf("pder")
        ident_b = Buf("ident")
        ones_b = Buf("ones")
        onesf_b = Buf("onesf")
        zb_b = Buf("lnz")
        zsq_b = zb_b
        lns_b = [Buf(f"lns{i}") for i in range(4)]
        lnx_b = [Buf(f"lnx{i}") for i in range(2)]
        hc_b = Buf("hc")
        cc_b = Buf("cc")
        ps_b = [Buf(f"ps{i}") for i in range(8)]
        ps_rr = [0]

        ps_pin = set()

        def psum():
            while True:
                i = ps_rr[0] % 8
                ps_rr[0] += 1
                if i not in ps_pin:
                    return psum_t[i], ps_b[i]

        def CT(name, lo=0, hi=None, rows=128):
            o, w = CT_LAY[name]
            hi = w if hi is None else hi
            return ctab_t[0:rows, o + lo:o + hi]

        def PT(name, lo=0, hi=None):
            o, w = PT_LAY[name]
            hi = w if hi is None else hi
            return ptab_t[:, o + lo:o + hi]

        wctr = [0]
        cast_pat = ["pool", "pool", "act"]
        cast_pats = {"ffn": ["pool", "act"], "ret": ["act", "act", "pool"], "rec": ["pool"]}

        def wload(dram_ap, nelem):
            i = wctr[0]
            wctr[0] += 1
            s, w = i % NSTG, i % NWB
            fw.dma(stg_t[s][:, 0:nelem], dram_ap, stg_b[s], write=True)
            ce = cast_pat[i % len(cast_pat)]
            if ce == "act":
                fw.op("act", lambda e, o=wb_t[w][:, 0:nelem], a=stg_t[s][:, 0:nelem]: e.copy(out=o, in_=a),
                      reads=(stg_b[s],), writes=(wb_b[w],))
            else:
                fw.op(ce, lambda e, o=wb_t[w][:, 0:nelem], a=stg_t[s][:, 0:nelem]: e.tensor_copy(out=o, in_=a),
                      reads=(stg_b[s],), writes=(wb_b[w],))
            return wb_t[w][:, 0:nelem], wb_b[w]

        class WQ:
            def __init__(self, items, depth=2):
                self.items = list(items)
                self.loaded = []
                self.depth = depth
                self.n = 0

            def get(self):
                while len(self.loaded) < self.depth + 1 and self.n < len(self.items):
                    ap_, ne = self.items[self.n]
                    self.loaded.append(wload(ap_, ne))
                    self.n += 1
                return self.loaded.pop(0)

        fw.dma(ctab_t[:], ctab_d[:, :], ctab_b, write=True)
        fw.dma(ptab_t[:], ptab_d[:, :], ptab_b, write=True)
        fw.op("dve", lambda e: e.tensor_copy(out=ident_t[:], in_=CT("ident")), reads=(ctab_b,), writes=(ident_b,))
        fw.op("dve", lambda e: e.memset(ones_t[:], 1.0 / D), writes=(ones_b,))
        fw.op("dve", lambda e: e.memset(onesf_t[:], 1.0 / D), writes=(onesf_b,))
        fw.op("dve", lambda e: e.tensor_scalar(out=pder_t[:, 0:96], in0=ptab_t[:, 0:96], scalar1=ALPHA, scalar2=None,
                                               op0=ALU.mult), reads=(ptab_b,), writes=(pder_b,))
        fw.op("act", lambda e: e.activation(out=pder_t[:, 96:106], in_=PT("lam"), func=AF.Exp, scale=-1.0),
              reads=(ptab_b,), writes=(pder_b,))
        fw.op("act", lambda e: e.activation(out=pder_t[:, 96:106], in_=pder_t[:, 96:106], func=AF.Ln,
                                            bias=CT("one"), scale=1.0), reads=(pder_b, ctab_b), writes=(pder_b,))
        fw.op("dve", lambda e: e.tensor_scalar(out=pder_t[:, 106:116], in0=pder_t[:, 96:106], scalar1=-2.0 * LRU_C,
                                               scalar2=None, op0=ALU.mult), reads=(pder_b,), writes=(pder_b,))
        fw.op("dve", lambda e: e.tensor_scalar(out=pder_t[:, 96:106], in0=pder_t[:, 96:106], scalar1=-LRU_C,
                                               scalar2=None, op0=ALU.mult), reads=(pder_b,), writes=(pder_b,))
        fw.op("dve", lambda e: e.tensor_scalar(out=pder_t[:, 116:126], in0=PT("b_a"), scalar1=0.5, scalar2=None, op0=ALU.mult),
              reads=(ptab_b,), writes=(pder_b,))
        fw.op("dve", lambda e: e.tensor_scalar(out=pder_t[:, 126:136], in0=PT("b_i"), scalar1=0.5, scalar2=None, op0=ALU.mult),
              reads=(ptab_b,), writes=(pder_b,))
        fw.op("dve", lambda e: e.tensor_scalar(out=pder_t[:, 136:146], in0=pder_t[:, 96:106], scalar1=0.5, scalar2=None, op0=ALU.mult),
              reads=(pder_b,), writes=(pder_b,))
        fw.op("dve", lambda e: e.memset(s32_t[:], 0.0), writes=tuple(s32_b))
        fw.op("dve", lambda e: e.memset(hc_t[:], 0.0), writes=(hc_b,))
        fw.op("dve", lambda e: e.memset(cc_t[:], 0.0), writes=(cc_b,))

        def AG(idx, kc):
            return pder_t[:, idx * 8 + kc:idx * 8 + kc + 1]

        def AB(idx, kc):
            return pder_t[:, 48 + idx * 8 + kc:48 + idx * 8 + kc + 1]

        def mm_group(out_ap, out_buf, pairs, reads, npe=None):
            def fn(e, pairs=pairs, out_ap=out_ap):
                ins = None
                n = len(pairs)
                for i, (l, r) in enumerate(pairs):
                    ins = e.matmul(out_ap, l, r, start=(i == 0), stop=(i == n - 1))
                return ins
            fw.op("pe", fn, reads=reads, writes=(out_buf,), npe=(npe or len(pairs)))

        def layer_norm(idx, final=False, g=0):
            fw.tag = f"ln{idx}"
            stats = []
            for ci, (c0, n) in enumerate(CS):
                z3 = xa_t[:, :, c0:c0 + n]
                pm, pmb = psum()
                mm_group(pm[:, 0:n], pmb, [(onesf_t[:], xa_t[:, kc, c0:c0 + n]) for kc in range(KC)], (onesf_b, xa_b[ci]), npe=2 * KC)
                fw.op("act", lambda e, z3=z3, n=n: e.activation(out=zsq_t[:, :, 0:n], in_=z3, func=AF.Square),
                      reads=(xa_b[ci],), writes=(zsq_b,))
                pe2, pe2b = psum()
                mm_group(pe2[:, 0:n], pe2b, [(ones_t[:], zsq_t[:, kc, 0:n]) for kc in range(KC)], (ones_b, zsq_b))
                if ci < 2:
                    vv, nmr = lns_t[:, 2 * ci, 0:n], lns_t[:, 2 * ci + 1, 0:n]
                    vb, nb_ = lns_b[2 * ci], lns_b[2 * ci + 1]
                else:
                    vv, nmr = lnx_t[:, 0, 0:n], lnx_t[:, 1, 0:n]
                    vb, nb_ = lnx_b[0], lnx_b[1]
                fw.op("act", lambda e, vv=vv, pm=pm, n=n: e.activation(out=vv, in_=pm[:, 0:n], func=AF.Square),
                      reads=(pmb,), writes=(vb,))
                fw.op("dve", lambda e, vv=vv, pe2=pe2, n=n: e.tensor_tensor(out=vv, in0=pe2[:, 0:n], in1=vv, op=ALU.subtract),
                      reads=(pe2b, vb), writes=(vb,))
                fw.op("act", lambda e, vv=vv: e.activation(out=vv, in_=vv, func=AF.Sqrt, bias=CT("lneps"), scale=1.0),
                      reads=(vb, ctab_b), writes=(vb,))
                fw.op("dve", lambda e, vv=vv: e.reciprocal(out=vv, in_=vv), reads=(vb,), writes=(vb,))
                fw.op("dve", lambda e, pm=pm, n=n, vv=vv, nmr=nmr: e.scalar_tensor_tensor(
                    out=nmr, in0=pm[:, 0:n], scalar=-1.0, in1=vv, op0=ALU.mult, op1=ALU.mult),
                    reads=(pmb, vb), writes=(nb_,))
                stats.append((vv, nmr, vb, nb_))
            for ci, (c0, n) in enumerate(CS):
                z3 = xa_t[:, :, c0:c0 + n]
                vv, nmr, vb, nb_ = stats[ci]
                fw.op("dve", lambda e, z3=z3, vv=vv, n=n: e.tensor_tensor(
                    out=z3, in0=z3, in1=vv.unsqueeze(1).broadcast_to([128, KC, n]), op=ALU.mult),
                    reads=(xa_b[ci], vb), writes=(xa_b[ci],))
                fw.op("dve", lambda e, z3=z3, nmr=nmr, n=n: e.tensor_tensor(
                    out=z3, in0=z3, in1=nmr.unsqueeze(1).broadcast_to([128, KC, n]), op=ALU.add),
                    reads=(xa_b[ci], nb_), writes=(xa_b[ci],))
                for kc in range(KC):
                    zc = xa_t[:, kc, c0:c0 + n]
                    xo = xb_t[:, kc, c0:c0 + n]
                    if not final:
                        fw.op("act", lambda e, zc=zc, xo=xo, kc=kc: e.activation(
                            out=xo, in_=zc, func=AF.Identity, bias=PT("ln_b", idx * 8 + kc, idx * 8 + kc + 1),
                            scale=PT("ln_g", idx * 8 + kc, idx * 8 + kc + 1)),
                            reads=(xa_b[ci], ptab_b), writes=(xb_b[ci],))
                    else:
                        fw.op("act", lambda e, zc=zc, kc=kc: e.activation(
                            out=zc, in_=zc, func=AF.Identity, bias=PT("ln_b", idx * 8 + kc, idx * 8 + kc + 1),
                            scale=PT("ln_g", idx * 8 + kc, idx * 8 + kc + 1)),
                            reads=(xa_b[ci], ptab_b), writes=(xa_b[ci],))
            if not final:
                for ci, (c0, n) in enumerate(CS):
                    for kc in range(KC):
                        zc = xa_t[:, kc, c0:c0 + n]
                        fw.op("dve", lambda e, zc=zc, kc=kc: e.tensor_scalar(
                            out=zc, in0=zc, scalar1=AG(idx, kc), scalar2=AB(idx, kc), op0=ALU.mult, op1=ALU.add),
                            reads=(xa_b[ci], pder_b), writes=(xa_b[ci],))

        def ffn(nm, l):
            cast_pat[:] = cast_pats["ffn"]
            arena.reset()
            fw.arena_reset()
            h_t = arena.alloc([FJ, T], BF16)
            s_t = [arena.alloc([512], BF16) for _ in range(2)]
            h_b = [fw.abuf(f"h{ci}") for ci in range(len(CS))]
            s_b = [fw.abuf(f"s{i}") for i in range(2)]
            win, wout = wd[f"{nm}in{l}"], wd[f"{nm}out{l}"]
            items = [(win[j], KC * 256) for j in range(FJ)] + \
                    [(wout[oc, kh], 11 * 128) for oc in range(8) for kh in range(2)]
            wq = WQ(items)
            si = 0
            fw.tag = f"{nm}{l}.in"
            for j in range(FJ):
                w, wbuf = wq.get()
                w3 = w.rearrange("p (k c) -> p k c", k=KC)
                for ci, (c0, n) in enumerate(CS):
                    pg, pgb = psum()
                    pu, pub = psum()
                    mm_group(pg[:, 0:n], pgb, [(w3[:, kc, 0:128], xb_t[:, kc, c0:c0 + n]) for kc in range(KC)],
                             (wbuf, xb_b[ci]))
                    mm_group(pu[:, 0:n], pub, [(w3[:, kc, 128:256], xb_t[:, kc, c0:c0 + n]) for kc in range(KC)],
                             (wbuf, xb_b[ci]))
                    sa, sab = s_t[si % 2], s_b[si % 2]
                    si += 1
                    fw.op("act", lambda e, sa=sa, pg=pg, n=n: e.activation(out=sa[:, 0:n], in_=pg[:, 0:n], func=AF.Silu),
                          reads=(pgb,), writes=(sab,))
                    fw.op("dve", lambda e, sa=sa, pu=pu, n=n, j=j, c0=c0: e.tensor_tensor(
                        out=h_t[:, j, c0:c0 + n], in0=pu[:, 0:n], in1=sa[:, 0:n], op=ALU.mult),
                        reads=(pub, sab), writes=(h_b[ci],))
            fw.tag = f"{nm}{l}.out"
            for oc in range(8):
                wA, wAb = wq.get()
                wB, wBb = wq.get()
                wA3 = wA.rearrange("p (k c) -> p k c", k=11)
                wB3 = wB.rearrange("p (k c) -> p k c", k=11)
                for ci, (c0, n) in enumerate(CS):
                    po, pob = psum()
                    pairs = [(wA3[:, k, :], h_t[:, k, c0:c0 + n]) for k in range(11)] + \
                            [(wB3[:, k, :], h_t[:, 11 + k, c0:c0 + n]) for k in range(11)]
                    mm_group(po[:, 0:n], pob, pairs, (wAb, wBb, h_b[ci]))
                    xs = xa_t[:, oc, c0:c0 + n]
                    fw.op("dve", lambda e, xs=xs, po=po, n=n: e.scalar_tensor_tensor(
                        out=xs, in0=po[:, 0:n], scalar=0.5, in1=xs, op0=ALU.mult, op1=ALU.add),
                        reads=(pob, xa_b[ci]), writes=(xa_b[ci],))

        def load_x(g):
            for ci, (c0, n) in enumerate(CS):
                src0 = g * PG + c0 if ci < 2 else SEQ + g * SCOL
                fw.dma(xa_t[:, :, c0:c0 + n], xT_d[:, :, src0:src0 + n], xa_b[ci], write=True)
                fw.op("pool", lambda e, c0=c0, n=n: e.tensor_copy(out=xb_t[:, :, c0:c0 + n], in_=xa_t[:, :, c0:c0 + n]),
                      reads=(xa_b[ci],), writes=(xb_b[ci],))
                fw.op("act", lambda e, c0=c0, n=n: e.mul(out=xa_t[:, :, c0:c0 + n], in_=xa_t[:, :, c0:c0 + n], mul=ALPHA),
                      reads=(xa_b[ci],), writes=(xa_b[ci],))

        def store_y(g):
            for ci, (c0, n) in enumerate(CS):
                dst0 = g * PG + c0 if ci < 2 else SEQ + g * SCOL
                fw.dma(yT_d[:, :, dst0:dst0 + n], xa_t[:, :, c0:c0 + n], xa_b[ci], write=False, is_output=True)

        def retention(g):
            cast_pat[:] = cast_pats["ret"]
            arena.reset()
            fw.arena_reset()
            onT = arena.alloc([4, T], BF16)
            onT_b = [fw.abuf(f"onT{ci}") for ci in range(len(CS))]
            rot_t = lnz_t[:].bitcast(F32)[:, 0:2 * T].rearrange("p (a b) -> p a b", a=2)
            rot_b = zb_b
            fw.dma(rot_t, rot_d[g], rot_b, write=True)
            qT = arena.alloc([2, T], BF16)
            qsT = arena.alloc([2, T], BF16)
            kT = arena.alloc([2, T], BF16)
            kd = arena.alloc([NCH + 1, 256], BF16)
            vt = arena.alloc([NCH + 1, 512], BF16)
            sg = arena.alloc([4, T], BF16)
            rt = [lns_t[:, i, :] for i in range(4)]
            NSS = 3
            sst = [arena.alloc([2, 512], F32) for _ in range(NSS)]
            ssb = [arena.alloc([2, 512], BF16) for _ in range(NSS)]
            NSB = 4
            sbf = [arena.alloc([2, 512], BF16) for _ in range(NSB)]
            sout = [lns_t[:, 2 * j:2 * j + 2, :] for j in range(2)]
            ontm = [arena.alloc([512], BF16) for _ in range(3)]
            sT = [arena.alloc([128], BF16) for _ in range(4)]
            qsm = arena.alloc([2, SG, 32], BF16)
            kdb = arena.alloc([SG, 256], BF16)
            gst = [arena.alloc([16], F32) for _ in range(3)]
            nb = lambda nm, k=1: [fw.abuf(f"{nm}{i}") for i in range(k)]
            qT_b, qsT_b, kT_b = nb("qT", 3), nb("qsT", 3), nb("kT", 3)
            kd_b, vt_b = nb("kd", NCH + 1), nb("vt", NCH + 1)
            sg_b = nb("sg", 3)
            rt_b = lns_b
            sst_b, ssb_b, sbf_b, ontm_b, sT_b = nb("sst", NSS), nb("ssb", NSS), nb("sbf", NSB), nb("ontm", 3), nb("sT", 4)
            qsm_b, kdb_b, gst_b = fw.abuf("qsm"), fw.abuf("kdb"), nb("gst", 3)

            win = wd["retin"]
            items = []
            for h in range(HEADS):
                items += [(win[h, k], KC * 256) for k in range(6)]
                items += [(wd["retout"][h, oc], 4 * 128) for oc in range(8)]
            wq = WQ(items)
            sctr = [0]
            def s_in(h, b):
                k = (h * SG + b) % NSS
                fw.dma(sst[k], sret_d[g * SG + b, h].rearrange("(f p) v -> p f v", p=128), sst_b[k], write=True)
                fw.op("pool", lambda e, k=k: e.tensor_copy(out=ssb[k][:, :, :], in_=sst[k][:, :, :]),
                      reads=(sst_b[k],), writes=(ssb_b[k],))

            for b in range(NSS):
                s_in(0, b)
            pend = []
            for h in range(HEADS):
                fw.tag = "ret.qk"
                for which, dstT, dst_b in (("q", qT, qT_b), ("k", kT, kT_b)):
                    w, wbuf = wq.get()
                    w3 = w.rearrange("p (k c) -> p k c", k=KC)
                    for ci, (c0, n) in enumerate(CS):
                        p1, p1b = psum()
                        p2, p2b = psum()
                        mm_group(p1[:, 0:n], p1b, [(w3[:, kc, 0:128], xb_t[:, kc, c0:c0 + n]) for kc in range(KC)],
                                 (wbuf, xb_b[ci]))
                        mm_group(p2[:, 0:n], p2b, [(w3[:, kc, 128:256], xb_t[:, kc, c0:c0 + n]) for kc in range(KC)],
                                 (wbuf, xb_b[ci]))
                        cosv, sinv = rot_t[:, 0, c0:c0 + n], rot_t[:, 1, c0:c0 + n]
                        t1, t2, t3, t4 = (rt[i][:, 0:n] for i in range(4))
                        fw.op("dve", lambda e, t1=t1, p1=p1, cosv=cosv, n=n: e.tensor_tensor(out=t1, in0=p1[:, 0:n], in1=cosv, op=ALU.mult),
                              reads=(p1b, rot_b), writes=(rt_b[0],))
                        fw.op("dve", lambda e, t2=t2, p2=p2, sinv=sinv, n=n: e.tensor_tensor(out=t2, in0=p2[:, 0:n], in1=sinv, op=ALU.mult),
                              reads=(p2b, rot_b), writes=(rt_b[1],))
                        fw.op("dve", lambda e, t3=t3, p1=p1, sinv=sinv, n=n: e.tensor_tensor(out=t3, in0=p1[:, 0:n], in1=sinv, op=ALU.mult),
                              reads=(p1b, rot_b), writes=(rt_b[2],))
                        fw.op("dve", lambda e, t4=t4, p2=p2, cosv=cosv, n=n: e.tensor_tensor(out=t4, in0=p2[:, 0:n], in1=cosv, op=ALU.mult),
                              reads=(p2b, rot_b), writes=(rt_b[3],))
                        fw.op("pool", lambda e, t1=t1, t2=t2, d=dstT[:, 0, c0:c0 + n]: e.tensor_tensor(out=d, in0=t1, in1=t2, op=ALU.subtract),
                              reads=(rt_b[0], rt_b[1]), writes=(dst_b[ci],))
                        fw.op("pool", lambda e, t3=t3, t4=t4, d=dstT[:, 1, c0:c0 + n]: e.tensor_tensor(out=d, in0=t3, in1=t4, op=ALU.add),
                              reads=(rt_b[2], rt_b[3]), writes=(dst_b[ci],))
                        if which == "q":
                            if ci < 2:
                                qd = CT("qdec", h * 128, (h + 1) * 128)
                                for fc in range(2):
                                    fw.op("pool", lambda e, fc=fc, c0=c0, qd=qd: e.tensor_tensor(
                                        out=qsT[:, fc, c0:c0 + 512].rearrange("p (a b) -> p a b", a=4),
                                        in0=qT[:, fc, c0:c0 + 512].rearrange("p (a b) -> p a b", a=4),
                                        in1=qd.unsqueeze(1).broadcast_to([128, 4, 128]), op=ALU.mult),
                                        reads=(qT_b[ci], ctab_b), writes=(qsT_b[ci],))
                            else:
                                qd = CT("qdecs", h * 32, (h + 1) * 32)
                                for fc in range(2):
                                    fw.op("pool", lambda e, fc=fc, c0=c0, qd=qd: e.tensor_tensor(
                                        out=qsT[:, fc, c0:c0 + 32], in0=qT[:, fc, c0:c0 + 32], in1=qd, op=ALU.mult),
                                        reads=(qT_b[ci], ctab_b), writes=(qsT_b[ci],))
                fw.tag = "ret.v"
                wv = [wq.get(), wq.get()]
                for c in range(NCH + 1):
                    ci = c // 4
                    c0 = c * 128
                    m = 128 if c < NCH else SCOL
                    pv, pvb = psum()
                    for half in range(2):
                        w3 = wv[half][0].rearrange("p (k c) -> p k c", k=KC)
                        mm_group(pv[0:m, half * 256:(half + 1) * 256], pvb,
                                 [(xb_t[:, kc, c0:c0 + m], w3[:, kc, :]) for kc in range(KC)],
                                 (wv[half][1], xb_b[ci]))
                    fw.op("act", lambda e, c=c, m=m, pv=pv: e.activation(out=vt[0:m, c, :], in_=pv[0:m, :], func=AF.Copy),
                          reads=(pvb,), writes=(vt_b[c],))
                fw.tag = "ret.g"
                for half in range(2):
                    w, wbuf = wq.get()
                    w3 = w.rearrange("p (k c) -> p k c", k=KC)
                    for vc2 in range(2):
                        vc = half * 2 + vc2
                        for ci, (c0, n) in enumerate(CS):
                            pg, pgb = psum()
                            mm_group(pg[:, 0:n], pgb, [(w3[:, kc, vc2 * 128:(vc2 + 1) * 128], xb_t[:, kc, c0:c0 + n])
                                                      for kc in range(KC)], (wbuf, xb_b[ci]))
                            fw.op("act", lambda e, pg=pg, n=n, vc=vc, c0=c0: e.activation(
                                out=sg[:, vc, c0:c0 + n], in_=pg[:, 0:n], func=AF.Silu), reads=(pgb,), writes=(sg_b[ci],))
                fw.tag = "ret.kT"
                for c in range(NCH + 1):
                    ci = c // 4
                    c0 = c * 128
                    m = 128 if c < NCH else SCOL
                    pk, pkb = psum()
                    pkb16 = pk[:].bitcast(BF16)

                    def fn(e, c0=c0, m=m, pkb16=pkb16):
                        ins = None
                        for fc in range(2):
                            ins = e.transpose(pkb16[0:m, fc * 128:(fc + 1) * 128], kT[:, fc, c0:c0 + m], ident_t[:])
                        return ins
                    fw.op("pe", fn, reads=(kT_b[ci], ident_b), writes=(pkb,), npe=2)
                    sc = CT("kdec", h, h + 1) if c < NCH else CT("kdecs", h, h + 1, rows=32)
                    fw.op("act", lambda e, c=c, m=m, pkb16=pkb16, sc=sc: e.activation(
                        out=kd[0:m, c, :], in_=pkb16[0:m, 0:256], func=AF.Identity, scale=sc),
                        reads=(pkb, ctab_b), writes=(kd_b[c],))
                cm = CT("colmask")
                for fc in range(2):
                    fw.op("pool", lambda e, fc=fc, cm=cm: e.tensor_tensor(
                        out=qsm[:, fc, :, :], in0=qsT[:, fc, PG:PG + 32].unsqueeze(1).broadcast_to([128, SG, 32]),
                        in1=cm.rearrange("p (a b) -> p a b", a=SG), op=ALU.mult),
                        reads=(qsT_b[2], ctab_b), writes=(qsm_b,))
                for b in range(SG):
                    fw.op("dve", lambda e, b=b: e.tensor_scalar(
                        out=kdb[0:32, b, :], in0=kd[0:32, NCH, :], scalar1=CT("rowmask", b, b + 1, rows=32), scalar2=None,
                        op0=ALU.mult), reads=(kd_b[NCH], ctab_b), writes=(kdb_b,))
                fw.tag = "ret.chunk"
                fw.op("act", lambda e, h=h: e.activation(out=sbf[0][:, :, :], in_=s32_t[:, h, :, :], func=AF.Copy),
                      reads=(s32_b[h],), writes=(sbf_b[0],))

                def SU(c):
                    for fc in range(2):
                        pS, pSb = psum()
                        mm_group(pS[:, :], pSb, [(kd[:, c, fc * 128:(fc + 1) * 128], vt[:, c, :])], (kd_b[c], vt_b[c]))
                        fw.op("dve", lambda e, pS=pS, fc=fc, h=h: e.scalar_tensor_tensor(
                            out=s32_t[:, h, fc, :], in0=s32_t[:, h, fc, :], scalar=cd[h], in1=pS[:, :],
                            op0=ALU.mult, op1=ALU.add), reads=(pSb, s32_b[h]), writes=(s32_b[h],))
                    nxt = (c + 1) % NSB
                    fw.op("act", lambda e, h=h, nxt=nxt: e.activation(out=sbf[nxt][:, :, :], in_=s32_t[:, h, :, :], func=AF.Copy),
                          reads=(s32_b[h],), writes=(sbf_b[nxt],))

                def SC(c, slot=None):
                    ci, c0 = c // 4, c * 128
                    samp = (c == NCH)
                    m = SCOL if samp else 128
                    psc, pscb = psum()
                    mm_group(psc[0:m, 0:m], pscb, [(kT[:, fc, c0:c0 + m], qT[:, fc, c0:c0 + m]) for fc in range(2)],
                             (kT_b[ci], qT_b[ci]))
                    sTt, sTb = (sT[c % 3], sT_b[c % 3]) if slot is None else (sT[slot], sT_b[slot])
                    mk = CT("maskT", h * 128, (h + 1) * 128) if not samp else CT("maskTs", h * 32, (h + 1) * 32, rows=32)
                    fw.op("dve", lambda e, sTt=sTt, psc=psc, mk=mk, m=m: e.tensor_tensor(
                        out=sTt[0:m, 0:m], in0=psc[0:m, 0:m], in1=mk, op=ALU.mult),
                        reads=(pscb, ctab_b), writes=(sTb,))

                def GN(c, po, pob, mid=None):
                    m = SCOL if c == NCH else 128
                    gs, gsb = gst[c % 3], gst_b[c % 3]
                    eps_ap = CT("gneps", rows=m)
                    fw.op("dve", lambda e, po=po, m=m, gs=gs: e.bn_stats(out=gs[0:m, 0:6], in_=po[0:m, :]),
                          reads=(pob,), writes=(gsb,))
                    fw.op("dve", lambda e, m=m, gs=gs: e.bn_aggr(out=gs[0:m, 6:8], in_=gs[0:m, 0:6]),
                          reads=(gsb,), writes=(gsb,))
                    fw.op("act", lambda e, m=m, eps_ap=eps_ap, gs=gs: e.activation(out=gs[0:m, 8:9], in_=gs[0:m, 7:8], func=AF.Sqrt,
                                                                                   bias=eps_ap, scale=1.0),
                          reads=(gsb, ctab_b), writes=(gsb,))
                    if mid is not None:
                        mid()
                    fw.op("dve", lambda e, m=m, gs=gs: e.reciprocal(out=gs[0:m, 8:9], in_=gs[0:m, 8:9]),
                          reads=(gsb,), writes=(gsb,))
                    fw.op("dve", lambda e, m=m, gs=gs: e.scalar_tensor_tensor(out=gs[0:m, 9:10], in0=gs[0:m, 6:7], scalar=-1.0,
                                                                              in1=gs[0:m, 8:9], op0=ALU.mult, op1=ALU.mult),
                          reads=(gsb,), writes=(gsb,))
                    ot, otb = ontm[c % 3], ontm_b[c % 3]
                    fw.op("act", lambda e, ot=ot, po=po, m=m, gs=gs: e.activation(out=ot[0:m, :], in_=po[0:m, :], func=AF.Identity,
                                                                                 bias=gs[0:m, 9:10], scale=gs[0:m, 8:9]),
                          reads=(pob, gsb), writes=(otb,))

                def TR(c):
                    ci, c0 = c // 4, c * 128
                    m = SCOL if c == NCH else 128
                    ot, otb = ontm[c % 3], ontm_b[c % 3]
                    pt_, ptb = psum()
                    pt16 = pt_[:].bitcast(BF16)

                    def fnT(e, ot=ot, m=m, pt16=pt16):
                        ins = None
                        for vc in range(4):
                            ins = e.transpose(pt16[:, vc * 128:vc * 128 + m], ot[0:m, vc * 128:(vc + 1) * 128], ident_t[0:m, 0:m])
                        return ins
                    fw.op("pe", fnT, reads=(otb, ident_b), writes=(ptb,), npe=4)
                    for vc in range(4):
                        fw.op("act", lambda e, vc=vc, m=m, c0=c0, pt16=pt16, h=h: e.activation(
                            out=onT[:, vc, c0:c0 + m], in_=pt16[:, vc * 128:vc * 128 + m], func=AF.Identity,
                            scale=PT("gn_g", h * 4 + vc, h * 4 + vc + 1)), reads=(ptb, ptab_b), writes=(onT_b[ci],))

                def O(c):
                    ci, c0 = c // 4, c * 128
                    sTt, sTb = sT[c % 3], sT_b[c % 3]
                    cur = c % NSB
                    po, pob = psum()
                    pairs = [(sTt[:, 0:128], vt[:, c, :])] + \
                            [(qsT[:, fc, c0:c0 + 128], sbf[cur][:, fc, :]) for fc in range(2)]
                    mm_group(po[:, :], pob, pairs, (sTb, vt_b[c], qsT_b[ci], sbf_b[cur]))
                    return po, pob

                fw.tag = "ret.chunk"
                SC(NCH, slot=3)
                sTt, sTb = sT[3], sT_b[3]
                po_s, pob_s = psum()
                po_idx = ps_b.index(pob_s)
                ps_pin.add(po_idx)

                def SAMP(b, sTt=sTt, sTb=sTb, po=po_s, pob=pob_s):
                    k = (h * SG + b) % NSS
                    oj = b % 2
                    sidx = g * SG + b
                    pairs = []
                    if b == 0:
                        pairs.append((sTt[0:32, 0:32], vt[0:32, NCH, :]))
                    pairs += [(qsm[:, fc, b, :], ssb[k][:, fc, :]) for fc in range(2)]

                    def fn(e, pairs=pairs, b=b, po=po):
                        ins = None
                        for i, (l, r) in enumerate(pairs):
                            ins = e.matmul(po[0:32, :], l, r, start=(b == 0 and i == 0),
                                           stop=(b == SG - 1 and i == len(pairs) - 1))
                        return ins
                    fw.op("pe", fn, reads=(sTb, vt_b[NCH], qsm_b, ssb_b[k]), writes=(pob,), npe=len(pairs))
                    for fc in range(2):
                        pS, pSb = psum()
                        mm_group(pS[:, :], pSb, [(kdb[0:32, b, fc * 128:(fc + 1) * 128], vt[0:32, NCH, :])],
                                 (kdb_b, vt_b[NCH]))
                        fw.op("dve", lambda e, pS=pS, fc=fc, k=k, h=h, oj=oj: e.scalar_tensor_tensor(
                            out=sout[oj][:, fc, :], in0=sst[k][:, fc, :], scalar=cds[h], in1=pS[:, :],
                            op0=ALU.mult, op1=ALU.add), reads=(pSb, sst_b[k]), writes=(lns_b[2 * oj], lns_b[2 * oj + 1]))
                    fw.dma(rets_d[sidx, h].rearrange("(f p) v -> p f v", p=128), sout[oj], lns_b[2 * oj], write=False,
                           is_output=True, extra=(lns_b[2 * oj + 1],))
                    pend.append((h, b))
                    if len(pend) > 0:
                        ph, pb = pend.pop(0)
                        nb_, nh_ = pb + NSS, ph
                        if nb_ >= SG:
                            nb_, nh_ = nb_ - SG, ph + 1
                        if nh_ < HEADS:
                            s_in(nh_, nb_)

                SU(0)
                SU(1)
                SC(0)
                for c in range(NCH):
                    if c + 2 < NCH:
                        SU(c + 2)
                    if c + 1 < NCH:
                        SC(c + 1)
                    po, pob = O(c)
                    GN(c, po, pob, mid=(lambda c=c: TR(c - 1)) if c >= 1 else None)
                    SAMP(c)
                TR(NCH - 1)
                c = NCH
                po, pob = po_s, pob_s
                ps_pin.discard(po_idx)
                GN(c, po, pob)
                TR(c)
                if g == NG - 1:
                    fw.dma(retp_d[h].rearrange("(f p) v -> p f v", p=128), s32_t[:, h, :, :], s32_b[h], write=False,
                           is_output=True)
                for ci, (c0, n) in enumerate(CS):
                    fw.op("dve", lambda e, c0=c0, n=n: e.tensor_tensor(
                        out=onT[:, :, c0:c0 + n], in0=onT[:, :, c0:c0 + n], in1=sg[:, :, c0:c0 + n], op=ALU.mult),
                        reads=(onT_b[ci], sg_b[ci]), writes=(onT_b[ci],))
                fw.tag = "ret.out"
                for oc in range(8):
                    w, wbuf = wq.get()
                    w3 = w.rearrange("p (k c) -> p k c", k=4)
                    for ci, (c0, n) in enumerate(CS):
                        po, pob = psum()
                        mm_group(po[:, 0:n], pob, [(w3[:, k, :], onT[:, k, c0:c0 + n]) for k in range(4)], (wbuf, onT_b[ci]))
                        xs = xa_t[:, oc, c0:c0 + n]
                        fw.op("dve", lambda e, xs=xs, po=po, n=n: e.tensor_tensor(out=xs, in0=po[:, 0:n], in1=xs, op=ALU.add),
                              reads=(pob, xa_b[ci]), writes=(xa_b[ci],))

        def rglru(g):
            cast_pat[:] = cast_pats["rec"]
            arena.reset()
            fw.arena_reset()
            gh = arena.alloc([RC, T], BF16)
            gh_b = [fw.abuf(f"gh{ci}") for ci in range(len(CS))]
            xc = arena.alloc([5, T], F32)
            xcb = arena.alloc([5, T], BF16)
            XPW = 3 + PG + SG * 7
            xp = [arena.alloc([XPW], F32), lns_t[:, :, :].rearrange("p a b -> p (a b)")[:, 0:XPW]]
            tmp = [[arena.alloc([T], F32) for _ in range(2)] for _ in range(2)]
            a2buf = [arena.alloc([T], F32) for _ in range(2)]
            cvs = arena.alloc([RC, SG, 3], F32)
            lrs = arena.alloc([RC, SG], F32)
            xc_b = [fw.abuf(f"xc{i}") for i in range(5)]
            xcb_b = [fw.abuf(f"xcb{i}") for i in range(5)]
            xp_b = [(fw.abuf("xp0"),), (lns_b[0], lns_b[1], lns_b[2])]
            tmp_b = [[fw.abuf(f"tmp{i}{j}") for j in range(2)] for i in range(2)]
            a2buf_b = [fw.abuf(f"a2buf{i}") for i in range(2)]
            cvs_b, lrs_b = fw.abuf("cvs"), fw.abuf("lrs")
            fw.dma(lrs, slru_d[:, :, g * SG:(g + 1) * SG], lrs_b, write=True)
            items = []
            for hf in range(2):
                items += [(wd["recin"][hf * 5 + fc], KC * 128) for fc in range(5)]
                items += [(wd["recin"][10 + hf * 5 + fc], KC * 128) for fc in range(5)]
                for fo in range(5):
                    items += [(wd["reca"][hf, fo], 5 * 128), (wd["reci"][hf, fo], 5 * 128)]
            items += [(wd["recout"][oc], RC * 128) for oc in range(8)]
            wq = WQ(items)
            tctr = 0
            for hf in range(2):
                fw.tag = "rec.gate"
                for fc in range(5):
                    F = hf * 5 + fc
                    w, wbuf = wq.get()
                    w3 = w.rearrange("p (k c) -> p k c", k=KC)
                    for ci, (c0, n) in enumerate(CS):
                        pg, pgb = psum()
                        mm_group(pg[:, 0:n], pgb, [(w3[:, kc, :], xb_t[:, kc, c0:c0 + n]) for kc in range(KC)], (wbuf, xb_b[ci]))
                        fw.op("act", lambda e, pg=pg, n=n, F=F, c0=c0: e.activation(
                            out=gh[:, F, c0:c0 + n], in_=pg[:, 0:n], func=AF.Gelu_apprx_tanh), reads=(pgb,), writes=(gh_b[ci],))
                fw.tag = "rec.conv"
                for fc in range(5):
                    F = hf * 5 + fc
                    w, wbuf = wq.get()
                    w3 = w.rearrange("p (k c) -> p k c", k=KC)
                    xpt, xpbs = xp[F % 2], xp_b[F % 2]
                    xps = xpt[:, 3 + PG:3 + PG + SG * 7].rearrange("p (b t) -> p b t", b=SG)
                    fw.op("dve", lambda e, xpt=xpt, F=F: e.tensor_copy(out=xpt[:, 0:3], in_=cc_t[:, F, :]),
                          reads=(cc_b,), writes=xpbs)
                    fw.dma(xps[:, :, 0:3], sconv_d[:, F, g * SG:(g + 1) * SG, :], xpbs[0], write=True, extra=xpbs[1:])
                    for ci, (c0, n) in enumerate(CS):
                        pg, pgb = psum()
                        mm_group(pg[:, 0:n], pgb, [(w3[:, kc, :], xb_t[:, kc, c0:c0 + n]) for kc in range(KC)], (wbuf, xb_b[ci]))
                        if ci < 2:
                            fw.op("act", lambda e, pg=pg, n=n, c0=c0, xpt=xpt: e.activation(
                                out=xpt[:, 3 + c0:3 + c0 + n], in_=pg[:, 0:n], func=AF.Copy), reads=(pgb,), writes=xpbs)
                        else:
                            fw.op("act", lambda e, pg=pg, xps=xps: e.activation(
                                out=xps[:, :, 3:7], in_=pg[:, 0:SCOL].rearrange("p (b t) -> p b t", b=SG), func=AF.Copy),
                                reads=(pgb,), writes=xpbs)
                    cw = lambda j, F=F: PT("conv_w", j * 10 + F, j * 10 + F + 1)
                    cbias = PT("conv_b", F, F + 1)
                    for (dst, src_of) in ((xc[:, fc, 0:PG], lambda j, xpt=xpt: xpt[:, j:j + PG]),
                                          (xc[:, fc, PG:T].rearrange("p (b t) -> p b t", b=SG), lambda j, xps=xps: xps[:, :, j:j + 4])):
                        fw.op("act", lambda e, dst=dst, src_of=src_of, cw=cw, cbias=cbias: e.activation(
                            out=dst, in_=src_of(3), func=AF.Identity, bias=cbias, scale=cw(3)),
                            reads=xpbs + (ptab_b,), writes=(xc_b[fc],))
                        for j in (2, 1, 0):
                            fw.op("dve", lambda e, dst=dst, src_of=src_of, cw=cw, j=j: e.scalar_tensor_tensor(
                                out=dst, in0=src_of(j), scalar=cw(j), in1=dst, op0=ALU.mult, op1=ALU.add),
                                reads=xpbs + (ptab_b, xc_b[fc]), writes=(xc_b[fc],))
                    fw.op("act", lambda e, fc=fc: e.activation(out=xcb[:, fc, :], in_=xc[:, fc, :], func=AF.Copy),
                          reads=(xc_b[fc],), writes=(xcb_b[fc],))
                    fw.op("dve", lambda e, xpt=xpt, F=F: e.tensor_copy(out=cc_t[:, F, :], in_=xpt[:, PG:PG + 3]),
                          reads=xpbs, writes=(cc_b,))
                    fw.op("dve", lambda e, xps=xps, F=F: e.tensor_copy(out=cvs[:, F, :, :], in_=xps[:, :, 4:7]),
                          reads=xpbs, writes=(cvs_b,))
                fw.tag = "rec.lru"
                for fo in range(5):
                    F = hf * 5 + fo
                    wa, wab = wq.get()
                    wi_, wib = wq.get()
                    wa3 = wa.rearrange("p (k c) -> p k c", k=5)
                    wi3 = wi_.rearrange("p (k c) -> p k c", k=5)
                    tm, tmb = tmp[tctr % 2], tmp_b[tctr % 2]
                    tctr += 1
                    r_t, i_t = tm
                    a2_t = a2buf[tctr % 2]
                    hs_t = a2_t
                    tmb = list(tmb) + [a2buf_b[tctr % 2], a2buf_b[tctr % 2]]
                    kcs = GATE_KCS[fo]
                    for ci, (c0, n) in enumerate(CS):
                        pr, prb = psum()
                        pi_, pib = psum()
                        mm_group(pr[:, 0:n], prb, [(wa3[:, kc, :], xcb[:, kc, c0:c0 + n]) for kc in kcs],
                                 (wab,) + tuple(xcb_b[kc] for kc in kcs))
                        mm_group(pi_[:, 0:n], pib, [(wi3[:, kc, :], xcb[:, kc, c0:c0 + n]) for kc in kcs],
                                 (wib,) + tuple(xcb_b[kc] for kc in kcs))
                        fw.op("act", lambda e, pr=pr, n=n, c0=c0, r_t=r_t, F=F: e.activation(
                            out=r_t[:, c0:c0 + n], in_=pr[:, 0:n], func=AF.Tanh, bias=pder_t[:, 116 + F:117 + F], scale=0.5),
                            reads=(prb, pder_b), writes=(tmb[0],))
                        fw.op("act", lambda e, pi_=pi_, n=n, c0=c0, i_t=i_t, F=F: e.activation(
                            out=i_t[:, c0:c0 + n], in_=pi_[:, 0:n], func=AF.Tanh, bias=pder_t[:, 126 + F:127 + F], scale=0.5),
                            reads=(pib, pder_b), writes=(tmb[1],))
                    cF = pder_t[:, 96 + F:97 + F]
                    c2F = pder_t[:, 106 + F:107 + F]
                    chF = pder_t[:, 136 + F:137 + F]
                    fw.op("act", lambda e, r_t=r_t, a2_t=a2_t, cF=cF: e.activation(out=a2_t[:, :], in_=r_t[:, :], func=AF.Exp, scale=cF, bias=cF),
                          reads=(tmb[0], pder_b), writes=(tmb[2],))
                    fw.op("act", lambda e, r_t=r_t, chF=chF: e.activation(out=r_t[:, :], in_=r_t[:, :], func=AF.Exp, scale=chF, bias=chF),
                          reads=(tmb[0], pder_b), writes=(tmb[0],))
                    fw.op("act", lambda e, a2_t=a2_t: e.activation(out=a2_t[:, :], in_=a2_t[:, :], func=AF.Sqrt,
                                                                   bias=CT("one"), scale=-1.0),
                          reads=(tmb[2], ctab_b), writes=(tmb[2],))
                    fw.op("dve", lambda e, i_t=i_t, fo=fo: e.scalar_tensor_tensor(out=i_t[:, :], in0=i_t[:, :], scalar=1.0, in1=xc[:, fo, :],
                                                                                 op0=ALU.add, op1=ALU.mult),
                          reads=(tmb[1], xc_b[fo]), writes=(tmb[1],))
                    fw.op("dve", lambda e, i_t=i_t, a2_t=a2_t: e.scalar_tensor_tensor(out=i_t[:, :], in0=i_t[:, :], scalar=0.5, in1=a2_t[:, :],
                                                                                     op0=ALU.mult, op1=ALU.mult),
                          reads=(tmb[1], tmb[2]), writes=(tmb[1],))
                    av = r_t[:, PG:T].rearrange("p (b t) -> p b t", b=SG)
                    uv = i_t[:, PG:T].rearrange("p (b t) -> p b t", b=SG)
                    hv = hs_t[:, PG:T].rearrange("p (b t) -> p b t", b=SG)
                    fw.op("dve", lambda e, av=av, hv=hv, F=F: e.tensor_tensor(out=hv[:, :, 0], in0=av[:, :, 0], in1=lrs[:, F, :], op=ALU.mult),
                          reads=(tmb[0], lrs_b), writes=(tmb[3],))
                    fw.op("dve", lambda e, uv=uv, hv=hv: e.tensor_tensor(out=uv[:, :, 0], in0=uv[:, :, 0], in1=hv[:, :, 0], op=ALU.add),
                          reads=(tmb[1], tmb[3]), writes=(tmb[1],))
                    fw.op("dve", lambda e, av=av: e.memset(av[:, :, 0], 0.0), writes=(tmb[0],))
                    fw.op("dve", lambda e, r_t=r_t, i_t=i_t, hs_t=hs_t, F=F: e.tensor_tensor_scan(
                        out=hs_t[:, 0:T], data0=r_t[:, 0:T], data1=i_t[:, 0:T], initial=hc_t[:, F:F + 1],
                        op0=ALU.mult, op1=ALU.add), reads=(tmb[0], tmb[1], hc_b), writes=(tmb[3],))
                    fw.op("dve", lambda e, hs_t=hs_t, F=F: e.tensor_copy(out=hc_t[:, F:F + 1], in_=hs_t[:, PG - 1:PG]),
                          reads=(tmb[3],), writes=(hc_b,))
                    fw.op("dve", lambda e, hv=hv, F=F: e.tensor_copy(out=lrs[:, F, :], in_=hv[:, :, DEC_T - 1]),
                          reads=(tmb[3],), writes=(lrs_b,))
                    for ci, (c0, n) in enumerate(CS):
                        fw.op("dve", lambda e, hs_t=hs_t, F=F, c0=c0, n=n: e.tensor_tensor(
                            out=gh[:, F, c0:c0 + n], in0=gh[:, F, c0:c0 + n], in1=hs_t[:, c0:c0 + n], op=ALU.mult),
                            reads=(tmb[3], gh_b[ci]), writes=(gh_b[ci],))
            fw.dma(convs_d[:, :, g * SG:(g + 1) * SG, :], cvs, cvs_b, write=False, is_output=True)
            fw.dma(lrus_d[:, :, g * SG:(g + 1) * SG], lrs, lrs_b, write=False, is_output=True)
            if g == NG - 1:
                fw.dma(convp_d[:, :, :], cc_t[:], cc_b, write=False, is_output=True)
                fw.dma(lrup_d[:, :], hc_t[:], hc_b, write=False, is_output=True)
            fw.tag = "rec.out"
            for oc in range(8):
                w, wbuf = wq.get()
                w3 = w.rearrange("p (k c) -> p k c", k=RC)
                for ci, (c0, n) in enumerate(CS):
                    po, pob = psum()
                    mm_group(po[:, 0:n], pob, [(w3[:, k, :], gh[:, k, c0:c0 + n]) for k in range(RC)], (wbuf, gh_b[ci]))
                    xs = xa_t[:, oc, c0:c0 + n]
                    fw.op("dve", lambda e, xs=xs, po=po, n=n: e.tensor_tensor(out=xs, in0=po[:, 0:n], in1=xs, op=ALU.add),
                          reads=(pob, xa_b[ci]), writes=(xa_b[ci],))

        stages = []
        for g in range(NG):
            stages.append(("load", g))
            for l in range(2):
                stages += [("ffn", "f1", l), ("ln", l * 3 + 0), ("mix", l, g), ("ln", l * 3 + 1), ("ffn", "f2", l), ("ln", l * 3 + 2)]
            stages.append(("store", g))
        nstage = 0
        for g in range(NG):
            load_x(g)
            done = False
            seq = [("ffn", "f1", 0), ("ln", 0), ("mix", 0), ("ln", 1), ("ffn", "f2", 0), ("ln", 2),
                   ("ffn", "f1", 1), ("ln", 3), ("mix", 1), ("ln", 4), ("ffn", "f2", 1), ("ln", 5)]
            for si, s in enumerate(seq):
                if stop is not None and si >= stop:
                    break
                if s[0] == "ffn":
                    ffn(s[1], s[2])
                elif s[0] == "ln":
                    layer_norm(s[1], final=(s[1] == 5))
                else:
                    if s[1] == 0:
                        retention(g)
                    else:
                        rglru(g)
            store_y(g)
        fw.finish()
        build_program.pe_tags = fw.pe_tags
        with nc.Block() as block:
            fw.replay(block)
    return nc


_CACHE = {}


def kernel(**inputs):
    stop = inputs.pop("_stop", None)
    ncores = inputs.pop("_cores", NCORES)
    ct, cd, cds = build_ctab()
    sh = prep_shared(inputs)
    in_maps = []
    for c in range(NCORES):
        d = dict(sh)
        d.update(prep_core(inputs, c))
        in_maps.append(d)
    key = ("nc", stop)
    nc = build_program(cd, cds, stop=stop)
    res = run_bass_kernel_spmd(nc, in_maps[:ncores], core_ids=list(range(ncores)))
    R = list(res.results) + [res.results[0]] * (NCORES - ncores)
    y_p = np.zeros((8, SEQ, D), np.float32)
    y_s = np.zeros((DEC_B, DEC_T, D), np.float32)
    ret_p = np.zeros((1, 8, HEADS, DK, DV), np.float32)
    conv_p = np.zeros((1, 8, 3, D_RNN), np.float32)
    lru_p = np.zeros((1, 8, D_RNN), np.float32)
    ret_s = np.zeros((1, DEC_B, HEADS, DK, DV), np.float32)
    conv_s = np.zeros((1, DEC_B, 3, D_RNN), np.float32)
    lru_s = np.zeros((1, DEC_B, D_RNN), np.float32)
    for c in range(NCORES):
        r = R[c]
        yT = np.asarray(r["yT"])
        yall = yT.transpose(2, 1, 0).reshape(SEQ + NSAMP * DEC_T, D)
        y_p[c] = yall[:SEQ]
        y_s[c * NSAMP:(c + 1) * NSAMP] = yall[SEQ:].reshape(NSAMP, DEC_T, D)
        ret_p[0, c] = np.asarray(r["ret_p"])
        ret_s[0, c * NSAMP:(c + 1) * NSAMP] = np.asarray(r["ret_s"])
        conv_p[0, c] = np.asarray(r["convT_p"]).transpose(2, 1, 0).reshape(3, D_RNN)
        conv_s[0, c * NSAMP:(c + 1) * NSAMP] = np.asarray(r["convT_s"]).transpose(2, 3, 1, 0).reshape(NSAMP, 3, D_RNN)
        lru_p[0, c] = np.asarray(r["lruT_p"]).T.reshape(D_RNN)
        lru_s[0, c * NSAMP:(c + 1) * NSAMP] = np.asarray(r["lruT_s"]).transpose(2, 1, 0).reshape(NSAMP, D_RNN)
    return (y_p, y_s, ret_p, conv_p, lru_p, ret_s, conv_s, lru_s)
```
